# Optimizing a Trainium2 kernel written in Bass

```python
import math
import jax, jax.numpy as jnp
from jax import lax
import numpy as np

D_MODEL = 2048
BATCH = 4
SEQ = 4096
DEPTH = 2
DEC_BATCH = 16
DEC_SEQ = 64
PAST_LEN = 2048

CHUNK = 64
N_MIXERS = 2
N_A_LAYERS = (DEPTH + 1) // 2
N_B_LAYERS = DEPTH // 2
A_HEADS = 16
A_HEAD_DIM = D_MODEL // A_HEADS
BAND_CHUNKS = 8
BAND_ROWS = BAND_CHUNKS * CHUNK
REL_CLIP = 4 * CHUNK
MLA_HEADS = 16
MLA_Q_RANK = D_MODEL // 4
MLA_KV_RANK = D_MODEL // 4
MLA_NOPE_DIM = 128
MLA_ROPE_DIM = 64
MLA_V_DIM = 128
ROPE_THETA = 10000.0
Q_BLOCK = 128
D_FF = 4 * D_MODEL
NORM_EPS = 1e-6
NEG_INF = -1e30

kernel_name = "hybrid_streaming_band_mla_step"


def rmsnorm(x, g):
    xf = x.astype(jnp.float32)
    y = xf * lax.rsqrt(jnp.mean(xf * xf, axis=-1, keepdims=True) + NORM_EPS)
    return (y * g.astype(jnp.float32)).astype(x.dtype)


def rope(x, pos):
    half = x.shape[-1] // 2
    inv = 1.0 / (ROPE_THETA ** (jnp.arange(half, dtype=jnp.float32) / half))
    ang = pos.astype(jnp.float32)[:, None] * inv[None, :]
    shp = (pos.shape[0],) + (1,) * (x.ndim - 3) + (half,)
    cos = jnp.cos(ang).reshape(shp)
    sin = jnp.sin(ang).reshape(shp)
    xf = x.astype(jnp.float32)
    x1, x2 = xf[..., :half], xf[..., half:]
    return jnp.concatenate([x1 * cos - x2 * sin, x2 * cos + x1 * sin], axis=-1).astype(x.dtype)


def masked_softmax(scores, mask):
    return jax.nn.softmax(jnp.where(mask, scores, NEG_INF), axis=-1)


def a_project(h, w_qkv):
    b, s, _ = h.shape
    qkv = (h @ w_qkv).reshape(b, s, 3, A_HEADS, A_HEAD_DIM)
    return qkv[:, :, 0], qkv[:, :, 1], qkv[:, :, 2]


def band_attend(q, k, v, q_pos, k_pos, rel_bias):
    scores = jnp.einsum('bqhd,bkhd->bhqk', q, k).astype(jnp.float32) * (A_HEAD_DIM ** -0.5)
    rel = jnp.clip(q_pos[:, None] - k_pos[None, :], -REL_CLIP, REL_CLIP) + REL_CLIP
    scores = scores + rel_bias[:, rel].astype(jnp.float32)[None]
    qc = q_pos[:, None] // CHUNK
    kc = k_pos[None, :] // CHUNK
    mask = (k_pos[None, :] >= 0) & (kc <= qc) & (kc >= qc - BAND_CHUNKS)
    p = masked_softmax(scores, mask[None, None])
    return jnp.einsum('bhqk,bkhd->bqhd', p.astype(v.dtype), v)


def mixer_a_prompt(h, w_qkv, w_o, rel_bias):
    b, s, _ = h.shape
    q, k, v = a_project(h, w_qkv)
    pad = ((0, 0), (BAND_ROWS, 0), (0, 0), (0, 0))
    kpad, vpad = jnp.pad(k, pad), jnp.pad(v, pad)
    band = BAND_ROWS + CHUNK

    def one_chunk(c):
        start = c * CHUNK
        q_c = lax.dynamic_slice_in_dim(q, start, CHUNK, axis=1)
        k_b = lax.dynamic_slice_in_dim(kpad, start, band, axis=1)
        v_b = lax.dynamic_slice_in_dim(vpad, start, band, axis=1)
        q_pos = start + jnp.arange(CHUNK)
        k_pos = start - BAND_ROWS + jnp.arange(band)
        return band_attend(q_c, k_b, v_b, q_pos, k_pos, rel_bias)

    o = lax.map(one_chunk, jnp.arange(s // CHUNK))
    o = jnp.moveaxis(o, 0, 1).reshape(b, s, A_HEADS * A_HEAD_DIM)
    keep = min(BAND_ROWS, s)
    return o @ w_o, k[:, s - keep:], v[:, s - keep:]


def mixer_a_sample(h, cache_k, cache_v, w_qkv, w_o, rel_bias):
    b, s, _ = h.shape
    q, k, v = a_project(h, w_qkv)
    n_cache = cache_k.shape[1]
    kk = jnp.concatenate([cache_k.astype(k.dtype), k], axis=1)
    vv = jnp.concatenate([cache_v.astype(v.dtype), v], axis=1)
    q_pos = PAST_LEN + jnp.arange(s)
    k_pos = jnp.concatenate([PAST_LEN - n_cache + jnp.arange(n_cache), q_pos])
    o = band_attend(q, kk, vv, q_pos, k_pos, rel_bias).reshape(b, s, A_HEADS * A_HEAD_DIM)
    return o @ w_o, k, v


def mla_project(h, pos, w_dq, q_norm, w_uq, w_dkv, kv_norm, w_uk):
    cq = rmsnorm(h @ w_dq, q_norm)
    q = jnp.einsum('bsr,rhe->bshe', cq, w_uq)
    q_nope, q_rope = q[..., :MLA_NOPE_DIM], rope(q[..., MLA_NOPE_DIM:], pos)
    q_lat = jnp.einsum('bshn,chn->bshc', q_nope, w_uk)
    dkv = h @ w_dkv
    ckv = rmsnorm(dkv[..., :MLA_KV_RANK], kv_norm)
    kr = rope(dkv[..., MLA_KV_RANK:], pos)
    return q_lat, q_rope, ckv, kr


def mla_attend(q_lat, q_rope, ckv, kr, q_pos, k_pos, w_uv):
    s = jnp.einsum('bqhc,bkc->bhqk', q_lat, ckv) + jnp.einsum('bqhr,bkr->bhqk', q_rope, kr)
    s = s.astype(jnp.float32) * ((MLA_NOPE_DIM + MLA_ROPE_DIM) ** -0.5)
    mask = (k_pos[None, :] // CHUNK) <= (q_pos[:, None] // CHUNK)
    p = masked_softmax(s, mask[None, None])
    o_lat = jnp.einsum('bhqk,bkc->bqhc', p.astype(ckv.dtype), ckv)
    return jnp.einsum('bqhc,chv->bqhv', o_lat, w_uv)


def mixer_b_prompt(h, w_dq, q_norm, w_uq, w_dkv, kv_norm, w_uk, w_uv, w_o):
    b, s, _ = h.shape
    pos = jnp.arange(s)
    q_lat, q_rope, ckv, kr = mla_project(h, pos, w_dq, q_norm, w_uq, w_dkv, kv_norm, w_uk)

    def one_block(n):
        start = n * Q_BLOCK
        ql = lax.dynamic_slice_in_dim(q_lat, start, Q_BLOCK, axis=1)
        qr = lax.dynamic_slice_in_dim(q_rope, start, Q_BLOCK, axis=1)
        return mla_attend(ql, qr, ckv, kr, start + jnp.arange(Q_BLOCK), pos, w_uv)

    o = lax.map(one_block, jnp.arange(s // Q_BLOCK))
    o = jnp.moveaxis(o, 0, 1).reshape(b, s, MLA_HEADS * MLA_V_DIM)
    return o @ w_o, ckv, kr


def mixer_b_sample(h, cache_ckv, cache_kr, w_dq, q_norm, w_uq, w_dkv, kv_norm, w_uk, w_uv, w_o):
    b, s, _ = h.shape
    q_pos = PAST_LEN + jnp.arange(s)
    q_lat, q_rope, ckv, kr = mla_project(h, q_pos, w_dq, q_norm, w_uq, w_dkv, kv_norm, w_uk)
    n_cache = cache_ckv.shape[1]
    ckv_all = jnp.concatenate([cache_ckv.astype(ckv.dtype), ckv], axis=1)
    kr_all = jnp.concatenate([cache_kr.astype(kr.dtype), kr], axis=1)
    k_pos = jnp.concatenate([PAST_LEN - n_cache + jnp.arange(n_cache), q_pos])
    o = mla_attend(q_lat, q_rope, ckv_all, kr_all, q_pos, k_pos, w_uv).reshape(b, s, MLA_HEADS * MLA_V_DIM)
    return o @ w_o, ckv, kr


def sq_relu_mlp(h, w1, w2):
    return jnp.square(jax.nn.relu(h @ w1)) @ w2


def setup_inputs(seed: int = 0) -> dict:
    key = jax.random.key(seed)
    ks = jax.random.split(key, 24)
    f32 = jnp.float32

    def nrm(k, shape, fan_in):
        return jax.random.normal(k, shape, f32) * (fan_in ** -0.5)

    def gain(k, shape):
        return 1.0 + 0.01 * jax.random.normal(k, shape, f32)

    a_win = min(BAND_ROWS, PAST_LEN)
    hd_a = A_HEADS * A_HEAD_DIM
    return {
        "x_prompt": jax.random.normal(ks[0], (BATCH, SEQ, D_MODEL), f32),
        "x_sample": jax.random.normal(ks[1], (DEC_BATCH, DEC_SEQ, D_MODEL), f32),
        "cache_a_k": jax.random.normal(ks[2], (N_A_LAYERS, DEC_BATCH, a_win, A_HEADS, A_HEAD_DIM), f32),
        "cache_a_v": jax.random.normal(ks[3], (N_A_LAYERS, DEC_BATCH, a_win, A_HEADS, A_HEAD_DIM), f32),
        "cache_mla_ckv": jax.random.normal(ks[4], (N_B_LAYERS, DEC_BATCH, PAST_LEN, MLA_KV_RANK), f32),
        "cache_mla_kr": jax.random.normal(ks[5], (N_B_LAYERS, DEC_BATCH, PAST_LEN, MLA_ROPE_DIM), f32),
        "ln_mix_pre": gain(ks[6], (DEPTH, D_MODEL)),
        "ln_mix_post": gain(ks[7], (DEPTH, D_MODEL)),
        "ln_ffn_pre": gain(ks[8], (DEPTH, D_MODEL)),
        "ln_ffn_post": gain(ks[9], (DEPTH, D_MODEL)),
        "a_w_qkv": nrm(ks[10], (N_A_LAYERS, D_MODEL, 3 * hd_a), D_MODEL),
        "a_w_o": nrm(ks[11], (N_A_LAYERS, hd_a, D_MODEL), hd_a),
        "a_rel_bias": 0.1 * jax.random.normal(ks[12], (N_A_LAYERS, A_HEADS, 2 * REL_CLIP + 1), f32),
        "mla_w_dq": nrm(ks[13], (N_B_LAYERS, D_MODEL, MLA_Q_RANK), D_MODEL),
        "mla_q_norm": gain(ks[14], (N_B_LAYERS, MLA_Q_RANK)),
        "mla_w_uq": nrm(ks[15], (N_B_LAYERS, MLA_Q_RANK, MLA_HEADS, MLA_NOPE_DIM + MLA_ROPE_DIM), MLA_Q_RANK),
        "mla_w_dkv": nrm(ks[16], (N_B_LAYERS, D_MODEL, MLA_KV_RANK + MLA_ROPE_DIM), D_MODEL),
        "mla_kv_norm": gain(ks[17], (N_B_LAYERS, MLA_KV_RANK)),
        "mla_w_uk": nrm(ks[18], (N_B_LAYERS, MLA_KV_RANK, MLA_HEADS, MLA_NOPE_DIM), MLA_KV_RANK),
        "mla_w_uv": nrm(ks[19], (N_B_LAYERS, MLA_KV_RANK, MLA_HEADS, MLA_V_DIM), MLA_KV_RANK),
        "mla_w_o": nrm(ks[20], (N_B_LAYERS, MLA_HEADS * MLA_V_DIM, D_MODEL), MLA_HEADS * MLA_V_DIM),
        "ffn_w1": nrm(ks[21], (DEPTH, D_MODEL, D_FF), D_MODEL),
        "ffn_w2": nrm(ks[22], (DEPTH, D_FF, D_MODEL), D_FF),
    }


def reference(x_prompt, x_sample, cache_a_k, cache_a_v, cache_mla_ckv, cache_mla_kr,
              ln_mix_pre, ln_mix_post, ln_ffn_pre, ln_ffn_post,
              a_w_qkv, a_w_o, a_rel_bias,
              mla_w_dq, mla_q_norm, mla_w_uq, mla_w_dkv, mla_kv_norm, mla_w_uk, mla_w_uv, mla_w_o,
              ffn_w1, ffn_w2):
    xp, xs = x_prompt, x_sample
    a_k_p, a_v_p, a_k_s, a_v_s = [], [], [], []
    b_c_p, b_r_p, b_c_s, b_r_s = [], [], [], []
    for i in range(DEPTH):
        j = i // N_MIXERS
        hp = rmsnorm(xp, ln_mix_pre[i])
        hs = rmsnorm(xs, ln_mix_pre[i])
        if i % N_MIXERS == 0:
            op, kp, vp = mixer_a_prompt(hp, a_w_qkv[j], a_w_o[j], a_rel_bias[j])
            os_, ks_, vs_ = mixer_a_sample(hs, cache_a_k[j], cache_a_v[j], a_w_qkv[j], a_w_o[j], a_rel_bias[j])
            a_k_p.append(kp); a_v_p.append(vp); a_k_s.append(ks_); a_v_s.append(vs_)
        else:
            op, cp, rp = mixer_b_prompt(hp, mla_w_dq[j], mla_q_norm[j], mla_w_uq[j], mla_w_dkv[j],
                                        mla_kv_norm[j], mla_w_uk[j], mla_w_uv[j], mla_w_o[j])
            os_, cs_, rs_ = mixer_b_sample(hs, cache_mla_ckv[j], cache_mla_kr[j], mla_w_dq[j], mla_q_norm[j],
                                           mla_w_uq[j], mla_w_dkv[j], mla_kv_norm[j], mla_w_uk[j],
                                           mla_w_uv[j], mla_w_o[j])
            b_c_p.append(cp); b_r_p.append(rp); b_c_s.append(cs_); b_r_s.append(rs_)
        xp = xp + rmsnorm(op, ln_mix_post[i])
        xs = xs + rmsnorm(os_, ln_mix_post[i])
        xp = xp + rmsnorm(sq_relu_mlp(rmsnorm(xp, ln_ffn_pre[i]), ffn_w1[i], ffn_w2[i]), ln_ffn_post[i])
        xs = xs + rmsnorm(sq_relu_mlp(rmsnorm(xs, ln_ffn_pre[i]), ffn_w1[i], ffn_w2[i]), ln_ffn_post[i])
    y_prompt, y_sample = xp, xs
    new_a_k_prompt = jnp.stack(a_k_p, axis=0)
    new_a_v_prompt = jnp.stack(a_v_p, axis=0)
    new_a_k_sample = jnp.stack(a_k_s, axis=0)
    new_a_v_sample = jnp.stack(a_v_s, axis=0)
    new_mla_ckv_prompt = jnp.stack(b_c_p, axis=0)
    new_mla_kr_prompt = jnp.stack(b_r_p, axis=0)
    new_mla_ckv_sample = jnp.stack(b_c_s, axis=0)
    new_mla_kr_sample = jnp.stack(b_r_s, axis=0)
    return (y_prompt, y_sample, new_a_k_prompt, new_a_v_prompt, new_a_k_sample, new_a_v_sample,
            new_mla_ckv_prompt, new_mla_kr_prompt, new_mla_ckv_sample, new_mla_kr_sample)
```

```python
import contextlib
import os
import numpy as np
import ml_dtypes
import concourse.bass as bass
import concourse.mybir as mybir
from concourse.bass_utils import run_bass_kernel_spmd

F32 = mybir.dt.float32
BF16 = mybir.dt.bfloat16
AF = mybir.ActivationFunctionType
ALU = mybir.AluOpType
AX = mybir.AxisListType

N_CORES = 8
D = 2048
SEQ = 4096
HALF = 2048
TT = 512
NEG = -1.0e30
EPS = 1e-6
SC_A = 128 ** -0.5
SC_B = 192 ** -0.5


class Buf:
    __slots__ = ("name", "w", "r")

    def __init__(self, name=""):
        self.name = name
        self.w = None
        self.r = {}


class Sched:
    NSLOT = 8
    ENGS = ("pe", "act", "dve", "pool", "sp")

    def __init__(self, nc):
        self.nc = nc
        self.prog = {k: [] for k in self.ENGS}
        self.cnt = {k: 0 for k in self.ENGS}
        self.dcnt = {"sp": 0, "pool": 0}
        self.waited = {k: {} for k in self.ENGS}
        self.sems = {}

    def _need(self, eng, tok, waits):
        if tok is None:
            return
        key, val = tok
        if key == ("eng", "pe") and eng == "pe":
            return
        if self.waited[eng].get(key, 0) >= val:
            return
        self.waited[eng][key] = val
        waits.append((key, val))

    def _deps(self, eng, reads, writes, waits):
        for b in reads:
            self._need(eng, b.w, waits)
        for b in writes:
            self._need(eng, b.w, waits)
            for key, val in b.r.items():
                self._need(eng, (key, val), waits)

    def _commit(self, tok, reads, writes):
        key, val = tok
        for b in reads:
            if b.r.get(key, 0) < val:
                b.r[key] = val
        for b in writes:
            b.w = tok
            b.r = {}

    def op(self, eng, fn, reads=(), writes=()):
        waits = []
        self._deps(eng, reads, writes, waits)
        self.cnt[eng] += 1
        tok = (("eng", eng), self.cnt[eng])
        self.prog[eng].append((waits, fn, ("eng", eng), 1))
        self._commit(tok, reads, writes)
        return tok

    def dma(self, q, fn, reads=(), writes=()):
        waits = []
        i = self.dcnt[q]
        self.dcnt[q] += 1
        slot = i % self.NSLOT
        key = ("dma", q, slot)
        if i >= self.NSLOT:
            self._need(q, (key, 16 * (i // self.NSLOT)), waits)
        self._deps(q, reads, writes, waits)
        tok = (key, 16 * (i // self.NSLOT + 1))
        self.prog[q].append((waits, fn, key, 16))
        self._commit(tok, reads, writes)
        return tok

    def _all_tokens(self, queues):
        toks = [(("eng", e), c) for e, c in self.cnt.items() if c > 0 and e != "sp"]
        for q in queues:
            n = self.dcnt[q]
            for s in range(min(n, self.NSLOT)):
                toks.append((("dma", q, s), 16 * ((n - s + self.NSLOT - 1) // self.NSLOT)))
        return toks

    def barrier(self, final=False):
        toks = self._all_tokens(("sp", "pool") if final else ("pool",))
        for e in (self.ENGS if final else ("pe", "act", "dve", "pool")):
            waits = []
            for t in toks:
                if t[0] == ("eng", e):
                    continue
                if self.waited[e].get(t[0], 0) >= t[1]:
                    continue
                self.waited[e][t[0]] = t[1]
                waits.append(t)
            if waits:
                self.prog[e].append((waits, None, None, 0))

    def emit(self, st):
        nc = self.nc
        keys = [("eng", e) for e in self.ENGS]
        for q in self.dcnt:
            keys += [("dma", q, s) for s in range(self.NSLOT)]
        for k in keys:
            self.sems[k] = st.enter_context(nc.semaphore("s_" + "_".join(map(str, k))))
        self.barrier(final=True)
        block = st.enter_context(nc.Block())

        def replay(name):
            def run(e):
                for waits, fn, key, inc in self.prog[name]:
                    for k, v in waits:
                        e.wait_ge(self.sems[k], v)
                    if fn is not None:
                        fn(e).then_inc(self.sems[key], inc)
            return run

        block.tensor(replay("pe"))
        block.scalar(replay("act"))
        block.vector(replay("dve"))
        block.gpsimd(replay("pool"))
        block.sync(replay("sp"))


def build_nc(l0_tiles=9, l1_tiles=5):
    nc = bass.Bass("TRN2", target_bir_lowering=False)
    st = contextlib.ExitStack()
    S = Sched(nc)

    def din(name, shape, dt=F32):
        return nc.dram_tensor(name, list(shape), dt, kind="ExternalInput").ap()

    def dout(name, shape):
        return nc.dram_tensor(name, list(shape), F32, kind="ExternalOutput").ap()

    def dscr(name, shape, dt):
        return nc.dram_tensor(name, list(shape), dt).ap()

    xp = din("xp", [SEQ, D]); xs = din("xs", [128, D])
    cak = din("cak", [2, 512, D]); cav = din("cav", [2, 512, D])
    cckv = din("cckv", [2, 2048, 512]); ckr = din("ckr", [2, 2048, 64])
    gains = din("gains", [128, 136])
    w_qkv = din("w_qkv", [D, 6144]); w_ao = din("w_ao", [D, D])
    relT = din("relT", [16, 128, 640])
    w_dq = din("w_dq", [D, 512]); w_uq = din("w_uq", [512, 16, 256])
    w_dkv = din("w_dkv", [D, 640]); w_uk = din("w_uk", [512, D]); w_uv = din("w_uv", [512, D])
    w_bo = din("w_bo", [D, D])
    w1 = din("w1", [2, D, 4 * D]); w2 = din("w2", [2, 4 * D, D])
    ident_f_d = din("ident_f", [128, 128]); cb_d = din("cb", [128, 256], BF16)
    kindb_d = din("kindb", [10, 640], BF16); qmb_d = din("qmb", [10, 640], BF16)
    kind_d = din("kind", [64, 4096], BF16); qm_d = din("qm", [64, 2048], BF16)
    sel_d = din("sel", [128, 2])
    cosk_d = din("cosk", [64, SEQ]); sink_d = din("sink", [64, SEQ])
    cosq_d = din("cosq", [64, HALF]); sinq_d = din("sinq", [64, HALF])
    coss_d = din("coss", [64, 128]); sins_d = din("sins", [64, 128])
    y_p = dout("y_p", [HALF, D]); y_s = dout("y_s", [128, D])
    nak_p = dout("nak_p", [512, D]); nav_p = dout("nav_p", [512, D])
    nak_s = dout("nak_s", [128, D]); nav_s = dout("nav_s", [128, D])
    nckv_p = dout("nckv_p", [SEQ, 512]); nkr_p = dout("nkr_p", [SEQ, 64])
    nckv_s = dout("nckv_s", [128, 512]); nkr_s = dout("nkr_s", [128, 64])

    class WMat:
        def __init__(self, name, src, Kd, Nd, gw=512, kper=8):
            self.src, self.gw, self.kper = src, gw, min(kper, Kd // 128)
            self.ng, self.nkp = Nd // gw, (Kd // 128) // self.kper
            self.d = dscr(name, [self.ng * self.nkp, 128, self.kper, gw], BF16)
            self.b = [Buf() for _ in range(self.ng * self.nkp)]
            self.done = False

        def cast(self):
            if self.done:
                return
            self.done = True
            for g in range(self.ng):
                for kp in range(self.nkp):
                    i = g * self.nkp + kp
                    r0 = kp * self.kper * 128
                    src = self.src[r0:r0 + self.kper * 128, g * self.gw:(g + 1) * self.gw].rearrange("(k p) c -> p k c", p=128)
                    dst = self.d[i]
                    S.dma("pool", lambda e, dst=dst, src=src: e.dma_start(out=dst, in_=src), writes=[self.b[i]])

        def piece(self, g, kp):
            self.cast()
            i = g * self.nkp + kp
            return self.d[i], self.b[i]

    Wqkv = WMat("Wqkv", w_qkv, D, 6144)
    Wao = WMat("Wao", w_ao, D, D)
    W1m = [WMat("W1_%d" % l, w1[l], D, 4 * D) for l in range(2)]
    W2m = [WMat("W2_%d" % l, w2[l], 4 * D, D) for l in range(2)]
    Wdq = WMat("Wdq", w_dq, D, 512)
    Wdkv_c = WMat("Wdkv_c", w_dkv[:, 0:512], D, 512)
    Wdkv_r = WMat("Wdkv_r", w_dkv[:, 512:640], D, 128, gw=128)
    Wuq = [WMat("Wuq%d" % h, w_uq[:, h, :], 512, 256, gw=256, kper=4) for h in range(16)]
    Wuk = WMat("Wuk", w_uk, 512, D, gw=128, kper=4)
    Wuv = WMat("Wuv", w_uv, 512, D, gw=256, kper=4)
    Wbo = WMat("Wbo", w_bo, D, D)
    relB = dscr("relB", [16, 128, 640], BF16); B_relB = Buf()
    KTs = dscr("KTs", [128, 16, SEQ], BF16); Vs = dscr("Vs", [SEQ, D], BF16)
    B_kv = [Buf() for _ in range(8)]
    x1s = dscr("x1s", [128, 16, SEQ], F32); B_x1 = [Buf() for _ in range(8)]
    x1ss = dscr("x1ss", [128, 16, 128], F32); B_x1s = Buf()
    ckvTs = dscr("ckvTs", [128, 4, SEQ], BF16); krTs = dscr("krTs", [64, SEQ], BF16)
    B_lat = [Buf() for _ in range(8)]
    ckvTss = dscr("ckvTss", [128, 4, 128], BF16); krTss = dscr("krTss", [64, 128], BF16); B_lats = Buf()

    K1 = 1024
    NA = 80 * K1
    AR = st.enter_context(nc.sbuf_tensor("arena", [128, NA], BF16))

    def v16(off, n):
        return AR[:, off:off + n]

    def v32(off, n):
        return AR[:, off:off + 2 * n].bitcast(F32)

    NWS = 3
    WS = [st.enter_context(nc.sbuf_tensor("ws%d" % i, [128, 4096], BF16)) for i in range(NWS)]
    B_ws = [Buf() for _ in range(NWS)]
    wcount = [0]
    cst = st.enter_context(nc.sbuf_tensor("sb_cst", [128, 256], BF16)); B_c = Buf()
    idf = st.enter_context(nc.sbuf_tensor("sb_idf", [128, 128], F32))
    gn = st.enter_context(nc.sbuf_tensor("sb_gn", [128, 136], F32))
    sel = st.enter_context(nc.sbuf_tensor("sb_sel", [128, 2], F32))
    mk = st.enter_context(nc.sbuf_tensor("sb_mk", [16, 1280], BF16))
    ident_b = cst[:, 0:128]; ones_b = cst[:, 128:256]
    sm = st.enter_context(nc.sbuf_tensor("sb_sm", [128, 32], F32)); B_sm = [Buf() for _ in range(8)]; B_rs = [Buf() for _ in range(8)]
    rstd = st.enter_context(nc.sbuf_tensor("sb_rstd", [128, 512], F32)); B_rstd = Buf()
    sqt = [st.enter_context(nc.sbuf_tensor("sqt%d" % i, [128, 512], BF16)) for i in range(2)]
    B_sq = [Buf(), Buf()]
    NTF = 2
    tmpf = [st.enter_context(nc.sbuf_tensor("tmpf%d" % i, [128, 512], F32)) for i in range(NTF)]
    B_tf = [Buf() for _ in range(NTF)]
    PA = [st.enter_context(nc.psum_tensor("pa%d" % i, [128, 1024], F32)) for i in range(2)]
    B_PA = [Buf(), Buf()]
    PB = [st.enter_context(nc.psum_tensor("pb%d" % i, [128, 512], F32)) for i in range(4)]
    B_PB = [Buf() for _ in range(4)]
    rr = {"pb": 0, "ev": 0, "sq": 0, "tf": 0, "sm": 0, "u": 0, "g": 0}

    S.dma("pool", lambda e: e.dma_start(out=cst[:], in_=cb_d), writes=[B_c])
    S.dma("pool", lambda e: e.dma_start(out=idf[:], in_=ident_f_d), writes=[B_c])
    S.dma("pool", lambda e: e.dma_start(out=gn[:], in_=gains), writes=[B_c])
    S.dma("pool", lambda e: e.dma_start(out=sel[:], in_=sel_d), writes=[B_c])
    S.dma("pool", lambda e: e.dma_start(out=mk[0:10, 0:640], in_=kindb_d), writes=[B_c])
    S.dma("pool", lambda e: e.dma_start(out=mk[0:10, 640:1280], in_=qmb_d), writes=[B_c])
    for h in range(16):
        S.dma("pool", lambda e, h=h: e.dma_start(out=relB[h], in_=relT[h]), writes=[B_relB])

    def wload(wm, g, kp):
        pap, pbuf = wm.piece(g, kp)
        i = wcount[0] % NWS
        wcount[0] += 1
        kc, gw = pap.shape[1], pap.shape[2]
        dst = WS[i][:, 0:kc * gw].rearrange("p (k c) -> p k c", c=gw)
        S.dma("sp", lambda e: e.dma_start(out=dst, in_=pap), reads=[pbuf], writes=[B_ws[i]])
        return dst, B_ws[i]

    def nextpb():
        i = rr["pb"] % 4
        rr["pb"] += 1
        return PB[i], B_PB[i]

    def accs4(n):
        g = rr["g"] % 2
        rr["g"] += 1
        if g == 0:
            return [(PB[m], B_PB[m]) for m in range(4)]
        return [(PA[m // 2][:, (m % 2) * 512:(m % 2 + 1) * 512], B_PA[m // 2]) for m in range(4)]

    def ubufs(l):
        return list({id(b): b for b in l}.values())

    def mm(out_ap, pairs, reads, writes):
        def fn(e):
            n = len(pairs)
            ins = None
            for i, (l, r) in enumerate(pairs):
                ins = e.matmul(out_ap, lhsT=l, rhs=r, start=(i == 0), stop=(i == n - 1))
            return ins
        S.op("pe", fn, reads, writes)

    def evac(out_ap, in_ap, reads, writes, scale=None, eng=None):
        if eng is None:
            eng = ("act", "dve")[rr["ev"] % 2]
            rr["ev"] += 1
        if eng == "act":
            if scale is None:
                S.op("act", lambda e: e.activation(out=out_ap, in_=in_ap, func=AF.Copy), reads, writes)
            else:
                S.op("act", lambda e: e.activation(out=out_ap, in_=in_ap, func=AF.Copy, scale=scale), reads, writes)
        else:
            if scale is None:
                S.op("dve", lambda e: e.tensor_copy(out=out_ap, in_=in_ap), reads, writes)
            else:
                S.op("dve", lambda e: e.tensor_scalar(out=out_ap, in0=in_ap, scalar1=scale, scalar2=None,
                                                     op0=ALU.mult), reads, writes)

    def rms(src, nch, n, dn, gcol0, dst=None, resid=None, bsrc=(), bdst=()):
        pb, bpb = nextpb()
        for c in range(nch):
            i = rr["sq"] % 2
            rr["sq"] += 1
            sq = sqt[i][:, 0:n]
            a = src(c)
            S.op("act", lambda e, sq=sq, a=a: e.activation(out=sq, in_=a, func=AF.Square),
                 reads=list(bsrc), writes=[B_sq[i]])
            def fn(e, sq=sq, c=c):
                return e.matmul(pb[:, 0:n], lhsT=ones_b, rhs=sq, start=(c == 0), stop=(c == nch - 1))
            S.op("pe", fn, reads=[B_sq[i], B_c], writes=[bpb])
        S.op("act", lambda e: e.activation(out=rstd[:, 0:n], in_=pb[:, 0:n], func=AF.Ln, scale=1.0 / dn, bias=EPS),
             reads=[bpb], writes=[B_rstd])
        S.op("act", lambda e: e.activation(out=rstd[:, 0:n], in_=rstd[:, 0:n], func=AF.Exp, scale=-0.5),
             reads=[B_rstd], writes=[B_rstd])
        for c in range(nch):
            a = src(c)
            g = gn[:, gcol0 + c:gcol0 + c + 1]
            if resid is None:
                o = dst(c)
                S.op("dve", lambda e, o=o, a=a, g=g: e.scalar_tensor_tensor(
                    out=o, in0=a, scalar=g, in1=rstd[:, 0:n], op0=ALU.mult, op1=ALU.mult),
                    reads=list(bsrc) + [B_rstd, B_c], writes=list(bdst))
            else:
                i = rr["tf"] % NTF
                rr["tf"] += 1
                t = tmpf[i][:, 0:n]
                x = resid(c)
                S.op("dve", lambda e, t=t, a=a, g=g: e.scalar_tensor_tensor(
                    out=t, in0=a, scalar=g, in1=rstd[:, 0:n], op0=ALU.mult, op1=ALU.mult),
                    reads=list(bsrc) + [B_rstd, B_c], writes=[B_tf[i]])
                S.op("pool", lambda e, x=x, t=t: e.tensor_tensor(out=x, in0=x, in1=t, op=ALU.add),
                     reads=[B_tf[i]], writes=list(bdst))

    def lin4(wm, rhs, n, sink, bsrc, groups=None, outw=None):
        outw = outw or (wm.gw // 128)
        for g in (groups if groups is not None else range(wm.ng)):
            accs = accs4(n)
            for kp in range(wm.nkp):
                wv, wb = wload(wm, g, kp)
                ops = [(accs[m][0][:, 0:n], wv[:, k, m * 128:(m + 1) * 128], rhs(kp * wm.kper + k),
                        (kp == 0 and k == 0), (kp == wm.nkp - 1 and k == wm.kper - 1))
                       for k in range(wm.kper) for m in range(outw)]
                def fn(e, ops=ops):
                    ins = None
                    for o, l, r, a0, a1 in ops:
                        ins = e.matmul(o, lhsT=l, rhs=r, start=a0, stop=a1)
                    return ins
                S.op("pe", fn, reads=list(bsrc) + [wb], writes=ubufs([a[1] for a in accs[:outw]]))
            for m in range(outw):
                sink(g * outw + m, accs[m][0][:, 0:n], accs[m][1])

    def lin_tok(wm, lhs, nsub, mrows, sink, bsrc, groups):
        for g in groups:
            accs = accs4(512)
            for kp in range(wm.nkp):
                wv, wb = wload(wm, g, kp)
                ops = [(accs[s][0][:mrows, 0:wm.gw], lhs(kp * wm.kper + k, s), wv[:, k, :],
                        (kp == 0 and k == 0), (kp == wm.nkp - 1 and k == wm.kper - 1))
                       for k in range(wm.kper) for s in range(nsub)]
                def fn(e, ops=ops):
                    ins = None
                    for o, l, r, a0, a1 in ops:
                        ins = e.matmul(o, lhsT=l, rhs=r, start=a0, stop=a1)
                    return ins
                S.op("pe", fn, reads=list(bsrc) + [wb], writes=ubufs([a[1] for a in accs[:nsub]]))
            for s in range(nsub):
                sink(g, s, accs[s][0][:mrows, 0:wm.gw], accs[s][1])

    def attn_unit(nq, nk, terms, tbufs, vt, vbufs, o_dst, o_bufs, bufset):
        Sb, B_S, Pbs, B_Ps, PT, B_PT = bufset
        u = rr["u"] % 2
        rr["u"] += 1
        Pb, B_P = Pbs[u], B_Ps[u]
        if Sb is None:
            pa, bpa = PA[u], B_PA[u]
            for k0 in range(0, nk, 512):
                kn = min(512, nk - k0)
                mm(pa[:nq, k0:k0 + kn], terms(k0, kn), reads=list(tbufs), writes=[bpa])
            ssrc, bs = pa[:nq, 0:nk], bpa
        else:
            for s0 in range(0, nk, 1024):
                sn = min(1024, nk - s0)
                j = (s0 // 1024) % 2
                for k0 in range(s0, s0 + sn, 512):
                    kn = min(512, nk - k0)
                    mm(PA[j][:nq, k0 - s0:k0 - s0 + kn], terms(k0, kn), reads=list(tbufs), writes=[B_PA[j]])
                evac(Sb[:nq, s0:s0 + sn], PA[j][:nq, 0:sn], reads=[B_PA[j]], writes=[B_S], eng="act")
            ssrc, bs = Sb[:nq, 0:nk], B_S
        i = rr["sm"] % 8
        rr["sm"] += 1
        mx = sm[:nq, 4 * i:4 * i + 1]; nm = sm[:nq, 4 * i + 1:4 * i + 2]
        rs = sm[:nq, 4 * i + 2:4 * i + 3]; ri = sm[:nq, 4 * i + 3:4 * i + 4]
        bsm, brs = B_sm[i], B_rs[i]
        S.op("pool", lambda e: e.memset(rs, 0.0), reads=[], writes=[brs])
        S.op("dve", lambda e: e.reduce_max(out=mx, in_=ssrc, axis=AX.X), reads=[bs], writes=[bsm])
        S.op("dve", lambda e: e.tensor_scalar(out=nm, in0=mx, scalar1=-1.0, scalar2=None, op0=ALU.mult),
             reads=[bsm], writes=[bsm])
        S.op("act", lambda e: e.activation(out=Pb[:nq, 0:nk], in_=ssrc, func=AF.Exp, bias=nm, scale=1.0,
                                            accum_out=rs), reads=[bs, bsm], writes=[B_P, brs])
        nkt = (nk + 127) // 128

        def stageB():
            for g0 in range(0, nkt, 4):
                pb, bpb = nextpb()
                pbb = pb[:].bitcast(BF16)
                gn_ = min(4, nkt - g0)
                tops = [(pbb[:min(128, nk - 128 * t), (t - g0) * 128:(t - g0) * 128 + nq],
                         Pb[:nq, 128 * t:128 * t + min(128, nk - 128 * t)]) for t in range(g0, g0 + gn_)]
                def fn(e, tops=tops):
                    ins = None
                    for o, a_ in tops:
                        ins = e.transpose(out=o, in_=a_, identity=ident_b[:nq, :nq])
                    return ins
                S.op("pe", fn, reads=[B_P, B_c], writes=[bpb])
                full = (nk - 128 * (g0 + gn_ - 1)) >= 128
                gi = (g0 // 4) % 8
                if full:
                    evac(PT[:, g0:g0 + gn_, 0:nq], pbb[:, 0:gn_ * 128].rearrange("p (t q) -> p t q", q=128)[:, :, 0:nq],
                         reads=[bpb], writes=[B_PT[gi]], eng="dve")
                else:
                    for t in range(g0, g0 + gn_):
                        kn = min(128, nk - 128 * t)
                        evac(PT[:kn, t, 0:nq], pbb[:kn, (t - g0) * 128:(t - g0) * 128 + nq],
                             reads=[bpb], writes=[B_PT[gi]], eng="dve")
            S.op("dve", lambda e: e.reciprocal(out=ri, in_=rs), reads=[brs], writes=[brs])
            pv, bpv = nextpb()
            pvops = [(PT[:min(128, nk - 128 * t), t, 0:nq], vt(t, min(128, nk - 128 * t))) for t in range(nkt)]
            def fnpv(e):
                ins = None
                for t, (l, r) in enumerate(pvops):
                    ins = e.matmul(pv[:nq, 0:128], lhsT=l, rhs=r, start=(t == 0), stop=(t == nkt - 1))
                return ins
            S.op("pe", fnpv, reads=ubufs([B_PT[(g // 4) % 8] for g in range(0, nkt, 4)] + list(vbufs)), writes=[bpv])
            S.op("dve", lambda e: e.tensor_scalar(out=o_dst, in0=pv[:nq, 0:128], scalar1=ri, scalar2=None, op0=ALU.mult),
                 reads=[bpv, brs], writes=list(o_bufs))

        return stageB

    def run_units(makers):
        pend = None
        for mk_ in makers:
            nxt = mk_()
            if pend is not None:
                pend()
            pend = nxt
        if pend is not None:
            pend()

    def tr4_f32(dst, src, nrows_in, reads, writes):
        raise NotImplementedError

    def load_rows_T(dst3, bdst, src_rows, stage, bstage, nchunks=16):
        S.dma("pool", lambda e: e.dma_start(out=stage[:, 0:nchunks * 128], in_=src_rows), writes=[bstage])
        for c0 in range(0, nchunks, 4):
            pb, bpb = nextpb()
            def fn(e, c0=c0, pb=pb):
                ins = None
                for c in range(c0, c0 + 4):
                    ins = e.transpose(out=pb[:, (c - c0) * 128:(c - c0 + 1) * 128],
                                      in_=stage[:, c * 128:(c + 1) * 128], identity=idf[:])
                return ins
            S.op("pe", fn, reads=[bstage, B_c], writes=[bpb])
            evac(dst3(c0), pb[:, 0:512].rearrange("p (c t) -> p c t", t=128), reads=[bpb], writes=[bdst])

    def store_tok(src, bsrc, nch, np_, ntok, dst_rows, stage, bstage):
        for s in range(ntok // 128):
            j = s % 2
            for c0 in range(0, nch, 4):
                cn = min(4, nch - c0)
                pb, bpb = nextpb()
                tops = [(pb[:, (c - c0) * np_:(c - c0 + 1) * np_], src(c)[:, s * 128:(s + 1) * 128]) for c in range(c0, c0 + cn)]
                def fn(e, tops=tops):
                    ins = None
                    for o, a in tops:
                        ins = e.transpose(out=o, in_=a, identity=idf[:np_, :np_])
                    return ins
                S.op("pe", fn, reads=list(bsrc) + [B_c], writes=[bpb])
                evac(stage[j][:, c0 * np_:(c0 + cn) * np_], pb[:, 0:cn * np_], reads=[bpb], writes=[bstage[j]])
            S.dma("pool", lambda e, s=s, j=j: e.dma_start(out=dst_rows(s), in_=stage[j][:, 0:nch * np_]),
                  reads=[bstage[j]])

    G_MPRE, G_MPOST, G_FPRE, G_FPOST, G_QN, G_KVN = 0, 32, 64, 96, 128, 132

    def ffn(l, xT, bx, hT, bh, yT, by, hid, ntok):
        rms(lambda c: xT[:, c, 0:ntok], 16, ntok, D, G_FPRE + 16 * l, dst=lambda c: hT[:, c, 0:ntok],
            bsrc=[bx], bdst=[bh])
        bhid = [Buf() for _ in range(64)]
        def sink1(m, ps, bps):
            i = rr["tf"] % NTF
            rr["tf"] += 1
            t = tmpf[i][:, 0:ntok]
            S.op("act", lambda e: e.activation(out=t, in_=ps, func=AF.Relu), reads=[bps], writes=[B_tf[i]])
            eng = ("pool", "dve")[m % 2]
            S.op(eng, lambda e: e.tensor_tensor(out=hid[:, m, 0:ntok], in0=t, in1=t, op=ALU.mult),
                 reads=[B_tf[i]], writes=[bhid[m]])
        lin4(W1m[l], lambda k: hT[:, k, 0:ntok], ntok, sink1, [bh])
        def sink2(m, ps, bps):
            evac(yT[:, m, 0:ntok], ps, reads=[bps], writes=[by])
        lin4(W2m[l], lambda k: hid[:, k, 0:ntok], ntok, sink2, bhid)
        rms(lambda c: yT[:, c, 0:ntok], 16, ntok, D, G_FPOST + 16 * l, resid=lambda c: xT[:, c, 0:ntok],
            bsrc=[by], bdst=[bx])

    xT = v32(0, 8192).rearrange("p (c t) -> p c t", t=512)
    hT = v16(16 * K1, 8192).rearrange("p (c t) -> p c t", t=512)
    yT = v32(24 * K1, 8192).rearrange("p (c t) -> p c t", t=512)
    hid = v16(40 * K1, 32768).rearrange("p (f t) -> p f t", t=512)
    QT = v16(24 * K1, 8192).rearrange("p (h t) -> p h t", t=512)
    KT = v16(32 * K1, 16384).rearrange("p (h t) -> p h t", t=1024)
    Vt = v16(48 * K1, 16384).rearrange("p (s c) -> p s c", c=2048)
    otok = v16(64 * K1, 4096).rearrange("p (j c) -> p j c", c=2048)
    stg = [v32(68 * K1 + 4096 * j, 2048) for j in range(2)]
    btile = [v16(76 * K1 + 640 * j, 640) for j in range(2)]
    btile6 = [v16(72 * K1 + 640 * j, 640) for j in range(6)]
    B_bt6 = [Buf() for _ in range(6)]
    Pb0 = [v16(77 * K1 + 256 + 640 * j, 640) for j in range(2)]
    PT0 = v16(78 * K1 + 512, 640).rearrange("p (t q) -> p t q", q=128)
    B_P0, B_PT0 = [Buf(), Buf()], [Buf() for _ in range(8)]
    set0 = (None, None, Pb0, B_P0, PT0, B_PT0)
    oT = hT
    B_bt = [Buf(), Buf()]

    def l0_tile(t):
        smp = (t == 8)
        ntok = 128 if smp else 512
        nsub = ntok // 128
        S.barrier()
        bx, bh, bq, bk, bv, bo, by = Buf(), Buf(), Buf(), Buf(), Buf(), Buf(), Buf()
        bstg = [Buf(), Buf()]
        botok = [Buf(), Buf()]
        for s in range(nsub):
            rows = xs[:, :] if smp else xp[t * 512 + s * 128:t * 512 + (s + 1) * 128, :]
            load_rows_T(lambda c0, s=s: xT[:, c0:c0 + 4, s * 128:(s + 1) * 128], bx, rows, stg[s % 2], bstg[s % 2])
        rms(lambda c: xT[:, c, 0:ntok], 16, ntok, D, G_MPRE, dst=lambda c: hT[:, c, 0:ntok], bsrc=[bx], bdst=[bh])
        def sinkq(m, ps, bps):
            evac(QT[:, m, 0:ntok], ps, reads=[bps], writes=[bq], scale=SC_A)
        lin4(Wqkv, lambda k: hT[:, k, 0:ntok], ntok, sinkq, [bh], groups=range(0, 4))
        kcur = 640 if smp else 512
        def sinkk(m, ps, bps):
            evac(KT[:, m - 16, kcur:kcur + ntok], ps, reads=[bps], writes=[bk])
        lin4(Wqkv, lambda k: hT[:, k, 0:ntok], ntok, sinkk, [bh], groups=range(4, 8))
        if int(os.environ.get('STOP_AT', '99')) <= 1:
            return
        want_out = smp or t == 7
        if want_out:
            def sinkko(g, s, ps, bps):
                i = rr["tf"] % NTF
                rr["tf"] += 1
                evac(tmpf[i][:, :], ps, reads=[bps], writes=[B_tf[i]])
                dst = (nak_s if smp else nak_p)[s * 128:(s + 1) * 128, (g - 4) * 512:(g - 3) * 512]
                S.dma("pool", lambda e: e.dma_start(out=dst, in_=tmpf[i][:, :]), reads=[B_tf[i]])
            for s_ in range(nsub):
                def sk(g, s, ps, bps, s_=s_):
                    sinkko(g, s_, ps, bps)
                lin_tok(Wqkv, lambda k, s, s_=s_: hT[:, k, s_ * 128:(s_ + 1) * 128], 1, 128, sk, [bh], groups=range(4, 8))
        if int(os.environ.get('STOP_AT', '99')) <= 2:
            return
        if not smp:
            def sinkv(g, s, ps, bps):
                evac(Vt[:, 4 + s, (g - 8) * 512:(g - 7) * 512], ps, reads=[bps], writes=[bv])
            lin_tok(Wqkv, lambda k, s: hT[:, k, s * 128:(s + 1) * 128], nsub, 128, sinkv, [bh], groups=range(8, 12))
            if want_out:
                for s_ in range(nsub):
                    def sv(g, s, ps, bps, s_=s_):
                        i = rr["tf"] % NTF
                        rr["tf"] += 1
                        evac(tmpf[i][:, :], ps, reads=[bps], writes=[B_tf[i]])
                        dst = nav_p[s_ * 128:(s_ + 1) * 128, (g - 8) * 512:(g - 7) * 512]
                        S.dma("pool", lambda e: e.dma_start(out=dst, in_=tmpf[i][:, :]), reads=[B_tf[i]])
                    lin_tok(Wqkv, lambda k, s, s_=s_: hT[:, k, s_ * 128:(s_ + 1) * 128], 1, 128, sv, [bh], groups=range(8, 12))
        else:
            def sinkvs(g, a, ps, bps):
                evac(Vt[:64, 5 + a, (g - 8) * 512:(g - 7) * 512], ps, reads=[bps], writes=[bv])
            def sinkvo(g, s, ps, bps):
                i = rr["tf"] % NTF
                rr["tf"] += 1
                evac(tmpf[i][:, :], ps, reads=[bps], writes=[B_tf[i]])
                dst = nav_s[:, (g - 8) * 512:(g - 7) * 512]
                S.dma("pool", lambda e: e.dma_start(out=dst, in_=tmpf[i][:, :]), reads=[B_tf[i]])
            lin_tok(Wqkv, lambda k, s: hT[:, k, 0:128], 1, 128, sinkvo, [bh], groups=range(8, 12))
            lin_tok(Wqkv, lambda k, a: hT[:, k, a * 64:(a + 1) * 64], 2, 64, sinkvs, [bh], groups=range(8, 12))
        if int(os.environ.get('STOP_AT', '99')) <= 3:
            return
        if not smp:
            if t < 7:
                S.dma("pool", lambda e, t=t: e.dma_start(out=KTs[:, :, t * 512:(t + 1) * 512], in_=KT[:, :, 512:1024]),
                      reads=[bk], writes=[B_kv[t]])
                S.dma("pool", lambda e, t=t: e.dma_start(
                    out=Vs[t * 512:(t + 1) * 512, :].rearrange("(s p) c -> p s c", p=128), in_=Vt[:, 4:8, :]),
                    reads=[bv], writes=[B_kv[t]])
            if t == 0:
                S.op("pool", lambda e: e.memset(KT[:, :, 0:512], 0.0), writes=[bk])
                S.op("pool", lambda e: e.memset(Vt[:, 0:4, :], 0.0), writes=[bv])
            else:
                S.dma("pool", lambda e, t=t: e.dma_start(out=KT[:, :, 0:512], in_=KTs[:, :, (t - 1) * 512:t * 512]),
                      reads=[B_kv[t - 1]], writes=[bk])
                S.dma("pool", lambda e, t=t: e.dma_start(
                    out=Vt[:, 0:4, :], in_=Vs[(t - 1) * 512:t * 512, :].rearrange("(s p) c -> p s c", p=128)),
                    reads=[B_kv[t - 1]], writes=[bv])
        kindb = mk[0:10, 0:640]

        def band_units(s, a=None, t=t):
            nq = 128 if a is None else 64
            nk = 640 if a is None else 576
            q0 = s * 128 if a is None else a * 64
            w0 = s * 128 if a is None else 0
            var = min(4 * t + s, 4) if a is None else None
            j = (s if a is None else a) % 2
            def mk_unit(h):
                if a is None:
                    bt_, bbt = btile6[h % 6], B_bt6[h % 6]
                else:
                    bt_, bbt = btile[h % 2], B_bt[h % 2]
                S.dma("sp", lambda e, h=h, bt_=bt_: e.dma_start(out=bt_, in_=relB[h]), reads=[B_relB], writes=[bbt])

                def terms(k0, kn, h=h, bt_=bt_):
                    l = [(QT[:, h, q0:q0 + nq], KT[:, h, w0 + k0:w0 + k0 + kn]),
                         (ident_b[:nq, :nq], bt_[:nq, k0:k0 + kn])]
                    if var is not None:
                        l.append((mk[0:10, 640 + var * 128:640 + (var + 1) * 128], kindb[:, k0:k0 + kn]))
                    return l

                def vt(kt, kn, h=h):
                    if a is not None and kt == 4:
                        return Vt[:kn, 5 + a, h * 128:(h + 1) * 128]
                    return Vt[:kn, (s if a is None else 0) + kt, h * 128:(h + 1) * 128]
                return attn_unit(nq, nk, terms, [bq, bk, bbt, B_c], vt, [bv],
                                 otok[:nq, j, h * 128:(h + 1) * 128], [botok[j]], set0)
            run_units([(lambda h=h: mk_unit(h)) for h in range(16)])
            for c0 in range(0, 16, 4):
                pb, bpb = nextpb()
                pbb = pb[:].bitcast(BF16)
                def fn(e, c0=c0, pbb=pbb):
                    ins = None
                    for c in range(c0, c0 + 4):
                        ins = e.transpose(out=pbb[:, (c - c0) * 128:(c - c0) * 128 + nq],
                                          in_=otok[:nq, j, c * 128:(c + 1) * 128], identity=ident_b[:nq, :nq])
                    return ins
                S.op("pe", fn, reads=[botok[j], B_c], writes=[bpb])
                evac(oT[:, c0:c0 + 4, q0:q0 + nq],
                     pbb[:, 0:512].rearrange("p (c q) -> p c q", q=128)[:, :, 0:nq], reads=[bpb], writes=[bo, bh])
        if not smp:
            for s in range(nsub):
                band_units(s)
        else:
            for a in ([] if os.environ.get('SKIP_A') else range(2)):
                for s4 in range(4):
                    load_rows_T(lambda c0, s4=s4: KT[:, c0:c0 + 4, s4 * 128:(s4 + 1) * 128], bk,
                                cak[a, s4 * 128:(s4 + 1) * 128, :], stg[s4 % 2], bstg[s4 % 2])
                for s4 in range(4):
                    j = s4 % 2
                    S.dma("pool", lambda e, a=a, s4=s4, j=j: e.dma_start(out=stg[j], in_=cav[a, s4 * 128:(s4 + 1) * 128, :]),
                          writes=[bstg[j]])
                    S.op("dve", lambda e, s4=s4, j=j: e.tensor_copy(out=Vt[:, s4, :], in_=stg[j]),
                         reads=[bstg[j]], writes=[bv])
                S.op("pool", lambda e, a=a: e.tensor_copy(out=KT[:, :, 512:576], in_=KT[:, :, 640 + a * 64:704 + a * 64]),
                     reads=[bk], writes=[bk])
                band_units(0, a)
        if int(os.environ.get('STOP_AT', '99')) <= 4:
            return
        S.barrier()
        by = Buf()
        def sinko(m, ps, bps):
            evac(yT[:, m, 0:ntok], ps, reads=[bps], writes=[by])
        lin4(Wao, lambda k: oT[:, k, 0:ntok], ntok, sinko, [bo, bh])
        rms(lambda c: yT[:, c, 0:ntok], 16, ntok, D, G_MPOST, resid=lambda c: xT[:, c, 0:ntok], bsrc=[by], bdst=[bx])
        if int(os.environ.get('STOP_AT', '99')) <= 5:
            return
        S.barrier()
        by = Buf(); bh = Buf()
        ffn(0, xT, bx, hT, bh, yT, by, hid, ntok)
        if smp:
            S.dma("pool", lambda e: e.dma_start(out=x1ss[:, :, :], in_=xT[:, :, 0:128]), reads=[bx], writes=[B_x1s])
        else:
            S.dma("pool", lambda e, t=t: e.dma_start(out=x1s[:, :, t * 512:(t + 1) * 512], in_=xT[:, :, :]),
                  reads=[bx], writes=[B_x1[t]])
        if int(os.environ.get('STOP_AT', '99')) <= 6:
            return
        S.barrier()
        by = Buf(); bh = Buf()
        rms(lambda c: xT[:, c, 0:ntok], 16, ntok, D, G_MPRE + 16, dst=lambda c: hT[:, c, 0:ntok], bsrc=[bx], bdst=[bh])
        def sinkc(m, ps, bps):
            evac(yT[:, m, 0:ntok], ps, reads=[bps], writes=[by])
        lin4(Wdkv_c, lambda k: hT[:, k, 0:ntok], ntok, sinkc, [bh])
        cs = v32(68 * K1, 1024).rearrange("p (a t) -> p a t", t=512)
        bcs, bkr = Buf(), Buf()
        cd, sd, p0 = (coss_d, sins_d, 0) if smp else (cosk_d, sink_d, t * 512)
        S.dma("pool", lambda e: e.dma_start(out=cs[:64, 0, 0:ntok], in_=cd[:, p0:p0 + ntok]), writes=[bcs])
        S.dma("pool", lambda e: e.dma_start(out=cs[:64, 1, 0:ntok], in_=sd[:, p0:p0 + ntok]), writes=[bcs])
        krf = yT[:64, 5, 0:ntok]
        def sinkr(m, ps, bps):
            pass
        accs = accs4(ntok)
        for kp in range(Wdkv_r.nkp):
            wv, wb = wload(Wdkv_r, 0, kp)
            ops = [(accs[m][0][:64, 0:ntok], wv[:, k, m * 64:(m + 1) * 64], hT[:, kp * Wdkv_r.kper + k, 0:ntok],
                    (kp == 0 and k == 0), (kp == Wdkv_r.nkp - 1 and k == Wdkv_r.kper - 1))
                   for k in range(Wdkv_r.kper) for m in range(2)]
            def fn(e, ops=ops):
                ins = None
                for o, l, r, a0, a1 in ops:
                    ins = e.matmul(o, lhsT=l, rhs=r, start=a0, stop=a1)
                return ins
            S.op("pe", fn, reads=[bh, wb], writes=ubufs([accs[0][1], accs[1][1]]))
        S.op("dve", lambda e: e.tensor_tensor(out=yT[:64, 4, 0:ntok], in0=accs[0][0][:64, 0:ntok], in1=cs[:64, 0, 0:ntok], op=ALU.mult),
             reads=[accs[0][1], bcs], writes=[bkr])
        S.op("dve", lambda e: e.tensor_tensor(out=krf, in0=accs[1][0][:64, 0:ntok], in1=cs[:64, 1, 0:ntok], op=ALU.mult),
             reads=[accs[1][1], bcs], writes=[bkr])
        S.op("pool", lambda e: e.tensor_tensor(out=krf, in0=krf, in1=yT[:64, 4, 0:ntok], op=ALU.add), reads=[bkr], writes=[bkr])
        bcn = Buf()
        rms(lambda c: yT[:, c, 0:ntok], 4, ntok, 512, G_KVN, dst=lambda c: yT[:, 8 + c, 0:ntok], bsrc=[by], bdst=[bcn])
        lat16 = hid[:, 40:45, :]
        bl16 = Buf()
        S.op("pool", lambda e: e.tensor_copy(out=lat16[:, 0:4, 0:ntok], in_=yT[:, 8:12, 0:ntok]), reads=[bcn], writes=[bl16])
        S.op("pool", lambda e: e.tensor_copy(out=lat16[:64, 4, 0:ntok], in_=krf), reads=[bkr], writes=[bl16])
        if smp:
            S.dma("pool", lambda e: e.dma_start(out=ckvTss[:, :, :], in_=lat16[:, 0:4, 0:128]), reads=[bl16], writes=[B_lats])
            S.dma("pool", lambda e: e.dma_start(out=krTss[:, :], in_=lat16[:64, 4, 0:128]), reads=[bl16], writes=[B_lats])
        else:
            S.dma("pool", lambda e, t=t: e.dma_start(out=ckvTs[:, :, t * 512:(t + 1) * 512], in_=lat16[:, 0:4, :]),
                  reads=[bl16], writes=[B_lat[t]])
            S.dma("pool", lambda e, t=t: e.dma_start(out=krTs[:, t * 512:(t + 1) * 512], in_=lat16[:64, 4, :]),
                  reads=[bl16], writes=[B_lat[t]])
        stg2 = [v32(72 * K1 + 2048 * j, 1024) for j in range(2)]
        bst2 = [Buf(), Buf()]
        if smp:
            store_tok(lambda c: yT[:, 8 + c, :], [bcn], 4, 128, ntok, lambda s: nckv_s[:, :], stg2, bst2)
            store_tok(lambda c: yT[:64, 5, :], [bkr], 1, 64, ntok, lambda s: nkr_s[:, :], stg2, bst2)
        else:
            store_tok(lambda c: yT[:, 8 + c, :], [bcn], 4, 128, ntok,
                      lambda s, t=t: nckv_p[t * 512 + s * 128:t * 512 + (s + 1) * 128, :], stg2, bst2)
            store_tok(lambda c: yT[:64, 5, :], [bkr], 1, 64, ntok,
                      lambda s, t=t: nkr_p[t * 512 + s * 128:t * 512 + (s + 1) * 128, :], stg2, bst2)
        if t == 0:
            for wm in [Wdq] + Wuq + [Wuk, Wuv, Wbo, W1m[1], W2m[1]]:
                wm.cast()

    for t_ in (range(l0_tiles) if isinstance(l0_tiles, int) else l0_tiles):
        l0_tile(t_)

    cq = v16(0, 2048).rearrange("p (c t) -> p c t", t=512)
    ckvT = v16(2 * K1, 16384).rearrange("p (c t) -> p c t", t=4096)
    krT = v16(18 * K1, 4096)
    kind = v16(22 * K1, 4096)
    qm = v16(26 * K1, 2048)
    knT = v16(28 * K1, 4096)
    v2 = v16(32 * K1, 8192).rearrange("p (t c) -> p t c", c=256)
    qn = v16(40 * K1, 1024).rearrange("p (j t) -> p j t", t=512)
    qr = v16(41 * K1, 1024).rearrange("p (j t) -> p j t", t=512)
    otk = v16(42 * K1, 8192).rearrange("p (s c) -> p s c", c=2048)
    rtab = v32(50 * K1, 1024).rearrange("p (a t) -> p a t", t=512)
    rtmp = v32(52 * K1, 1024).rearrange("p (a t) -> p a t", t=512)
    stq = [v32(54 * K1 + 1024 * j, 512) for j in range(2)]
    Sb1 = v32(56 * K1, 4096)
    Pb1 = v16(64 * K1, 4096)
    PT1 = v16(68 * K1, 4096).rearrange("p (t q) -> p t q", q=128)
    ovT = v16(72 * K1, 8192).rearrange("p (h t) -> p h t", t=512)
    Pb1b = st.enter_context(nc.sbuf_tensor("pb1b", [128, 4096], BF16))
    set1 = (Sb1, Buf(), [Pb1, Pb1b[:, :]], [Buf(), Buf()], PT1, [Buf() for _ in range(8)])
    B_keys, B_mask, B_kn, B_v2, B_qh, B_otk, B_rtab, B_rtmp, B_ov, B_cq = (
        Buf(), Buf(), Buf(), Buf(), [Buf(), Buf()], Buf(), Buf(), Buf(), Buf(), Buf())
    B_stq = [Buf(), Buf()]

    def l1_tile(i):
        smp = (i == 4)
        ntok = 128 if smp else 512
        nsub = ntok // 128
        S.barrier()
        bx, bh, by = Buf(), Buf(), Buf()

        def load_x1(bx, by, i=i, smp=smp):
            if smp:
                S.dma("pool", lambda e: e.dma_start(out=xT[:, :, 0:128], in_=x1ss[:, :, :]), reads=[B_x1s], writes=[bx])
            else:
                S.dma("pool", lambda e: e.dma_start(out=xT[:, :, :], in_=x1s[:, :, i * 512:(i + 1) * 512]),
                      reads=[B_x1[i]], writes=[bx])
                S.dma("pool", lambda e: e.dma_start(out=yT[:, :, :], in_=x1s[:, :, 2048 + i * 512:2048 + (i + 1) * 512]),
                      reads=[B_x1[4 + i]], writes=[by])
                S.op("dve", lambda e: e.tensor_scalar(out=yT[:, :, :], in0=yT[:, :, :], scalar1=sel[:, 1:2], scalar2=None,
                                                     op0=ALU.mult), reads=[by, B_c], writes=[by])
                S.op("dve", lambda e: e.scalar_tensor_tensor(out=xT[:, :, :], in0=xT[:, :, :], scalar=sel[:, 0:1],
                                                            in1=yT[:, :, :], op0=ALU.mult, op1=ALU.add),
                     reads=[by, B_c], writes=[bx])
        load_x1(bx, by)
        rms(lambda c: xT[:, c, 0:ntok], 16, ntok, D, G_MPRE + 16, dst=lambda c: hT[:, c, 0:ntok], bsrc=[bx], bdst=[bh])
        by = Buf()
        def sinkcq(m, ps, bps):
            evac(yT[:, m, 0:ntok], ps, reads=[bps], writes=[by])
        lin4(Wdq, lambda k: hT[:, k, 0:ntok], ntok, sinkcq, [bh])
        rms(lambda c: yT[:, c, 0:ntok], 4, ntok, 512, G_QN, dst=lambda c: cq[:, c, 0:ntok], bsrc=[by], bdst=[B_cq, bx])
        S.barrier()
        def seq_body(a):
            if a is None:
                nk = 2048 + 512 * (i + 1)
                S.dma("pool", lambda e, nk=nk: e.dma_start(out=ckvT[:, :, 0:nk], in_=ckvTs[:, :, 0:nk]),
                      reads=B_lat, writes=[B_keys])
                S.dma("pool", lambda e, nk=nk: e.dma_start(out=krT[:64, 0:nk], in_=krTs[:, 0:nk]), reads=B_lat, writes=[B_keys])
                S.dma("pool", lambda e: e.dma_start(out=kind[:64, :], in_=kind_d), writes=[B_mask])
                S.dma("pool", lambda e: e.dma_start(out=qm[:64, :], in_=qm_d), writes=[B_mask])
            else:
                nk = 2112
                for s16 in range(16):
                    j = s16 % 2
                    load_rows_T(lambda c0, s16=s16: ckvT[:, c0:c0 + 4, s16 * 128:(s16 + 1) * 128], B_keys,
                                cckv[a, s16 * 128:(s16 + 1) * 128, :], stq[j], B_stq[j], nchunks=4)
                for s4 in range(4):
                    j = s4 % 2
                    S.dma("pool", lambda e, a=a, s4=s4, j=j: e.dma_start(
                        out=stq[j][:, 0:256].rearrange("p (s r) -> p s r", r=64),
                        in_=ckr[a, s4 * 512:(s4 + 1) * 512, :].rearrange("(s p) r -> p s r", p=128)), writes=[B_stq[j]])
                    pb, bpb = nextpb()
                    def fn(e, pb=pb, j=j):
                        ins = None
                        for c in range(4):
                            ins = e.transpose(out=pb[:64, c * 128:(c + 1) * 128], in_=stq[j][:, c * 64:(c + 1) * 64], identity=idf[:])
                        return ins
                    S.op("pe", fn, reads=[B_stq[j], B_c], writes=[bpb])
                    evac(krT[:64, s4 * 512:(s4 + 1) * 512], pb[:64, 0:512], reads=[bpb], writes=[B_keys])
                S.dma("pool", lambda e, a=a: e.dma_start(out=ckvT[:, :, 2048:2112], in_=ckvTss[:, :, a * 64:(a + 1) * 64]),
                      reads=[B_lats], writes=[B_keys])
                S.dma("pool", lambda e, a=a: e.dma_start(out=krT[:64, 2048:2112], in_=krTss[:, a * 64:(a + 1) * 64]),
                      reads=[B_lats], writes=[B_keys])
            nq_tok = ntok if a is None else 64
            q0 = 0 if a is None else a * 64
            if smp:
                S.dma("pool", lambda e, q0=q0: e.dma_start(out=rtab[:64, 0, 0:64], in_=coss_d[:, q0:q0 + 64]), writes=[B_rtab])
                S.dma("pool", lambda e, q0=q0: e.dma_start(out=rtab[:64, 1, 0:64], in_=sins_d[:, q0:q0 + 64]), writes=[B_rtab])
            else:
                S.dma("pool", lambda e: e.dma_start(out=rtab[:64, 0, :], in_=cosq_d[:, i * 512:(i + 1) * 512]), writes=[B_rtab])
                S.dma("pool", lambda e: e.dma_start(out=rtab[:64, 1, :], in_=sinq_d[:, i * 512:(i + 1) * 512]), writes=[B_rtab])
            nkt = (nk + 127) // 128
            def head_body(h):
                j = h % 2
                if h % 2 == 0:
                    wvv, bwv = wload(Wuv, h // 2, 0)
                    for g0 in range(0, nkt, 2):
                        pb, bpb = nextpb()
                        gcnt = min(2, nkt - g0)
                        ops = [(pb[:min(128, nk - 128 * kt), (kt - g0) * 256:(kt - g0 + 1) * 256],
                                ckvT[:, c, kt * 128:kt * 128 + min(128, nk - 128 * kt)], wvv[:, c, :], c == 0, c == 3)
                               for kt in range(g0, g0 + gcnt) for c in range(4)]
                        def fn(e, ops=ops):
                            ins = None
                            for o, l, r, a0, a1 in ops:
                                ins = e.matmul(o, lhsT=l, rhs=r, start=a0, stop=a1)
                            return ins
                        S.op("pe", fn, reads=[B_keys, bwv], writes=[bpb])
                        for kt in range(g0, g0 + gcnt):
                            kn = min(128, nk - 128 * kt)
                            evac(v2[:kn, kt, :], pb[:kn, (kt - g0) * 256:(kt - g0 + 1) * 256], reads=[bpb], writes=[B_v2])
                wq, bw = wload(Wuq[h], 0, 0)
                pb, bpb = nextpb()
                mm(pb[:, 0:nq_tok], [(wq[:, c, 0:128], cq[:, c, q0:q0 + nq_tok]) for c in range(4)], reads=[bw, B_cq], writes=[bpb])
                evac(qn[:, j, 0:nq_tok], pb[:, 0:nq_tok], reads=[bpb], writes=[B_qh[j]], scale=SC_B)
                pr, bpr = nextpb()
                mm(pr[:64, 0:nq_tok], [(wq[:, c, 128:192], cq[:, c, q0:q0 + nq_tok]) for c in range(4)], reads=[bw, B_cq], writes=[bpr])
                pw, bpw = nextpb()
                mm(pw[:64, 0:nq_tok], [(wq[:, c, 192:256], cq[:, c, q0:q0 + nq_tok]) for c in range(4)], reads=[bw, B_cq], writes=[bpw])
                S.op("dve", lambda e, pr=pr: e.tensor_tensor(out=rtmp[:64, 0, 0:nq_tok], in0=pr[:64, 0:nq_tok],
                                                             in1=rtab[:64, 0, 0:nq_tok], op=ALU.mult), reads=[bpr, B_rtab], writes=[B_rtmp])
                S.op("dve", lambda e, pw=pw: e.tensor_tensor(out=rtmp[:64, 1, 0:nq_tok], in0=pw[:64, 0:nq_tok],
                                                             in1=rtab[:64, 1, 0:nq_tok], op=ALU.mult), reads=[bpw, B_rtab], writes=[B_rtmp])
                S.op("pool", lambda e: e.tensor_tensor(out=rtmp[:64, 0, 0:nq_tok], in0=rtmp[:64, 0, 0:nq_tok],
                                                       in1=rtmp[:64, 1, 0:nq_tok], op=ALU.add), reads=[B_rtmp], writes=[B_rtmp])
                S.op("pool", lambda e, j=j: e.tensor_scalar(out=qr[:64, j, 0:nq_tok], in0=rtmp[:64, 0, 0:nq_tok], scalar1=SC_B,
                                                            scalar2=None, op0=ALU.mult), reads=[B_rtmp], writes=[B_qh[j]])
                wk, bwk = wload(Wuk, h, 0)
                for k0 in range(0, nk, 512):
                    kn = min(512, nk - k0)
                    pb, bpb = nextpb()
                    mm(pb[:, 0:kn], [(wk[:, c, :], ckvT[:, c, k0:k0 + kn]) for c in range(4)], reads=[bwk, B_keys], writes=[bpb])
                    evac(knT[:, k0:k0 + kn], pb[:, 0:kn], reads=[bpb], writes=[B_kn])
                def unit_body(s):
                    if a is None:
                        jq = 4 * i + s
                        nk_u = 128 * (17 + jq)
                        nq, qc0, srow = 128, s * 128, s
                    else:
                        jq, nk_u, nq, qc0, srow = None, 2112, 64, 0, a

                    def terms(k0, kn, j=j, qc0=qc0, nq=nq, jq=jq):
                        l = [(qn[:, j, qc0:qc0 + nq], knT[:, k0:k0 + kn]),
                             (qr[:64, j, qc0:qc0 + nq], krT[:64, k0:k0 + kn])]
                        if jq is not None:
                            l.append((qm[:64, jq * 128:(jq + 1) * 128], kind[:64, k0:k0 + kn]))
                        return l

                    def vt(kt, kn, h=h):
                        return v2[:kn, kt, (h % 2) * 128:(h % 2 + 1) * 128]
                    return attn_unit(nq, nk_u, terms, [B_qh[j], B_kn, B_keys, B_mask], vt, [B_v2],
                                     otk[:nq, srow, h * 128:(h + 1) * 128], [B_otk], set1)
                run_units([(lambda s_=s_: unit_body(s_)) for s_ in range(nsub if a is None else 1)])
            for h_ in range(16):
                head_body(h_)
            for s in range(nsub if a is None else 1):
                srow = s if a is None else a
                nq = 128 if a is None else 64
                c0q = s * 128 if a is None else a * 64
                for c0 in range(0, 16, 4):
                    pb, bpb = nextpb()
                    pbb = pb[:].bitcast(BF16)
                    def fn(e, c0=c0, pbb=pbb, srow=srow, nq=nq):
                        ins = None
                        for c in range(c0, c0 + 4):
                            ins = e.transpose(out=pbb[:, (c - c0) * 128:(c - c0) * 128 + nq],
                                              in_=otk[:nq, srow, c * 128:(c + 1) * 128], identity=ident_b[:nq, :nq])
                        return ins
                    S.op("pe", fn, reads=[B_otk, B_c], writes=[bpb])
                    evac(ovT[:, c0:c0 + 4, c0q:c0q + nq],
                         pbb[:, 0:512].rearrange("p (c q) -> p c q", q=128)[:, :, 0:nq], reads=[bpb], writes=[B_ov])
        for a_ in ([None] if not smp else [0, 1]):
            seq_body(a_)
        S.barrier()
        bx, bh, by = Buf(), Buf(), Buf()
        load_x1(bx, by)
        by = Buf()
        def sinkbo(m, ps, bps):
            evac(yT[:, m, 0:ntok], ps, reads=[bps], writes=[by])
        lin4(Wbo, lambda k: ovT[:, k, 0:ntok], ntok, sinkbo, [B_ov, bx])
        rms(lambda c: yT[:, c, 0:ntok], 16, ntok, D, G_MPOST + 16, resid=lambda c: xT[:, c, 0:ntok], bsrc=[by], bdst=[bx])
        S.barrier()
        by = Buf()
        ffn(1, xT, bx, hT, bh, yT, by, hid, ntok)
        S.barrier()
        bstg = [Buf(), Buf()]
        stgy = [v32(24 * K1 + 4096 * j, 2048) for j in range(2)]
        if smp:
            store_tok(lambda c: xT[:, c, :], [bx], 16, 128, ntok, lambda s: y_s[:, :], stgy, bstg)
        else:
            store_tok(lambda c: xT[:, c, :], [bx], 16, 128, ntok,
                      lambda s, i=i: y_p[i * 512 + s * 128:i * 512 + (s + 1) * 128, :], stgy, bstg)

    for i_ in (range(l1_tiles) if isinstance(l1_tiles, int) else l1_tiles):
        l1_tile(i_)
    S.emit(st)
    st.close()
    return nc


def _pos_tables(pos):
    inv = 1.0 / (10000.0 ** (np.arange(32, dtype=np.float32) / 32.0))
    ang = pos.astype(np.float32)[None, :] * inv[:, None].astype(np.float32)
    c = np.cos(ang).astype(np.float32); s = np.sin(ang).astype(np.float32)
    return np.concatenate([c, c], 0), np.concatenate([-s, s], 0)


def _consts(half):
    bf = ml_dtypes.bfloat16
    ident = np.eye(128, dtype=np.float32)
    cb = np.concatenate([ident, np.ones((128, 128), np.float32)], 1).astype(bf)
    kindb = np.zeros((10, 640), np.float32)
    for c in range(10):
        kindb[c, c * 64:(c + 1) * 64] = 1.0
    qmb = np.zeros((10, 5 * 128), np.float32)
    for var in range(5):
        for r in range(128):
            qc = r // 64
            for c in range(10):
                masked = (c > 8 + qc) or (c < qc)
                if var < 4 and c < 8 - 2 * var:
                    masked = True
                qmb[c, var * 128 + r] = NEG if masked else 0.0
    kind = np.zeros((64, 4096), np.float32)
    for c in range(64):
        kind[c, c * 64:(c + 1) * 64] = 1.0
    qm = np.zeros((64, 2048), np.float32)
    for r in range(2048):
        qc = (half * 2048 + r) // 64
        qm[qc + 1:, r] = NEG
    sel = np.zeros((128, 2), np.float32)
    sel[:, half] = 1.0
    cosk, sink = _pos_tables(np.arange(4096))
    cosq, sinq = _pos_tables(half * 2048 + np.arange(2048))
    p = 2048 + np.arange(64)
    coss, sins = _pos_tables(np.concatenate([p, p]))
    return dict(ident_f=ident, cb=cb, kindb=kindb.astype(bf), qmb=qmb.astype(bf), kind=kind.astype(bf),
                qm=qm.astype(bf), sel=sel, cosk=cosk, sink=sink, cosq=cosq, sinq=sinq, coss=coss, sins=sins)


def _colmajor(g):
    return np.ascontiguousarray(g.reshape(-1, 128).T)


_NC_CACHE = {}


def prep(inp):
    f = lambda a: np.ascontiguousarray(np.asarray(a, dtype=np.float32))
    x_prompt, x_sample = f(inp["x_prompt"]), f(inp["x_sample"])
    gains = np.concatenate(
        [_colmajor(f(inp[n])[l]) for n in ("ln_mix_pre", "ln_mix_post", "ln_ffn_pre", "ln_ffn_post") for l in range(2)]
        + [_colmajor(f(inp["mla_q_norm"])[0]), _colmajor(f(inp["mla_kv_norm"])[0])], axis=1)
    rb = f(inp["a_rel_bias"])[0]
    r = np.arange(128)[:, None]; w = np.arange(640)[None, :]
    relT = np.ascontiguousarray(rb[:, np.clip(512 + r - w, -256, 256) + 256])
    wuq = f(inp["mla_w_uq"])[0]
    w_uq = np.ascontiguousarray(np.concatenate([wuq, wuq[:, :, 160:192], wuq[:, :, 128:160]], axis=2))
    wdkv = f(inp["mla_w_dkv"])[0]
    w_dkv = np.ascontiguousarray(np.concatenate([wdkv, wdkv[:, 544:576], wdkv[:, 512:544]], axis=1))
    shared = dict(
        gains=np.ascontiguousarray(gains), w_qkv=f(inp["a_w_qkv"])[0], w_ao=f(inp["a_w_o"])[0], relT=relT,
        w_dq=f(inp["mla_w_dq"])[0], w_uq=w_uq, w_dkv=w_dkv,
        w_uk=f(inp["mla_w_uk"])[0].reshape(512, 2048), w_uv=f(inp["mla_w_uv"])[0].reshape(512, 2048),
        w_bo=f(inp["mla_w_o"])[0], w1=f(inp["ffn_w1"]), w2=f(inp["ffn_w2"]))
    cak, cav = f(inp["cache_a_k"])[0].reshape(16, 512, 2048), f(inp["cache_a_v"])[0].reshape(16, 512, 2048)
    cckv, ckr = f(inp["cache_mla_ckv"])[0], f(inp["cache_mla_kr"])[0]
    consts = [_consts(0), _consts(1)]
    in_maps = []
    for c in range(N_CORES):
        b, half = c // 2, c % 2
        m = dict(shared)
        m.update(consts[half])
        m.update(xp=x_prompt[b], xs=x_sample[2 * c:2 * c + 2].reshape(128, 2048),
                 cak=cak[2 * c:2 * c + 2], cav=cav[2 * c:2 * c + 2],
                 cckv=cckv[2 * c:2 * c + 2], ckr=ckr[2 * c:2 * c + 2])
        in_maps.append(m)
    return in_maps


def post(res):
    R = lambda c, n: np.asarray(res[c][n], dtype=np.float32)
    y_prompt = np.stack([np.concatenate([R(2 * b, "y_p"), R(2 * b + 1, "y_p")], 0) for b in range(4)], 0)
    y_sample = np.concatenate([R(c, "y_s").reshape(2, 64, 2048) for c in range(8)], 0)
    nakp = np.stack([R(2 * b + 1, "nak_p") for b in range(4)], 0).reshape(1, 4, 512, 16, 128)
    navp = np.stack([R(2 * b + 1, "nav_p") for b in range(4)], 0).reshape(1, 4, 512, 16, 128)
    naks = np.concatenate([R(c, "nak_s").reshape(2, 64, 16, 128) for c in range(8)], 0)[None]
    navs = np.concatenate([R(c, "nav_s").reshape(2, 64, 16, 128) for c in range(8)], 0)[None]
    ckvp = np.stack([np.concatenate([R(2 * b, "nckv_p")[:2048], R(2 * b + 1, "nckv_p")[2048:]], 0) for b in range(4)], 0)[None]
    krp = np.stack([np.concatenate([R(2 * b, "nkr_p")[:2048], R(2 * b + 1, "nkr_p")[2048:]], 0) for b in range(4)], 0)[None]
    ckvs = np.concatenate([R(c, "nckv_s").reshape(2, 64, 512) for c in range(8)], 0)[None]
    krs = np.concatenate([R(c, "nkr_s").reshape(2, 64, 64) for c in range(8)], 0)[None]
    return (y_prompt, y_sample, nakp, navp, naks, navs, ckvp, krp, ckvs, krs)


def kernel(**inp):
    in_maps = prep(inp)
    if "nc" not in _NC_CACHE:
        _NC_CACHE["nc"] = build_nc()
    res = run_bass_kernel_spmd(_NC_CACHE["nc"], in_maps, core_ids=list(range(N_CORES))).results
    return post(res)
```

```python
import contextlib
import os
import numpy as np
import ml_dtypes
import concourse.bass as bass
import concourse.mybir as mybir
from concourse.bass_utils import run_bass_kernel_spmd

F32 = mybir.dt.float32
BF16 = mybir.dt.bfloat16
AF = mybir.ActivationFunctionType
ALU = mybir.AluOpType
AX = mybir.AxisListType

N_CORES = 8
D = 2048
SEQ = 4096
HALF = 2048
TT = 512
NEG = -1.0e30
EPS = 1e-6
SC_A = 128 ** -0.5
SC_B = 192 ** -0.5


class Buf:
    __slots__ = ("name", "w", "r")

    def __init__(self, name=""):
        self.name = name
        self.w = None
        self.r = {}


class Sched:
    NSLOT = 8
    ENGS = ("pe", "act", "dve", "pool", "sp")

    def __init__(self, nc):
        self.nc = nc
        self.prog = {k: [] for k in self.ENGS}
        self.cnt = {k: 0 for k in self.ENGS}
        self.dcnt = {"sp": 0, "pool": 0}
        self.waited = {k: {} for k in self.ENGS}
        self.sems = {}

    def _need(self, eng, tok, waits):
        if tok is None:
            return
        key, val = tok
        if key == ("eng", "pe") and eng == "pe":
            return
        if self.waited[eng].get(key, 0) >= val:
            return
        self.waited[eng][key] = val
        waits.append((key, val))

    def _deps(self, eng, reads, writes, waits):
        for b in reads:
            self._need(eng, b.w, waits)
        for b in writes:
            self._need(eng, b.w, waits)
            for key, val in b.r.items():
                self._need(eng, (key, val), waits)

    def _commit(self, tok, reads, writes):
        key, val = tok
        for b in reads:
            if b.r.get(key, 0) < val:
                b.r[key] = val
        for b in writes:
            b.w = tok
            b.r = {}

    def op(self, eng, fn, reads=(), writes=()):
        waits = []
        self._deps(eng, reads, writes, waits)
        self.cnt[eng] += 1
        tok = (("eng", eng), self.cnt[eng])
        self.prog[eng].append((waits, fn, ("eng", eng), 1))
        self._commit(tok, reads, writes)
        return tok

    def dma(self, q, fn, reads=(), writes=()):
        waits = []
        i = self.dcnt[q]
        self.dcnt[q] += 1
        slot = i % self.NSLOT
        key = ("dma", q, slot)
        if i >= self.NSLOT:
            self._need(q, (key, 16 * (i // self.NSLOT)), waits)
        self._deps(q, reads, writes, waits)
        tok = (key, 16 * (i // self.NSLOT + 1))
        self.prog[q].append((waits, fn, key, 16))
        self._commit(tok, reads, writes)
        return tok

    def _all_tokens(self, queues):
        toks = [(("eng", e), c) for e, c in self.cnt.items() if c > 0 and e != "sp"]
        for q in queues:
            n = self.dcnt[q]
            for s in range(min(n, self.NSLOT)):
                toks.append((("dma", q, s), 16 * ((n - s + self.NSLOT - 1) // self.NSLOT)))
        return toks

    def barrier(self, final=False):
        toks = self._all_tokens(("sp", "pool") if final else ("pool",))
        for e in (self.ENGS if final else ("pe", "act", "dve", "pool")):
            waits = []
            for t in toks:
                if t[0] == ("eng", e):
                    continue
                if self.waited[e].get(t[0], 0) >= t[1]:
                    continue
                self.waited[e][t[0]] = t[1]
                waits.append(t)
            if waits:
                self.prog[e].append((waits, None, None, 0))

    def emit(self, st):
        nc = self.nc
        keys = [("eng", e) for e in self.ENGS]
        for q in self.dcnt:
            keys += [("dma", q, s) for s in range(self.NSLOT)]
        for k in keys:
            self.sems[k] = st.enter_context(nc.semaphore("s_" + "_".join(map(str, k))))
        self.barrier(final=True)
        block = st.enter_context(nc.Block())

        def replay(name):
            def run(e):
                for waits, fn, key, inc in self.prog[name]:
                    for k, v in waits:
                        e.wait_ge(self.sems[k], v)
                    if fn is not None:
                        fn(e).then_inc(self.sems[key], inc)
            return run

        block.tensor(replay("pe"))
        block.scalar(replay("act"))
        block.vector(replay("dve"))
        block.gpsimd(replay("pool"))
        block.sync(replay("sp"))


def build_nc(l0_tiles=9, l1_tiles=5):
    nc = bass.Bass("TRN2", target_bir_lowering=False)
    st = contextlib.ExitStack()
    S = Sched(nc)

    def din(name, shape, dt=F32):
        return nc.dram_tensor(name, list(shape), dt, kind="ExternalInput").ap()

    def dout(name, shape):
        return nc.dram_tensor(name, list(shape), F32, kind="ExternalOutput").ap()

    def dscr(name, shape, dt):
        return nc.dram_tensor(name, list(shape), dt).ap()

    xp = din("xp", [SEQ, D]); xs = din("xs", [128, D])
    cak = din("cak", [2, 512, D]); cav = din("cav", [2, 512, D])
    cckv = din("cckv", [2, 2048, 512]); ckr = din("ckr", [2, 2048, 64])
    gains = din("gains", [128, 136])
    w_qkv = din("w_qkv", [D, 6144]); w_ao = din("w_ao", [D, D])
    relT = din("relT", [16, 128, 640])
    w_dq = din("w_dq", [D, 512]); w_uq = din("w_uq", [512, 16, 256])
    w_dkv = din("w_dkv", [D, 640]); w_uk = din("w_uk", [512, D]); w_uv = din("w_uv", [512, D])
    w_bo = din("w_bo", [D, D])
    w1 = din("w1", [2, D, 4 * D]); w2 = din("w2", [2, 4 * D, D])
    ident_f_d = din("ident_f", [128, 128]); cb_d = din("cb", [128, 256], BF16)
    kindb_d = din("kindb", [10, 640], BF16); qmb_d = din("qmb", [10, 640], BF16)
    kind_d = din("kind", [64, 4096], BF16); qm_d = din("qm", [64, 2048], BF16)
    sel_d = din("sel", [128, 2])
    cosk_d = din("cosk", [64, SEQ]); sink_d = din("sink", [64, SEQ])
    cosq_d = din("cosq", [64, HALF]); sinq_d = din("sinq", [64, HALF])
    coss_d = din("coss", [64, 128]); sins_d = din("sins", [64, 128])
    y_p = dout("y_p", [HALF, D]); y_s = dout("y_s", [128, D])
    nak_p = dout("nak_p", [512, D]); nav_p = dout("nav_p", [512, D])
    nak_s = dout("nak_s", [128, D]); nav_s = dout("nav_s", [128, D])
    nckv_p = dout("nckv_p", [SEQ, 512]); nkr_p = dout("nkr_p", [SEQ, 64])
    nckv_s = dout("nckv_s", [128, 512]); nkr_s = dout("nkr_s", [128, 64])

    class WMat:
        def __init__(self, name, src, Kd, Nd, gw=512, kper=8):
            self.src, self.gw, self.kper = src, gw, min(kper, Kd // 128)
            self.ng, self.nkp = Nd // gw, (Kd // 128) // self.kper
            self.d = dscr(name, [self.ng * self.nkp, 128, self.kper, gw], BF16)
            self.b = [Buf() for _ in range(self.ng * self.nkp)]
            self.done = False

        def cast(self):
            if self.done:
                return
            self.done = True
            for g in range(self.ng):
                for kp in range(self.nkp):
                    i = g * self.nkp + kp
                    r0 = kp * self.kper * 128
                    src = self.src[r0:r0 + self.kper * 128, g * self.gw:(g + 1) * self.gw].rearrange("(k p) c -> p k c", p=128)
                    dst = self.d[i]
                    S.dma("pool", lambda e, dst=dst, src=src: e.dma_start(out=dst, in_=src), writes=[self.b[i]])

        def piece(self, g, kp):
            self.cast()
            i = g * self.nkp + kp
            return self.d[i], self.b[i]

    Wqkv = WMat("Wqkv", w_qkv, D, 6144)
    Wao = WMat("Wao", w_ao, D, D)
    W1m = [WMat("W1_%d" % l, w1[l], D, 4 * D) for l in range(2)]
    W2m = [WMat("W2_%d" % l, w2[l], 4 * D, D) for l in range(2)]
    Wdq = WMat("Wdq", w_dq, D, 512)
    Wdkv_c = WMat("Wdkv_c", w_dkv[:, 0:512], D, 512)
    Wdkv_r = WMat("Wdkv_r", w_dkv[:, 512:640], D, 128, gw=128)
    Wuq = [WMat("Wuq%d" % h, w_uq[:, h, :], 512, 256, gw=256, kper=4) for h in range(16)]
    Wuk = WMat("Wuk", w_uk, 512, D, gw=128, kper=4)
    Wuv = WMat("Wuv", w_uv, 512, D, gw=256, kper=4)
    Wbo = WMat("Wbo", w_bo, D, D)
    relB = dscr("relB", [16, 128, 640], BF16); B_relB = Buf()
    KTs = dscr("KTs", [128, 16, SEQ], BF16); Vs = dscr("Vs", [SEQ, D], BF16)
    B_kv = [Buf() for _ in range(8)]
    x1s = dscr("x1s", [128, 16, SEQ], F32); B_x1 = [Buf() for _ in range(8)]
    x1ss = dscr("x1ss", [128, 16, 128], F32); B_x1s = Buf()
    ckvTs = dscr("ckvTs", [128, 4, SEQ], BF16); krTs = dscr("krTs", [64, SEQ], BF16)
    B_lat = [Buf() for _ in range(8)]
    ckvTss = dscr("ckvTss", [128, 4, 128], BF16); krTss = dscr("krTss", [64, 128], BF16); B_lats = Buf()

    K1 = 1024
    NA = 80 * K1
    AR = st.enter_context(nc.sbuf_tensor("arena", [128, NA], BF16))

    def v16(off, n):
        return AR[:, off:off + n]

    def v32(off, n):
        return AR[:, off:off + 2 * n].bitcast(F32)

    NWS = 3
    WS = [st.enter_context(nc.sbuf_tensor("ws%d" % i, [128, 4096], BF16)) for i in range(NWS)]
    B_ws = [Buf() for _ in range(NWS)]
    wcount = [0]
    cst = st.enter_context(nc.sbuf_tensor("sb_cst", [128, 256], BF16)); B_c = Buf()
    idf = st.enter_context(nc.sbuf_tensor("sb_idf", [128, 128], F32))
    gn = st.enter_context(nc.sbuf_tensor("sb_gn", [128, 136], F32))
    sel = st.enter_context(nc.sbuf_tensor("sb_sel", [128, 2], F32))
    mk = st.enter_context(nc.sbuf_tensor("sb_mk", [16, 1280], BF16))
    ident_b = cst[:, 0:128]; ones_b = cst[:, 128:256]
    sm = st.enter_context(nc.sbuf_tensor("sb_sm", [128, 32], F32)); B_sm = [Buf() for _ in range(8)]; B_rs = [Buf() for _ in range(8)]
    rstd = st.enter_context(nc.sbuf_tensor("sb_rstd", [128, 512], F32)); B_rstd = Buf()
    NSQ = 4
    sqt = [st.enter_context(nc.sbuf_tensor("sqt%d" % i, [128, 512], BF16)) for i in range(NSQ)]
    B_sq = [Buf() for _ in range(NSQ)]
    NTF = 2
    tmpf = [st.enter_context(nc.sbuf_tensor("tmpf%d" % i, [128, 512], F32)) for i in range(NTF)]
    B_tf = [Buf() for _ in range(NTF)]
    PA = [st.enter_context(nc.psum_tensor("pa%d" % i, [128, 1024], F32)) for i in range(2)]
    B_PA = [Buf(), Buf()]
    PB = [st.enter_context(nc.psum_tensor("pb%d" % i, [128, 512], F32)) for i in range(4)]
    B_PB = [Buf() for _ in range(4)]
    rr = {"pb": 0, "ev": 0, "sq": 0, "tf": 0, "sm": 0, "u": 0, "g": 0}

    S.dma("pool", lambda e: e.dma_start(out=cst[:], in_=cb_d), writes=[B_c])
    S.dma("pool", lambda e: e.dma_start(out=idf[:], in_=ident_f_d), writes=[B_c])
    S.dma("pool", lambda e: e.dma_start(out=gn[:], in_=gains), writes=[B_c])
    S.dma("pool", lambda e: e.dma_start(out=sel[:], in_=sel_d), writes=[B_c])
    S.dma("pool", lambda e: e.dma_start(out=mk[0:10, 0:640], in_=kindb_d), writes=[B_c])
    S.dma("pool", lambda e: e.dma_start(out=mk[0:10, 640:1280], in_=qmb_d), writes=[B_c])
    for h in range(16):
        S.dma("pool", lambda e, h=h: e.dma_start(out=relB[h], in_=relT[h]), writes=[B_relB])

    def wload(wm, g, kp):
        pap, pbuf = wm.piece(g, kp)
        i = wcount[0] % NWS
        wcount[0] += 1
        kc, gw = pap.shape[1], pap.shape[2]
        dst = WS[i][:, 0:kc * gw].rearrange("p (k c) -> p k c", c=gw)
        S.dma("sp", lambda e: e.dma_start(out=dst, in_=pap), reads=[pbuf], writes=[B_ws[i]])
        return dst, B_ws[i]

    def nextpb():
        i = rr["pb"] % 4
        rr["pb"] += 1
        return PB[i], B_PB[i]

    def accs4(n):
        g = rr["g"] % 2
        rr["g"] += 1
        if g == 0:
            return [(PB[m], B_PB[m]) for m in range(4)]
        return [(PA[m // 2][:, (m % 2) * 512:(m % 2 + 1) * 512], B_PA[m // 2]) for m in range(4)]

    def ubufs(l):
        return list({id(b): b for b in l}.values())

    def mm(out_ap, pairs, reads, writes):
        def fn(e):
            n = len(pairs)
            ins = None
            for i, (l, r) in enumerate(pairs):
                ins = e.matmul(out_ap, lhsT=l, rhs=r, start=(i == 0), stop=(i == n - 1))
            return ins
        S.op("pe", fn, reads, writes)

    def evac(out_ap, in_ap, reads, writes, scale=None, eng=None):
        if eng is None:
            eng = ("act", "dve")[rr["ev"] % 2]
            rr["ev"] += 1
        if eng == "act":
            if scale is None:
                S.op("act", lambda e: e.activation(out=out_ap, in_=in_ap, func=AF.Copy), reads, writes)
            else:
                S.op("act", lambda e: e.activation(out=out_ap, in_=in_ap, func=AF.Copy, scale=scale), reads, writes)
        else:
            if scale is None:
                S.op("dve", lambda e: e.tensor_copy(out=out_ap, in_=in_ap), reads, writes)
            else:
                S.op("dve", lambda e: e.tensor_scalar(out=out_ap, in0=in_ap, scalar1=scale, scalar2=None,
                                                     op0=ALU.mult), reads, writes)

    def rms(src, nch, n, dn, gcol0, dst=None, resid=None, bsrc=(), bdst=()):
        pb, bpb = nextpb()
        for c in range(nch):
            i = rr["sq"] % NSQ
            rr["sq"] += 1
            sq = sqt[i][:, 0:n]
            a = src(c)
            if c % 3 == 2:
                S.op("pool", lambda e, sq=sq, a=a: e.tensor_tensor(out=sq, in0=a, in1=a, op=ALU.mult),
                     reads=list(bsrc), writes=[B_sq[i]])
            else:
                S.op("act", lambda e, sq=sq, a=a: e.activation(out=sq, in_=a, func=AF.Square),
                     reads=list(bsrc), writes=[B_sq[i]])
            def fn(e, sq=sq, c=c):
                return e.matmul(pb[:, 0:n], lhsT=ones_b, rhs=sq, start=(c == 0), stop=(c == nch - 1))
            S.op("pe", fn, reads=[B_sq[i], B_c], writes=[bpb])
        S.op("act", lambda e: e.activation(out=rstd[:, 0:n], in_=pb[:, 0:n], func=AF.Ln, scale=1.0 / dn, bias=EPS),
             reads=[bpb], writes=[B_rstd])
        S.op("act", lambda e: e.activation(out=rstd[:, 0:n], in_=rstd[:, 0:n], func=AF.Exp, scale=-0.5),
             reads=[B_rstd], writes=[B_rstd])
        for c in range(nch):
            a = src(c)
            g = gn[:, gcol0 + c:gcol0 + c + 1]
            if resid is None:
                o = dst(c)
                S.op("dve", lambda e, o=o, a=a, g=g: e.scalar_tensor_tensor(
                    out=o, in0=a, scalar=g, in1=rstd[:, 0:n], op0=ALU.mult, op1=ALU.mult),
                    reads=list(bsrc) + [B_rstd, B_c], writes=list(bdst))
            else:
                i = rr["tf"] % NTF
                rr["tf"] += 1
                t = tmpf[i][:, 0:n]
                x = resid(c)
                S.op("dve", lambda e, t=t, a=a, g=g: e.scalar_tensor_tensor(
                    out=t, in0=a, scalar=g, in1=rstd[:, 0:n], op0=ALU.mult, op1=ALU.mult),
                    reads=list(bsrc) + [B_rstd, B_c], writes=[B_tf[i]])
                S.op("pool", lambda e, x=x, t=t: e.tensor_tensor(out=x, in0=x, in1=t, op=ALU.add),
                     reads=[B_tf[i]], writes=list(bdst))

    def lin4(wm, rhs, n, sink, bsrc, groups=None, outw=None):
        outw = outw or (wm.gw // 128)
        for g in (groups if groups is not None else range(wm.ng)):
            accs = accs4(n)
            for kp in range(wm.nkp):
                wv, wb = wload(wm, g, kp)
                ops = [(accs[m][0][:, 0:n], wv[:, k, m * 128:(m + 1) * 128], rhs(kp * wm.kper + k),
                        (kp == 0 and k == 0), (kp == wm.nkp - 1 and k == wm.kper - 1))
                       for k in range(wm.kper) for m in range(outw)]
                def fn(e, ops=ops):
                    ins = None
                    for o, l, r, a0, a1 in ops:
                        ins = e.matmul(o, lhsT=l, rhs=r, start=a0, stop=a1)
                    return ins
                S.op("pe", fn, reads=list(bsrc) + [wb], writes=ubufs([a[1] for a in accs[:outw]]))
            for m in range(outw):
                sink(g * outw + m, accs[m][0][:, 0:n], accs[m][1])

    def lin_tok(wm, lhs, nsub, mrows, sink, bsrc, groups):
        for g in groups:
            accs = accs4(512)
            for kp in range(wm.nkp):
                wv, wb = wload(wm, g, kp)
                ops = [(accs[s][0][:mrows, 0:wm.gw], lhs(kp * wm.kper + k, s), wv[:, k, :],
                        (kp == 0 and k == 0), (kp == wm.nkp - 1 and k == wm.kper - 1))
                       for k in range(wm.kper) for s in range(nsub)]
                def fn(e, ops=ops):
                    ins = None
                    for o, l, r, a0, a1 in ops:
                        ins = e.matmul(o, lhsT=l, rhs=r, start=a0, stop=a1)
                    return ins
                S.op("pe", fn, reads=list(bsrc) + [wb], writes=ubufs([a[1] for a in accs[:nsub]]))
            for s in range(nsub):
                sink(g, s, accs[s][0][:mrows, 0:wm.gw], accs[s][1])

    def attn_unit(nq, nk, terms, tbufs, vt, vbufs, o_dst, o_bufs, bufset):
        Sb, B_S, Pbs, B_Ps, PT, B_PT = bufset
        u = rr["u"] % 2
        rr["u"] += 1
        Pb, B_P = Pbs[u], B_Ps[u]
        if Sb is None:
            pa, bpa = PA[u], B_PA[u]
            for k0 in range(0, nk, 512):
                kn = min(512, nk - k0)
                mm(pa[:nq, k0:k0 + kn], terms(k0, kn), reads=list(tbufs), writes=[bpa])
            ssrc, bs = pa[:nq, 0:nk], bpa
        else:
            for s0 in range(0, nk, 1024):
                sn = min(1024, nk - s0)
                j = (s0 // 1024) % 2
                for k0 in range(s0, s0 + sn, 512):
                    kn = min(512, nk - k0)
                    mm(PA[j][:nq, k0 - s0:k0 - s0 + kn], terms(k0, kn), reads=list(tbufs), writes=[B_PA[j]])
                evac(Sb[:nq, s0:s0 + sn], PA[j][:nq, 0:sn], reads=[B_PA[j]], writes=[B_S], eng="act")
            ssrc, bs = Sb[:nq, 0:nk], B_S
        i = rr["sm"] % 8
        rr["sm"] += 1
        mx = sm[:nq, 4 * i:4 * i + 1]; nm = sm[:nq, 4 * i + 1:4 * i + 2]
        rs = sm[:nq, 4 * i + 2:4 * i + 3]; ri = sm[:nq, 4 * i + 3:4 * i + 4]
        bsm, brs = B_sm[i], B_rs[i]
        S.op("pool", lambda e: e.memset(rs, 0.0), reads=[], writes=[brs])
        S.op("dve", lambda e: e.reduce_max(out=mx, in_=ssrc, axis=AX.X), reads=[bs], writes=[bsm])
        S.op("dve", lambda e: e.tensor_scalar(out=nm, in0=mx, scalar1=-1.0, scalar2=None, op0=ALU.mult),
             reads=[bsm], writes=[bsm])
        S.op("act", lambda e: e.activation(out=Pb[:nq, 0:nk], in_=ssrc, func=AF.Exp, bias=nm, scale=1.0,
                                            accum_out=rs), reads=[bs, bsm], writes=[B_P, brs])
        nkt = (nk + 127) // 128

        def stageB():
            for g0 in range(0, nkt, 4):
                pb, bpb = nextpb()
                pbb = pb[:].bitcast(BF16)
                gn_ = min(4, nkt - g0)
                tops = [(pbb[:min(128, nk - 128 * t), (t - g0) * 128:(t - g0) * 128 + nq],
                         Pb[:nq, 128 * t:128 * t + min(128, nk - 128 * t)]) for t in range(g0, g0 + gn_)]
                def fn(e, tops=tops):
                    ins = None
                    for o, a_ in tops:
                        ins = e.transpose(out=o, in_=a_, identity=ident_b[:nq, :nq])
                    return ins
                S.op("pe", fn, reads=[B_P, B_c], writes=[bpb])
                full = (nk - 128 * (g0 + gn_ - 1)) >= 128
                gi = (g0 // 4) % 8
                if full:
                    evac(PT[:, g0:g0 + gn_, 0:nq], pbb[:, 0:gn_ * 128].rearrange("p (t q) -> p t q", q=128)[:, :, 0:nq],
                         reads=[bpb], writes=[B_PT[gi]], eng="dve")
                else:
                    for t in range(g0, g0 + gn_):
                        kn = min(128, nk - 128 * t)
                        evac(PT[:kn, t, 0:nq], pbb[:kn, (t - g0) * 128:(t - g0) * 128 + nq],
                             reads=[bpb], writes=[B_PT[gi]], eng="dve")
            S.op("dve", lambda e: e.reciprocal(out=ri, in_=rs), reads=[brs], writes=[brs])
            pv, bpv = nextpb()
            pvops = [(PT[:min(128, nk - 128 * t), t, 0:nq], vt(t, min(128, nk - 128 * t))) for t in range(nkt)]
            def fnpv(e):
                ins = None
                for t, (l, r) in enumerate(pvops):
                    ins = e.matmul(pv[:nq, 0:128], lhsT=l, rhs=r, start=(t == 0), stop=(t == nkt - 1))
                return ins
            S.op("pe", fnpv, reads=ubufs([B_PT[(g // 4) % 8] for g in range(0, nkt, 4)] + list(vbufs)), writes=[bpv])
            S.op("dve", lambda e: e.tensor_scalar(out=o_dst, in0=pv[:nq, 0:128], scalar1=ri, scalar2=None, op0=ALU.mult),
                 reads=[bpv, brs], writes=list(o_bufs))

        return stageB

    def run_units(makers):
        pend = None
        for mk_ in makers:
            nxt = mk_()
            if pend is not None:
                pend()
            pend = nxt
        if pend is not None:
            pend()

    def tr4_f32(dst, src, nrows_in, reads, writes):
        raise NotImplementedError

    def load_rows_T(dst3, bdst, src_rows, stage, bstage, nchunks=16):
        S.dma("pool", lambda e: e.dma_start(out=stage[:, 0:nchunks * 128], in_=src_rows), writes=[bstage])
        for c0 in range(0, nchunks, 4):
            pb, bpb = nextpb()
            def fn(e, c0=c0, pb=pb):
                ins = None
                for c in range(c0, c0 + 4):
                    ins = e.transpose(out=pb[:, (c - c0) * 128:(c - c0 + 1) * 128],
                                      in_=stage[:, c * 128:(c + 1) * 128], identity=idf[:])
                return ins
            S.op("pe", fn, reads=[bstage, B_c], writes=[bpb])
            evac(dst3(c0), pb[:, 0:512].rearrange("p (c t) -> p c t", t=128), reads=[bpb], writes=[bdst])

    def store_tok(src, bsrc, nch, np_, ntok, dst_rows, stage, bstage):
        for s in range(ntok // 128):
            j = s % 2
            for c0 in range(0, nch, 4):
                cn = min(4, nch - c0)
                pb, bpb = nextpb()
                tops = [(pb[:, (c - c0) * np_:(c - c0 + 1) * np_], src(c)[:, s * 128:(s + 1) * 128]) for c in range(c0, c0 + cn)]
                def fn(e, tops=tops):
                    ins = None
                    for o, a in tops:
                        ins = e.transpose(out=o, in_=a, identity=idf[:np_, :np_])
                    return ins
                S.op("pe", fn, reads=list(bsrc) + [B_c], writes=[bpb])
                evac(stage[j][:, c0 * np_:(c0 + cn) * np_], pb[:, 0:cn * np_], reads=[bpb], writes=[bstage[j]])
            S.dma("pool", lambda e, s=s, j=j: e.dma_start(out=dst_rows(s), in_=stage[j][:, 0:nch * np_]),
                  reads=[bstage[j]])

    G_MPRE, G_MPOST, G_FPRE, G_FPOST, G_QN, G_KVN = 0, 32, 64, 96, 128, 132

    def ffn(l, xT, bx, hT, bh, yT, by, hid, ntok):
        rms(lambda c: xT[:, c, 0:ntok], 16, ntok, D, G_FPRE + 16 * l, dst=lambda c: hT[:, c, 0:ntok],
            bsrc=[bx], bdst=[bh])
        bhid = [Buf() for _ in range(64)]
        def sink1(m, ps, bps):
            i = rr["tf"] % NTF
            rr["tf"] += 1
            t = tmpf[i][:, 0:ntok]
            S.op("act", lambda e: e.activation(out=t, in_=ps, func=AF.Relu), reads=[bps], writes=[B_tf[i]])
            eng = ("pool", "dve")[m % 2]
            S.op(eng, lambda e: e.tensor_tensor(out=hid[:, m, 0:ntok], in0=t, in1=t, op=ALU.mult),
                 reads=[B_tf[i]], writes=[bhid[m]])
        lin4(W1m[l], lambda k: hT[:, k, 0:ntok], ntok, sink1, [bh])
        def sink2(m, ps, bps):
            evac(yT[:, m, 0:ntok], ps, reads=[bps], writes=[by])
        lin4(W2m[l], lambda k: hid[:, k, 0:ntok], ntok, sink2, bhid)
        rms(lambda c: yT[:, c, 0:ntok], 16, ntok, D, G_FPOST + 16 * l, resid=lambda c: xT[:, c, 0:ntok],
            bsrc=[by], bdst=[bx])

    xT = v32(0, 8192).rearrange("p (c t) -> p c t", t=512)
    hT = v16(16 * K1, 8192).rearrange("p (c t) -> p c t", t=512)
    yT = v32(24 * K1, 8192).rearrange("p (c t) -> p c t", t=512)
    hid = v16(40 * K1, 32768).rearrange("p (f t) -> p f t", t=512)
    QT = v16(24 * K1, 8192).rearrange("p (h t) -> p h t", t=512)
    KT = v16(32 * K1, 16384).rearrange("p (h t) -> p h t", t=1024)
    Vt = v16(48 * K1, 16384).rearrange("p (s c) -> p s c", c=2048)
    otok = v16(64 * K1, 4096).rearrange("p (j c) -> p j c", c=2048)
    stg = [v32(68 * K1 + 4096 * j, 2048) for j in range(2)]
    btile = [v16(76 * K1 + 640 * j, 640) for j in range(2)]
    btile6 = [v16(72 * K1 + 640 * j, 640) for j in range(6)]
    B_bt6 = [Buf() for _ in range(6)]
    Pb0 = [v16(77 * K1 + 256 + 640 * j, 640) for j in range(2)]
    PT0 = v16(78 * K1 + 512, 640).rearrange("p (t q) -> p t q", q=128)
    B_P0, B_PT0 = [Buf(), Buf()], [Buf() for _ in range(8)]
    set0 = (None, None, Pb0, B_P0, PT0, B_PT0)
    oT = hT
    B_bt = [Buf(), Buf()]

    def l0_tile(t):
        smp = (t == 8)
        ntok = 128 if smp else 512
        nsub = ntok // 128
        S.barrier()
        bx, bh, bq, bk, bv, bo, by = Buf(), Buf(), Buf(), Buf(), Buf(), Buf(), Buf()
        bstg = [Buf(), Buf()]
        botok = [Buf(), Buf()]
        for s in range(nsub):
            rows = xs[:, :] if smp else xp[t * 512 + s * 128:t * 512 + (s + 1) * 128, :]
            load_rows_T(lambda c0, s=s: xT[:, c0:c0 + 4, s * 128:(s + 1) * 128], bx, rows, stg[s % 2], bstg[s % 2])
        rms(lambda c: xT[:, c, 0:ntok], 16, ntok, D, G_MPRE, dst=lambda c: hT[:, c, 0:ntok], bsrc=[bx], bdst=[bh])
        def sinkq(m, ps, bps):
            evac(QT[:, m, 0:ntok], ps, reads=[bps], writes=[bq], scale=SC_A)
        lin4(Wqkv, lambda k: hT[:, k, 0:ntok], ntok, sinkq, [bh], groups=range(0, 4))
        kcur = 640 if smp else 512
        def sinkk(m, ps, bps):
            evac(KT[:, m - 16, kcur:kcur + ntok], ps, reads=[bps], writes=[bk])
        lin4(Wqkv, lambda k: hT[:, k, 0:ntok], ntok, sinkk, [bh], groups=range(4, 8))
        if int(os.environ.get('STOP_AT', '99')) <= 1:
            return
        want_out = smp or t == 7
        if want_out:
            def sinkko(g, s, ps, bps):
                i = rr["tf"] % NTF
                rr["tf"] += 1
                evac(tmpf[i][:, :], ps, reads=[bps], writes=[B_tf[i]])
                dst = (nak_s if smp else nak_p)[s * 128:(s + 1) * 128, (g - 4) * 512:(g - 3) * 512]
                S.dma("pool", lambda e: e.dma_start(out=dst, in_=tmpf[i][:, :]), reads=[B_tf[i]])
            for s_ in range(nsub):
                def sk(g, s, ps, bps, s_=s_):
                    sinkko(g, s_, ps, bps)
                lin_tok(Wqkv, lambda k, s, s_=s_: hT[:, k, s_ * 128:(s_ + 1) * 128], 1, 128, sk, [bh], groups=range(4, 8))
        if int(os.environ.get('STOP_AT', '99')) <= 2:
            return
        if not smp:
            def sinkv(g, s, ps, bps):
                evac(Vt[:, 4 + s, (g - 8) * 512:(g - 7) * 512], ps, reads=[bps], writes=[bv])
            lin_tok(Wqkv, lambda k, s: hT[:, k, s * 128:(s + 1) * 128], nsub, 128, sinkv, [bh], groups=range(8, 12))
            if want_out:
                for s_ in range(nsub):
                    def sv(g, s, ps, bps, s_=s_):
                        i = rr["tf"] % NTF
                        rr["tf"] += 1
                        evac(tmpf[i][:, :], ps, reads=[bps], writes=[B_tf[i]])
                        dst = nav_p[s_ * 128:(s_ + 1) * 128, (g - 8) * 512:(g - 7) * 512]
                        S.dma("pool", lambda e: e.dma_start(out=dst, in_=tmpf[i][:, :]), reads=[B_tf[i]])
                    lin_tok(Wqkv, lambda k, s, s_=s_: hT[:, k, s_ * 128:(s_ + 1) * 128], 1, 128, sv, [bh], groups=range(8, 12))
        else:
            def sinkvs(g, a, ps, bps):
                evac(Vt[:64, 5 + a, (g - 8) * 512:(g - 7) * 512], ps, reads=[bps], writes=[bv])
            def sinkvo(g, s, ps, bps):
                i = rr["tf"] % NTF
                rr["tf"] += 1
                evac(tmpf[i][:, :], ps, reads=[bps], writes=[B_tf[i]])
                dst = nav_s[:, (g - 8) * 512:(g - 7) * 512]
                S.dma("pool", lambda e: e.dma_start(out=dst, in_=tmpf[i][:, :]), reads=[B_tf[i]])
            lin_tok(Wqkv, lambda k, s: hT[:, k, 0:128], 1, 128, sinkvo, [bh], groups=range(8, 12))
            lin_tok(Wqkv, lambda k, a: hT[:, k, a * 64:(a + 1) * 64], 2, 64, sinkvs, [bh], groups=range(8, 12))
        if int(os.environ.get('STOP_AT', '99')) <= 3:
            return
        if not smp:
            if t < 7:
                S.dma("pool", lambda e, t=t: e.dma_start(out=KTs[:, :, t * 512:(t + 1) * 512], in_=KT[:, :, 512:1024]),
                      reads=[bk], writes=[B_kv[t]])
                S.dma("pool", lambda e, t=t: e.dma_start(
                    out=Vs[t * 512:(t + 1) * 512, :].rearrange("(s p) c -> p s c", p=128), in_=Vt[:, 4:8, :]),
                    reads=[bv], writes=[B_kv[t]])
            if t == 0:
                S.op("pool", lambda e: e.memset(KT[:, :, 0:512], 0.0), writes=[bk])
                S.op("pool", lambda e: e.memset(Vt[:, 0:4, :], 0.0), writes=[bv])
            else:
                S.dma("pool", lambda e, t=t: e.dma_start(out=KT[:, :, 0:512], in_=KTs[:, :, (t - 1) * 512:t * 512]),
                      reads=[B_kv[t - 1]], writes=[bk])
                S.dma("pool", lambda e, t=t: e.dma_start(
                    out=Vt[:, 0:4, :], in_=Vs[(t - 1) * 512:t * 512, :].rearrange("(s p) c -> p s c", p=128)),
                    reads=[B_kv[t - 1]], writes=[bv])
        kindb = mk[0:10, 0:640]

        def band_units(s, a=None, t=t):
            nq = 128 if a is None else 64
            nk = 640 if a is None else 576
            q0 = s * 128 if a is None else a * 64
            w0 = s * 128 if a is None else 0
            var = min(4 * t + s, 4) if a is None else None
            j = (s if a is None else a) % 2
            def mk_unit(h):
                if a is None:
                    bt_, bbt = btile6[h % 6], B_bt6[h % 6]
                else:
                    bt_, bbt = btile[h % 2], B_bt[h % 2]
                S.dma("sp", lambda e, h=h, bt_=bt_: e.dma_start(out=bt_, in_=relB[h]), reads=[B_relB], writes=[bbt])

                def terms(k0, kn, h=h, bt_=bt_):
                    l = [(QT[:, h, q0:q0 + nq], KT[:, h, w0 + k0:w0 + k0 + kn]),
                         (ident_b[:nq, :nq], bt_[:nq, k0:k0 + kn])]
                    if var is not None:
                        l.append((mk[0:10, 640 + var * 128:640 + (var + 1) * 128], kindb[:, k0:k0 + kn]))
                    return l

                def vt(kt, kn, h=h):
                    if a is not None and kt == 4:
                        return Vt[:kn, 5 + a, h * 128:(h + 1) * 128]
                    return Vt[:kn, (s if a is None else 0) + kt, h * 128:(h + 1) * 128]
                return attn_unit(nq, nk, terms, [bq, bk, bbt, B_c], vt, [bv],
                                 otok[:nq, j, h * 128:(h + 1) * 128], [botok[j]], set0)
            run_units([(lambda h=h: mk_unit(h)) for h in range(16)])
            for c0 in range(0, 16, 4):
                pb, bpb = nextpb()
                pbb = pb[:].bitcast(BF16)
                def fn(e, c0=c0, pbb=pbb):
                    ins = None
                    for c in range(c0, c0 + 4):
                        ins = e.transpose(out=pbb[:, (c - c0) * 128:(c - c0) * 128 + nq],
                                          in_=otok[:nq, j, c * 128:(c + 1) * 128], identity=ident_b[:nq, :nq])
                    return ins
                S.op("pe", fn, reads=[botok[j], B_c], writes=[bpb])
                evac(oT[:, c0:c0 + 4, q0:q0 + nq],
                     pbb[:, 0:512].rearrange("p (c q) -> p c q", q=128)[:, :, 0:nq], reads=[bpb], writes=[bo, bh])
        if not smp:
            for s in range(nsub):
                band_units(s)
        else:
            for a in ([] if os.environ.get('SKIP_A') else range(2)):
                for s4 in range(4):
                    load_rows_T(lambda c0, s4=s4: KT[:, c0:c0 + 4, s4 * 128:(s4 + 1) * 128], bk,
                                cak[a, s4 * 128:(s4 + 1) * 128, :], stg[s4 % 2], bstg[s4 % 2])
                for s4 in range(4):
                    j = s4 % 2
                    S.dma("pool", lambda e, a=a, s4=s4, j=j: e.dma_start(out=stg[j], in_=cav[a, s4 * 128:(s4 + 1) * 128, :]),
                          writes=[bstg[j]])
                    S.op("dve", lambda e, s4=s4, j=j: e.tensor_copy(out=Vt[:, s4, :], in_=stg[j]),
                         reads=[bstg[j]], writes=[bv])
                S.op("pool", lambda e, a=a: e.tensor_copy(out=KT[:, :, 512:576], in_=KT[:, :, 640 + a * 64:704 + a * 64]),
                     reads=[bk], writes=[bk])
                band_units(0, a)
        if int(os.environ.get('STOP_AT', '99')) <= 4:
            return
        S.barrier()
        by = Buf()
        def sinko(m, ps, bps):
            evac(yT[:, m, 0:ntok], ps, reads=[bps], writes=[by])
        lin4(Wao, lambda k: oT[:, k, 0:ntok], ntok, sinko, [bo, bh])
        rms(lambda c: yT[:, c, 0:ntok], 16, ntok, D, G_MPOST, resid=lambda c: xT[:, c, 0:ntok], bsrc=[by], bdst=[bx])
        if int(os.environ.get('STOP_AT', '99')) <= 5:
            return
        S.barrier()
        by = Buf(); bh = Buf()
        ffn(0, xT, bx, hT, bh, yT, by, hid, ntok)
        if smp:
            S.dma("pool", lambda e: e.dma_start(out=x1ss[:, :, :], in_=xT[:, :, 0:128]), reads=[bx], writes=[B_x1s])
        else:
            S.dma("pool", lambda e, t=t: e.dma_start(out=x1s[:, :, t * 512:(t + 1) * 512], in_=xT[:, :, :]),
                  reads=[bx], writes=[B_x1[t]])
        if int(os.environ.get('STOP_AT', '99')) <= 6:
            return
        S.barrier()
        by = Buf(); bh = Buf()
        rms(lambda c: xT[:, c, 0:ntok], 16, ntok, D, G_MPRE + 16, dst=lambda c: hT[:, c, 0:ntok], bsrc=[bx], bdst=[bh])
        def sinkc(m, ps, bps):
            evac(yT[:, m, 0:ntok], ps, reads=[bps], writes=[by])
        lin4(Wdkv_c, lambda k: hT[:, k, 0:ntok], ntok, sinkc, [bh])
        cs = v32(68 * K1, 1024).rearrange("p (a t) -> p a t", t=512)
        bcs, bkr = Buf(), Buf()
        cd, sd, p0 = (coss_d, sins_d, 0) if smp else (cosk_d, sink_d, t * 512)
        S.dma("pool", lambda e: e.dma_start(out=cs[:64, 0, 0:ntok], in_=cd[:, p0:p0 + ntok]), writes=[bcs])
        S.dma("pool", lambda e: e.dma_start(out=cs[:64, 1, 0:ntok], in_=sd[:, p0:p0 + ntok]), writes=[bcs])
        krf = yT[:64, 5, 0:ntok]
        def sinkr(m, ps, bps):
            pass
        accs = accs4(ntok)
        for kp in range(Wdkv_r.nkp):
            wv, wb = wload(Wdkv_r, 0, kp)
            ops = [(accs[m][0][:64, 0:ntok], wv[:, k, m * 64:(m + 1) * 64], hT[:, kp * Wdkv_r.kper + k, 0:ntok],
                    (kp == 0 and k == 0), (kp == Wdkv_r.nkp - 1 and k == Wdkv_r.kper - 1))
                   for k in range(Wdkv_r.kper) for m in range(2)]
            def fn(e, ops=ops):
                ins = None
                for o, l, r, a0, a1 in ops:
                    ins = e.matmul(o, lhsT=l, rhs=r, start=a0, stop=a1)
                return ins
            S.op("pe", fn, reads=[bh, wb], writes=ubufs([accs[0][1], accs[1][1]]))
        S.op("dve", lambda e: e.tensor_tensor(out=yT[:64, 4, 0:ntok], in0=accs[0][0][:64, 0:ntok], in1=cs[:64, 0, 0:ntok], op=ALU.mult),
             reads=[accs[0][1], bcs], writes=[bkr])
        S.op("dve", lambda e: e.tensor_tensor(out=krf, in0=accs[1][0][:64, 0:ntok], in1=cs[:64, 1, 0:ntok], op=ALU.mult),
             reads=[accs[1][1], bcs], writes=[bkr])
        S.op("pool", lambda e: e.tensor_tensor(out=krf, in0=krf, in1=yT[:64, 4, 0:ntok], op=ALU.add), reads=[bkr], writes=[bkr])
        bcn = Buf()
        rms(lambda c: yT[:, c, 0:ntok], 4, ntok, 512, G_KVN, dst=lambda c: yT[:, 8 + c, 0:ntok], bsrc=[by], bdst=[bcn])
        lat16 = hid[:, 40:45, :]
        bl16 = Buf()
        S.op("pool", lambda e: e.tensor_copy(out=lat16[:, 0:4, 0:ntok], in_=yT[:, 8:12, 0:ntok]), reads=[bcn], writes=[bl16])
        S.op("pool", lambda e: e.tensor_copy(out=lat16[:64, 4, 0:ntok], in_=krf), reads=[bkr], writes=[bl16])
        if smp:
            S.dma("pool", lambda e: e.dma_start(out=ckvTss[:, :, :], in_=lat16[:, 0:4, 0:128]), reads=[bl16], writes=[B_lats])
            S.dma("pool", lambda e: e.dma_start(out=krTss[:, :], in_=lat16[:64, 4, 0:128]), reads=[bl16], writes=[B_lats])
        else:
            S.dma("pool", lambda e, t=t: e.dma_start(out=ckvTs[:, :, t * 512:(t + 1) * 512], in_=lat16[:, 0:4, :]),
                  reads=[bl16], writes=[B_lat[t]])
            S.dma("pool", lambda e, t=t: e.dma_start(out=krTs[:, t * 512:(t + 1) * 512], in_=lat16[:64, 4, :]),
                  reads=[bl16], writes=[B_lat[t]])
        stg2 = [v32(72 * K1 + 2048 * j, 1024) for j in range(2)]
        bst2 = [Buf(), Buf()]
        if smp:
            store_tok(lambda c: yT[:, 8 + c, :], [bcn], 4, 128, ntok, lambda s: nckv_s[:, :], stg2, bst2)
            store_tok(lambda c: yT[:64, 5, :], [bkr], 1, 64, ntok, lambda s: nkr_s[:, :], stg2, bst2)
        else:
            store_tok(lambda c: yT[:, 8 + c, :], [bcn], 4, 128, ntok,
                      lambda s, t=t: nckv_p[t * 512 + s * 128:t * 512 + (s + 1) * 128, :], stg2, bst2)
            store_tok(lambda c: yT[:64, 5, :], [bkr], 1, 64, ntok,
                      lambda s, t=t: nkr_p[t * 512 + s * 128:t * 512 + (s + 1) * 128, :], stg2, bst2)
        if t == 0:
            for wm in [Wdq] + Wuq + [Wuk, Wuv, Wbo, W1m[1], W2m[1]]:
                wm.cast()

    for t_ in (range(l0_tiles) if isinstance(l0_tiles, int) else l0_tiles):
        l0_tile(t_)

    cq = v16(0, 2048).rearrange("p (c t) -> p c t", t=512)
    ckvT = v16(2 * K1, 16384).rearrange("p (c t) -> p c t", t=4096)
    krT = v16(18 * K1, 4096)
    kind = v16(22 * K1, 4096)
    qm = v16(26 * K1, 2048)
    knT = v16(28 * K1, 4096)
    v2 = v16(32 * K1, 8192).rearrange("p (t c) -> p t c", c=256)
    qn = v16(40 * K1, 1024).rearrange("p (j t) -> p j t", t=512)
    qr = v16(41 * K1, 1024).rearrange("p (j t) -> p j t", t=512)
    otk = v16(42 * K1, 8192).rearrange("p (s c) -> p s c", c=2048)
    rtab = v32(50 * K1, 1024).rearrange("p (a t) -> p a t", t=512)
    rtmp = v32(52 * K1, 1024).rearrange("p (a t) -> p a t", t=512)
    stq = [v32(54 * K1 + 1024 * j, 512) for j in range(2)]
    Sb1 = v32(56 * K1, 4096)
    Pb1 = v16(64 * K1, 4096)
    PT1 = v16(68 * K1, 4096).rearrange("p (t q) -> p t q", q=128)
    ovT = v16(72 * K1, 8192).rearrange("p (h t) -> p h t", t=512)
    Pb1b = st.enter_context(nc.sbuf_tensor("pb1b", [128, 4096], BF16))
    set1 = (Sb1, Buf(), [Pb1, Pb1b[:, :]], [Buf(), Buf()], PT1, [Buf() for _ in range(8)])
    B_keys, B_mask, B_kn, B_v2, B_qh, B_otk, B_rtab, B_rtmp, B_ov, B_cq = (
        Buf(), Buf(), Buf(), Buf(), [Buf(), Buf()], Buf(), Buf(), Buf(), Buf(), Buf())
    B_stq = [Buf(), Buf()]

    def l1_tile(i):
        smp = (i == 4)
        ntok = 128 if smp else 512
        nsub = ntok // 128
        S.barrier()
        bx, bh, by = Buf(), Buf(), Buf()

        def load_x1(bx, by, i=i, smp=smp):
            if smp:
                S.dma("pool", lambda e: e.dma_start(out=xT[:, :, 0:128], in_=x1ss[:, :, :]), reads=[B_x1s], writes=[bx])
            else:
                S.dma("pool", lambda e: e.dma_start(out=xT[:, :, :], in_=x1s[:, :, i * 512:(i + 1) * 512]),
                      reads=[B_x1[i]], writes=[bx])
                S.dma("pool", lambda e: e.dma_start(out=yT[:, :, :], in_=x1s[:, :, 2048 + i * 512:2048 + (i + 1) * 512]),
                      reads=[B_x1[4 + i]], writes=[by])
                S.op("dve", lambda e: e.tensor_scalar(out=yT[:, :, :], in0=yT[:, :, :], scalar1=sel[:, 1:2], scalar2=None,
                                                     op0=ALU.mult), reads=[by, B_c], writes=[by])
                S.op("dve", lambda e: e.scalar_tensor_tensor(out=xT[:, :, :], in0=xT[:, :, :], scalar=sel[:, 0:1],
                                                            in1=yT[:, :, :], op0=ALU.mult, op1=ALU.add),
                     reads=[by, B_c], writes=[bx])
        load_x1(bx, by)
        rms(lambda c: xT[:, c, 0:ntok], 16, ntok, D, G_MPRE + 16, dst=lambda c: hT[:, c, 0:ntok], bsrc=[bx], bdst=[bh])
        by = Buf()
        def sinkcq(m, ps, bps):
            evac(yT[:, m, 0:ntok], ps, reads=[bps], writes=[by])
        lin4(Wdq, lambda k: hT[:, k, 0:ntok], ntok, sinkcq, [bh])
        rms(lambda c: yT[:, c, 0:ntok], 4, ntok, 512, G_QN, dst=lambda c: cq[:, c, 0:ntok], bsrc=[by], bdst=[B_cq, bx])
        S.barrier()
        def seq_body(a):
            if a is None:
                nk = 2048 + 512 * (i + 1)
                S.dma("pool", lambda e, nk=nk: e.dma_start(out=ckvT[:, :, 0:nk], in_=ckvTs[:, :, 0:nk]),
                      reads=B_lat, writes=[B_keys])
                S.dma("pool", lambda e, nk=nk: e.dma_start(out=krT[:64, 0:nk], in_=krTs[:, 0:nk]), reads=B_lat, writes=[B_keys])
                S.dma("pool", lambda e: e.dma_start(out=kind[:64, :], in_=kind_d), writes=[B_mask])
                S.dma("pool", lambda e: e.dma_start(out=qm[:64, :], in_=qm_d), writes=[B_mask])
            else:
                nk = 2112
                for s16 in range(16):
                    j = s16 % 2
                    load_rows_T(lambda c0, s16=s16: ckvT[:, c0:c0 + 4, s16 * 128:(s16 + 1) * 128], B_keys,
                                cckv[a, s16 * 128:(s16 + 1) * 128, :], stq[j], B_stq[j], nchunks=4)
                for s4 in range(4):
                    j = s4 % 2
                    S.dma("pool", lambda e, a=a, s4=s4, j=j: e.dma_start(
                        out=stq[j][:, 0:256].rearrange("p (s r) -> p s r", r=64),
                        in_=ckr[a, s4 * 512:(s4 + 1) * 512, :].rearrange("(s p) r -> p s r", p=128)), writes=[B_stq[j]])
                    pb, bpb = nextpb()
                    def fn(e, pb=pb, j=j):
                        ins = None
                        for c in range(4):
                            ins = e.transpose(out=pb[:64, c * 128:(c + 1) * 128], in_=stq[j][:, c * 64:(c + 1) * 64], identity=idf[:])
                        return ins
                    S.op("pe", fn, reads=[B_stq[j], B_c], writes=[bpb])
                    evac(krT[:64, s4 * 512:(s4 + 1) * 512], pb[:64, 0:512], reads=[bpb], writes=[B_keys])
                S.dma("pool", lambda e, a=a: e.dma_start(out=ckvT[:, :, 2048:2112], in_=ckvTss[:, :, a * 64:(a + 1) * 64]),
                      reads=[B_lats], writes=[B_keys])
                S.dma("pool", lambda e, a=a: e.dma_start(out=krT[:64, 2048:2112], in_=krTss[:, a * 64:(a + 1) * 64]),
                      reads=[B_lats], writes=[B_keys])
            nq_tok = ntok if a is None else 64
            q0 = 0 if a is None else a * 64
            if smp:
                S.dma("pool", lambda e, q0=q0: e.dma_start(out=rtab[:64, 0, 0:64], in_=coss_d[:, q0:q0 + 64]), writes=[B_rtab])
                S.dma("pool", lambda e, q0=q0: e.dma_start(out=rtab[:64, 1, 0:64], in_=sins_d[:, q0:q0 + 64]), writes=[B_rtab])
            else:
                S.dma("pool", lambda e: e.dma_start(out=rtab[:64, 0, :], in_=cosq_d[:, i * 512:(i + 1) * 512]), writes=[B_rtab])
                S.dma("pool", lambda e: e.dma_start(out=rtab[:64, 1, :], in_=sinq_d[:, i * 512:(i + 1) * 512]), writes=[B_rtab])
            nkt = (nk + 127) // 128
            def head_body(h):
                j = h % 2
                if h % 2 == 0:
                    wvv, bwv = wload(Wuv, h // 2, 0)
                    for g0 in range(0, nkt, 2):
                        pb, bpb = nextpb()
                        gcnt = min(2, nkt - g0)
                        ops = [(pb[:min(128, nk - 128 * kt), (kt - g0) * 256:(kt - g0 + 1) * 256],
                                ckvT[:, c, kt * 128:kt * 128 + min(128, nk - 128 * kt)], wvv[:, c, :], c == 0, c == 3)
                               for kt in range(g0, g0 + gcnt) for c in range(4)]
                        def fn(e, ops=ops):
                            ins = None
                            for o, l, r, a0, a1 in ops:
                                ins = e.matmul(o, lhsT=l, rhs=r, start=a0, stop=a1)
                            return ins
                        S.op("pe", fn, reads=[B_keys, bwv], writes=[bpb])
                        for kt in range(g0, g0 + gcnt):
                            kn = min(128, nk - 128 * kt)
                            evac(v2[:kn, kt, :], pb[:kn, (kt - g0) * 256:(kt - g0 + 1) * 256], reads=[bpb], writes=[B_v2])
                wq, bw = wload(Wuq[h], 0, 0)
                pb, bpb = nextpb()
                mm(pb[:, 0:nq_tok], [(wq[:, c, 0:128], cq[:, c, q0:q0 + nq_tok]) for c in range(4)], reads=[bw, B_cq], writes=[bpb])
                evac(qn[:, j, 0:nq_tok], pb[:, 0:nq_tok], reads=[bpb], writes=[B_qh[j]], scale=SC_B)
                pr, bpr = nextpb()
                mm(pr[:64, 0:nq_tok], [(wq[:, c, 128:192], cq[:, c, q0:q0 + nq_tok]) for c in range(4)], reads=[bw, B_cq], writes=[bpr])
                pw, bpw = nextpb()
                mm(pw[:64, 0:nq_tok], [(wq[:, c, 192:256], cq[:, c, q0:q0 + nq_tok]) for c in range(4)], reads=[bw, B_cq], writes=[bpw])
                S.op("dve", lambda e, pr=pr: e.tensor_tensor(out=rtmp[:64, 0, 0:nq_tok], in0=pr[:64, 0:nq_tok],
                                                             in1=rtab[:64, 0, 0:nq_tok], op=ALU.mult), reads=[bpr, B_rtab], writes=[B_rtmp])
                S.op("dve", lambda e, pw=pw: e.tensor_tensor(out=rtmp[:64, 1, 0:nq_tok], in0=pw[:64, 0:nq_tok],
                                                             in1=rtab[:64, 1, 0:nq_tok], op=ALU.mult), reads=[bpw, B_rtab], writes=[B_rtmp])
                S.op("pool", lambda e: e.tensor_tensor(out=rtmp[:64, 0, 0:nq_tok], in0=rtmp[:64, 0, 0:nq_tok],
                                                       in1=rtmp[:64, 1, 0:nq_tok], op=ALU.add), reads=[B_rtmp], writes=[B_rtmp])
                S.op("pool", lambda e, j=j: e.tensor_scalar(out=qr[:64, j, 0:nq_tok], in0=rtmp[:64, 0, 0:nq_tok], scalar1=SC_B,
                                                            scalar2=None, op0=ALU.mult), reads=[B_rtmp], writes=[B_qh[j]])
                wk, bwk = wload(Wuk, h, 0)
                for k0 in range(0, nk, 512):
                    kn = min(512, nk - k0)
                    pb, bpb = nextpb()
                    mm(pb[:, 0:kn], [(wk[:, c, :], ckvT[:, c, k0:k0 + kn]) for c in range(4)], reads=[bwk, B_keys], writes=[bpb])
                    evac(knT[:, k0:k0 + kn], pb[:, 0:kn], reads=[bpb], writes=[B_kn])
                def unit_body(s):
                    if a is None:
                        jq = 4 * i + s
                        nk_u = 128 * (17 + jq)
                        nq, qc0, srow = 128, s * 128, s
                    else:
                        jq, nk_u, nq, qc0, srow = None, 2112, 64, 0, a

                    def terms(k0, kn, j=j, qc0=qc0, nq=nq, jq=jq):
                        l = [(qn[:, j, qc0:qc0 + nq], knT[:, k0:k0 + kn]),
                             (qr[:64, j, qc0:qc0 + nq], krT[:64, k0:k0 + kn])]
                        if jq is not None:
                            l.append((qm[:64, jq * 128:(jq + 1) * 128], kind[:64, k0:k0 + kn]))
                        return l

                    def vt(kt, kn, h=h):
                        return v2[:kn, kt, (h % 2) * 128:(h % 2 + 1) * 128]
                    return attn_unit(nq, nk_u, terms, [B_qh[j], B_kn, B_keys, B_mask], vt, [B_v2],
                                     otk[:nq, srow, h * 128:(h + 1) * 128], [B_otk], set1)
                run_units([(lambda s_=s_: unit_body(s_)) for s_ in range(nsub if a is None else 1)])
            for h_ in range(16):
                head_body(h_)
            for s in range(nsub if a is None else 1):
                srow = s if a is None else a
                nq = 128 if a is None else 64
                c0q = s * 128 if a is None else a * 64
                for c0 in range(0, 16, 4):
                    pb, bpb = nextpb()
                    pbb = pb[:].bitcast(BF16)
                    def fn(e, c0=c0, pbb=pbb, srow=srow, nq=nq):
                        ins = None
                        for c in range(c0, c0 + 4):
                            ins = e.transpose(out=pbb[:, (c - c0) * 128:(c - c0) * 128 + nq],
                                              in_=otk[:nq, srow, c * 128:(c + 1) * 128], identity=ident_b[:nq, :nq])
                        return ins
                    S.op("pe", fn, reads=[B_otk, B_c], writes=[bpb])
                    evac(ovT[:, c0:c0 + 4, c0q:c0q + nq],
                         pbb[:, 0:512].rearrange("p (c q) -> p c q", q=128)[:, :, 0:nq], reads=[bpb], writes=[B_ov])
        for a_ in ([None] if not smp else [0, 1]):
            seq_body(a_)
        S.barrier()
        bx, bh, by = Buf(), Buf(), Buf()
        load_x1(bx, by)
        by = Buf()
        def sinkbo(m, ps, bps):
            evac(yT[:, m, 0:ntok], ps, reads=[bps], writes=[by])
        lin4(Wbo, lambda k: ovT[:, k, 0:ntok], ntok, sinkbo, [B_ov, bx])
        rms(lambda c: yT[:, c, 0:ntok], 16, ntok, D, G_MPOST + 16, resid=lambda c: xT[:, c, 0:ntok], bsrc=[by], bdst=[bx])
        S.barrier()
        by = Buf()
        ffn(1, xT, bx, hT, bh, yT, by, hid, ntok)
        S.barrier()
        bstg = [Buf(), Buf()]
        stgy = [v32(24 * K1 + 4096 * j, 2048) for j in range(2)]
        if smp:
            store_tok(lambda c: xT[:, c, :], [bx], 16, 128, ntok, lambda s: y_s[:, :], stgy, bstg)
        else:
            store_tok(lambda c: xT[:, c, :], [bx], 16, 128, ntok,
                      lambda s, i=i: y_p[i * 512 + s * 128:i * 512 + (s + 1) * 128, :], stgy, bstg)

    for i_ in (range(l1_tiles) if isinstance(l1_tiles, int) else l1_tiles):
        l1_tile(i_)
    S.emit(st)
    st.close()
    return nc


def _pos_tables(pos):
    inv = 1.0 / (10000.0 ** (np.arange(32, dtype=np.float32) / 32.0))
    ang = pos.astype(np.float32)[None, :] * inv[:, None].astype(np.float32)
    c = np.cos(ang).astype(np.float32); s = np.sin(ang).astype(np.float32)
    return np.concatenate([c, c], 0), np.concatenate([-s, s], 0)


def _consts(half):
    bf = ml_dtypes.bfloat16
    ident = np.eye(128, dtype=np.float32)
    cb = np.concatenate([ident, np.ones((128, 128), np.float32)], 1).astype(bf)
    kindb = np.zeros((10, 640), np.float32)
    for c in range(10):
        kindb[c, c * 64:(c + 1) * 64] = 1.0
    qmb = np.zeros((10, 5 * 128), np.float32)
    for var in range(5):
        for r in range(128):
            qc = r // 64
            for c in range(10):
                masked = (c > 8 + qc) or (c < qc)
                if var < 4 and c < 8 - 2 * var:
                    masked = True
                qmb[c, var * 128 + r] = NEG if masked else 0.0
    kind = np.zeros((64, 4096), np.float32)
    for c in range(64):
        kind[c, c * 64:(c + 1) * 64] = 1.0
    qm = np.zeros((64, 2048), np.float32)
    for r in range(2048):
        qc = (half * 2048 + r) // 64
        qm[qc + 1:, r] = NEG
    sel = np.zeros((128, 2), np.float32)
    sel[:, half] = 1.0
    cosk, sink = _pos_tables(np.arange(4096))
    cosq, sinq = _pos_tables(half * 2048 + np.arange(2048))
    p = 2048 + np.arange(64)
    coss, sins = _pos_tables(np.concatenate([p, p]))
    return dict(ident_f=ident, cb=cb, kindb=kindb.astype(bf), qmb=qmb.astype(bf), kind=kind.astype(bf),
                qm=qm.astype(bf), sel=sel, cosk=cosk, sink=sink, cosq=cosq, sinq=sinq, coss=coss, sins=sins)


def _colmajor(g):
    return np.ascontiguousarray(g.reshape(-1, 128).T)


_NC_CACHE = {}


def prep(inp):
    f = lambda a: np.ascontiguousarray(np.asarray(a, dtype=np.float32))
    x_prompt, x_sample = f(inp["x_prompt"]), f(inp["x_sample"])
    gains = np.concatenate(
        [_colmajor(f(inp[n])[l]) for n in ("ln_mix_pre", "ln_mix_post", "ln_ffn_pre", "ln_ffn_post") for l in range(2)]
        + [_colmajor(f(inp["mla_q_norm"])[0]), _colmajor(f(inp["mla_kv_norm"])[0])], axis=1)
    rb = f(inp["a_rel_bias"])[0]
    r = np.arange(128)[:, None]; w = np.arange(640)[None, :]
    relT = np.ascontiguousarray(rb[:, np.clip(512 + r - w, -256, 256) + 256])
    wuq = f(inp["mla_w_uq"])[0]
    w_uq = np.ascontiguousarray(np.concatenate([wuq, wuq[:, :, 160:192], wuq[:, :, 128:160]], axis=2))
    wdkv = f(inp["mla_w_dkv"])[0]
    w_dkv = np.ascontiguousarray(np.concatenate([wdkv, wdkv[:, 544:576], wdkv[:, 512:544]], axis=1))
    shared = dict(
        gains=np.ascontiguousarray(gains), w_qkv=f(inp["a_w_qkv"])[0], w_ao=f(inp["a_w_o"])[0], relT=relT,
        w_dq=f(inp["mla_w_dq"])[0], w_uq=w_uq, w_dkv=w_dkv,
        w_uk=f(inp["mla_w_uk"])[0].reshape(512, 2048), w_uv=f(inp["mla_w_uv"])[0].reshape(512, 2048),
        w_bo=f(inp["mla_w_o"])[0], w1=f(inp["ffn_w1"]), w2=f(inp["ffn_w2"]))
    cak, cav = f(inp["cache_a_k"])[0].reshape(16, 512, 2048), f(inp["cache_a_v"])[0].reshape(16, 512, 2048)
    cckv, ckr = f(inp["cache_mla_ckv"])[0], f(inp["cache_mla_kr"])[0]
    consts = [_consts(0), _consts(1)]
    in_maps = []
    for c in range(N_CORES):
        b, half = c // 2, c % 2
        m = dict(shared)
        m.update(consts[half])
        m.update(xp=x_prompt[b], xs=x_sample[2 * c:2 * c + 2].reshape(128, 2048),
                 cak=cak[2 * c:2 * c + 2], cav=cav[2 * c:2 * c + 2],
                 cckv=cckv[2 * c:2 * c + 2], ckr=ckr[2 * c:2 * c + 2])
        in_maps.append(m)
    return in_maps


def post(res):
    R = lambda c, n: np.asarray(res[c][n], dtype=np.float32)
    y_prompt = np.stack([np.concatenate([R(2 * b, "y_p"), R(2 * b + 1, "y_p")], 0) for b in range(4)], 0)
    y_sample = np.concatenate([R(c, "y_s").reshape(2, 64, 2048) for c in range(8)], 0)
    nakp = np.stack([R(2 * b + 1, "nak_p") for b in range(4)], 0).reshape(1, 4, 512, 16, 128)
    navp = np.stack([R(2 * b + 1, "nav_p") for b in range(4)], 0).reshape(1, 4, 512, 16, 128)
    naks = np.concatenate([R(c, "nak_s").reshape(2, 64, 16, 128) for c in range(8)], 0)[None]
    navs = np.concatenate([R(c, "nav_s").reshape(2, 64, 16, 128) for c in range(8)], 0)[None]
    ckvp = np.stack([np.concatenate([R(2 * b, "nckv_p")[:2048], R(2 * b + 1, "nckv_p")[2048:]], 0) for b in range(4)], 0)[None]
    krp = np.stack([np.concatenate([R(2 * b, "nkr_p")[:2048], R(2 * b + 1, "nkr_p")[2048:]], 0) for b in range(4)], 0)[None]
    ckvs = np.concatenate([R(c, "nckv_s").reshape(2, 64, 512) for c in range(8)], 0)[None]
    krs = np.concatenate([R(c, "nkr_s").reshape(2, 64, 64) for c in range(8)], 0)[None]
    return (y_prompt, y_sample, nakp, navp, naks, navs, ckvp, krp, ckvs, krs)


def kernel(**inp):
    in_maps = prep(inp)
    if "nc" not in _NC_CACHE:
        _NC_CACHE["nc"] = build_nc()
    res = run_bass_kernel_spmd(_NC_CACHE["nc"], in_maps, core_ids=list(range(N_CORES))).results
    return post(res)
```

```python
import contextlib
import os
import numpy as np
import ml_dtypes
import concourse.bass as bass
import concourse.mybir as mybir
from concourse.bass_utils import run_bass_kernel_spmd

F32 = mybir.dt.float32
BF16 = mybir.dt.bfloat16
AF = mybir.ActivationFunctionType
ALU = mybir.AluOpType
AX = mybir.AxisListType

N_CORES = 8
D = 2048
SEQ = 4096
HALF = 2048
TT = 512
NEG = -1.0e30
EPS = 1e-6
SC_A = 128 ** -0.5
SC_B = 192 ** -0.5


class Buf:
    __slots__ = ("name", "w", "r")

    def __init__(self, name=""):
        self.name = name
        self.w = None
        self.r = {}


class Sched:
    NSLOT = 8
    ENGS = ("pe", "act", "dve", "pool", "sp")

    def __init__(self, nc):
        self.nc = nc
        self.prog = {k: [] for k in self.ENGS}
        self.cnt = {k: 0 for k in self.ENGS}
        self.dcnt = {"sp": 0, "pool": 0}
        self.waited = {k: {} for k in self.ENGS}
        self.sems = {}

    def _need(self, eng, tok, waits):
        if tok is None:
            return
        key, val = tok
        if key == ("eng", "pe") and eng == "pe":
            return
        if self.waited[eng].get(key, 0) >= val:
            return
        self.waited[eng][key] = val
        waits.append((key, val))

    def _deps(self, eng, reads, writes, waits):
        for b in reads:
            self._need(eng, b.w, waits)
        for b in writes:
            self._need(eng, b.w, waits)
            for key, val in b.r.items():
                self._need(eng, (key, val), waits)

    def _commit(self, tok, reads, writes):
        key, val = tok
        for b in reads:
            if b.r.get(key, 0) < val:
                b.r[key] = val
        for b in writes:
            b.w = tok
            b.r = {}

    def op(self, eng, fn, reads=(), writes=()):
        waits = []
        self._deps(eng, reads, writes, waits)
        self.cnt[eng] += 1
        tok = (("eng", eng), self.cnt[eng])
        self.prog[eng].append((waits, fn, ("eng", eng), 1))
        self._commit(tok, reads, writes)
        return tok

    def dma(self, q, fn, reads=(), writes=()):
        waits = []
        i = self.dcnt[q]
        self.dcnt[q] += 1
        slot = i % self.NSLOT
        key = ("dma", q, slot)
        if i >= self.NSLOT:
            self._need(q, (key, 16 * (i // self.NSLOT)), waits)
        self._deps(q, reads, writes, waits)
        tok = (key, 16 * (i // self.NSLOT + 1))
        self.prog[q].append((waits, fn, key, 16))
        self._commit(tok, reads, writes)
        return tok

    def _all_tokens(self, queues):
        toks = [(("eng", e), c) for e, c in self.cnt.items() if c > 0 and e != "sp"]
        for q in queues:
            n = self.dcnt[q]
            for s in range(min(n, self.NSLOT)):
                toks.append((("dma", q, s), 16 * ((n - s + self.NSLOT - 1) // self.NSLOT)))
        return toks

    def barrier(self, final=False):
        toks = self._all_tokens(("sp", "pool") if final else ("pool",))
        for e in (self.ENGS if final else ("pe", "act", "dve", "pool")):
            waits = []
            for t in toks:
                if t[0] == ("eng", e):
                    continue
                if self.waited[e].get(t[0], 0) >= t[1]:
                    continue
                self.waited[e][t[0]] = t[1]
                waits.append(t)
            if waits:
                self.prog[e].append((waits, None, None, 0))

    def emit(self, st):
        nc = self.nc
        keys = [("eng", e) for e in self.ENGS]
        for q in self.dcnt:
            keys += [("dma", q, s) for s in range(self.NSLOT)]
        for k in keys:
            self.sems[k] = st.enter_context(nc.semaphore("s_" + "_".join(map(str, k))))
        self.barrier(final=True)
        block = st.enter_context(nc.Block())

        def replay(name):
            def run(e):
                for waits, fn, key, inc in self.prog[name]:
                    for k, v in waits:
                        e.wait_ge(self.sems[k], v)
                    if fn is not None:
                        fn(e).then_inc(self.sems[key], inc)
            return run

        block.tensor(replay("pe"))
        block.scalar(replay("act"))
        block.vector(replay("dve"))
        block.gpsimd(replay("pool"))
        block.sync(replay("sp"))


def build_nc(l0_tiles=9, l1_tiles=5):
    nc = bass.Bass("TRN2", target_bir_lowering=False)
    st = contextlib.ExitStack()
    S = Sched(nc)

    def din(name, shape, dt=F32):
        return nc.dram_tensor(name, list(shape), dt, kind="ExternalInput").ap()

    def dout(name, shape):
        return nc.dram_tensor(name, list(shape), F32, kind="ExternalOutput").ap()

    def dscr(name, shape, dt):
        return nc.dram_tensor(name, list(shape), dt).ap()

    xp = din("xp", [SEQ, D]); xs = din("xs", [128, D])
    cak = din("cak", [2, 512, D]); cav = din("cav", [2, 512, D])
    cckv = din("cckv", [2, 2048, 512]); ckr = din("ckr", [2, 2048, 64])
    gains = din("gains", [128, 136])
    w_qkv = din("w_qkv", [D, 6144]); w_ao = din("w_ao", [D, D])
    relT = din("relT", [16, 128, 640])
    w_dq = din("w_dq", [D, 512]); w_uq = din("w_uq", [512, 16, 256])
    w_dkv = din("w_dkv", [D, 640]); w_uk = din("w_uk", [512, D]); w_uv = din("w_uv", [512, D])
    w_bo = din("w_bo", [D, D])
    w1 = din("w1", [2, D, 4 * D]); w2 = din("w2", [2, 4 * D, D])
    ident_f_d = din("ident_f", [128, 128]); cb_d = din("cb", [128, 256], BF16)
    kindb_d = din("kindb", [10, 640], BF16); qmb_d = din("qmb", [10, 640], BF16)
    kind_d = din("kind", [64, 4096], BF16); qm_d = din("qm", [64, 2048], BF16)
    sel_d = din("sel", [128, 2])
    cosk_d = din("cosk", [64, SEQ]); sink_d = din("sink", [64, SEQ])
    cosq_d = din("cosq", [64, HALF]); sinq_d = din("sinq", [64, HALF])
    coss_d = din("coss", [64, 128]); sins_d = din("sins", [64, 128])
    y_p = dout("y_p", [HALF, D]); y_s = dout("y_s", [128, D])
    nak_p = dout("nak_p", [512, D]); nav_p = dout("nav_p", [512, D])
    nak_s = dout("nak_s", [128, D]); nav_s = dout("nav_s", [128, D])
    nckv_p = dout("nckv_p", [SEQ, 512]); nkr_p = dout("nkr_p", [SEQ, 64])
    nckv_s = dout("nckv_s", [128, 512]); nkr_s = dout("nkr_s", [128, 64])

    class WMat:
        def __init__(self, name, src, Kd, Nd, gw=512, kper=8):
            self.src, self.gw, self.kper = src, gw, min(kper, Kd // 128)
            self.ng, self.nkp = Nd // gw, (Kd // 128) // self.kper
            self.d = dscr(name, [self.ng * self.nkp, 128, self.kper, gw], BF16)
            self.b = [Buf() for _ in range(self.ng * self.nkp)]
            self.done = False

        def cast(self):
            if self.done:
                return
            self.done = True
            for g in range(self.ng):
                for kp in range(self.nkp):
                    i = g * self.nkp + kp
                    r0 = kp * self.kper * 128
                    src = self.src[r0:r0 + self.kper * 128, g * self.gw:(g + 1) * self.gw].rearrange("(k p) c -> p k c", p=128)
                    dst = self.d[i]
                    S.dma("pool", lambda e, dst=dst, src=src: e.dma_start(out=dst, in_=src), writes=[self.b[i]])

        def piece(self, g, kp):
            self.cast()
            i = g * self.nkp + kp
            return self.d[i], self.b[i]

    Wqkv = WMat("Wqkv", w_qkv, D, 6144)
    Wao = WMat("Wao", w_ao, D, D)
    W1m = [WMat("W1_%d" % l, w1[l], D, 4 * D) for l in range(2)]
    W2m = [WMat("W2_%d" % l, w2[l], 4 * D, D) for l in range(2)]
    Wdq = WMat("Wdq", w_dq, D, 512)
    Wdkv_c = WMat("Wdkv_c", w_dkv[:, 0:512], D, 512)
    Wdkv_r = WMat("Wdkv_r", w_dkv[:, 512:640], D, 128, gw=128)
    Wuq = [WMat("Wuq%d" % h, w_uq[:, h, :], 512, 256, gw=256, kper=4) for h in range(16)]
    Wuk = WMat("Wuk", w_uk, 512, D, gw=128, kper=4)
    Wuv = WMat("Wuv", w_uv, 512, D, gw=256, kper=4)
    Wbo = WMat("Wbo", w_bo, D, D)
    relB = dscr("relB", [16, 128, 640], BF16); B_relB = Buf()
    KTs = dscr("KTs", [128, 16, SEQ], BF16); Vs = dscr("Vs", [SEQ, D], BF16)
    B_kv = [Buf() for _ in range(8)]
    x1s = dscr("x1s", [128, 16, SEQ], F32); B_x1 = [Buf() for _ in range(8)]
    x1ss = dscr("x1ss", [128, 16, 128], F32); B_x1s = Buf()
    ckvTs = dscr("ckvTs", [128, 4, SEQ], BF16); krTs = dscr("krTs", [64, SEQ], BF16)
    B_lat = [Buf() for _ in range(8)]
    ckvTss = dscr("ckvTss", [128, 4, 128], BF16); krTss = dscr("krTss", [64, 128], BF16); B_lats = Buf()

    K1 = 1024
    NA = 80 * K1
    AR = st.enter_context(nc.sbuf_tensor("arena", [128, NA], BF16))

    def v16(off, n):
        return AR[:, off:off + n]

    def v32(off, n):
        return AR[:, off:off + 2 * n].bitcast(F32)

    NWS = 3
    WS = [st.enter_context(nc.sbuf_tensor("ws%d" % i, [128, 4096], BF16)) for i in range(NWS)]
    B_ws = [Buf() for _ in range(NWS)]
    wcount = [0]
    cst = st.enter_context(nc.sbuf_tensor("sb_cst", [128, 256], BF16)); B_c = Buf()
    idf = st.enter_context(nc.sbuf_tensor("sb_idf", [128, 128], F32))
    gn = st.enter_context(nc.sbuf_tensor("sb_gn", [128, 136], F32))
    sel = st.enter_context(nc.sbuf_tensor("sb_sel", [128, 2], F32))
    mk = st.enter_context(nc.sbuf_tensor("sb_mk", [16, 1280], BF16))
    ident_b = cst[:, 0:128]; ones_b = cst[:, 128:256]
    sm = st.enter_context(nc.sbuf_tensor("sb_sm", [128, 32], F32)); B_sm = [Buf() for _ in range(8)]; B_rs = [Buf() for _ in range(8)]
    rstd = st.enter_context(nc.sbuf_tensor("sb_rstd", [128, 512], F32)); B_rstd = Buf()
    sqt = [st.enter_context(nc.sbuf_tensor("sqt%d" % i, [128, 512], BF16)) for i in range(2)]
    B_sq = [Buf(), Buf()]
    NTF = 2
    tmpf = [st.enter_context(nc.sbuf_tensor("tmpf%d" % i, [128, 512], F32)) for i in range(NTF)]
    B_tf = [Buf() for _ in range(NTF)]
    PA = [st.enter_context(nc.psum_tensor("pa%d" % i, [128, 1024], F32)) for i in range(2)]
    B_PA = [Buf(), Buf()]
    PB = [st.enter_context(nc.psum_tensor("pb%d" % i, [128, 512], F32)) for i in range(4)]
    B_PB = [Buf() for _ in range(4)]
    rr = {"pb": 0, "ev": 0, "sq": 0, "tf": 0, "sm": 0, "u": 0, "g": 0}

    S.dma("pool", lambda e: e.dma_start(out=cst[:], in_=cb_d), writes=[B_c])
    S.dma("pool", lambda e: e.dma_start(out=idf[:], in_=ident_f_d), writes=[B_c])
    S.dma("pool", lambda e: e.dma_start(out=gn[:], in_=gains), writes=[B_c])
    S.dma("pool", lambda e: e.dma_start(out=sel[:], in_=sel_d), writes=[B_c])
    S.dma("pool", lambda e: e.dma_start(out=mk[0:10, 0:640], in_=kindb_d), writes=[B_c])
    S.dma("pool", lambda e: e.dma_start(out=mk[0:10, 640:1280], in_=qmb_d), writes=[B_c])
    for h in range(16):
        S.dma("pool", lambda e, h=h: e.dma_start(out=relB[h], in_=relT[h]), writes=[B_relB])

    def wload(wm, g, kp):
        pap, pbuf = wm.piece(g, kp)
        i = wcount[0] % NWS
        wcount[0] += 1
        kc, gw = pap.shape[1], pap.shape[2]
        dst = WS[i][:, 0:kc * gw].rearrange("p (k c) -> p k c", c=gw)
        S.dma("sp", lambda e: e.dma_start(out=dst, in_=pap), reads=[pbuf], writes=[B_ws[i]])
        return dst, B_ws[i]

    def nextpb():
        i = rr["pb"] % 4
        rr["pb"] += 1
        return PB[i], B_PB[i]

    def accs4(n):
        g = rr["g"] % 2
        rr["g"] += 1
        if g == 0:
            return [(PB[m], B_PB[m]) for m in range(4)]
        return [(PA[m // 2][:, (m % 2) * 512:(m % 2 + 1) * 512], B_PA[m // 2]) for m in range(4)]

    def ubufs(l):
        return list({id(b): b for b in l}.values())

    def mm(out_ap, pairs, reads, writes):
        def fn(e):
            n = len(pairs)
            ins = None
            for i, (l, r) in enumerate(pairs):
                ins = e.matmul(out_ap, lhsT=l, rhs=r, start=(i == 0), stop=(i == n - 1))
            return ins
        S.op("pe", fn, reads, writes)

    def evac(out_ap, in_ap, reads, writes, scale=None, eng=None):
        if eng is None:
            eng = ("act", "dve")[rr["ev"] % 2]
            rr["ev"] += 1
        if eng == "act":
            if scale is None:
                S.op("act", lambda e: e.activation(out=out_ap, in_=in_ap, func=AF.Copy), reads, writes)
            else:
                S.op("act", lambda e: e.activation(out=out_ap, in_=in_ap, func=AF.Copy, scale=scale), reads, writes)
        else:
            if scale is None:
                S.op("dve", lambda e: e.tensor_copy(out=out_ap, in_=in_ap), reads, writes)
            else:
                S.op("dve", lambda e: e.tensor_scalar(out=out_ap, in0=in_ap, scalar1=scale, scalar2=None,
                                                     op0=ALU.mult), reads, writes)

    def rms(src, nch, n, dn, gcol0, dst=None, resid=None, bsrc=(), bdst=()):
        pb, bpb = nextpb()
        for c in range(nch):
            i = rr["sq"] % 2
            rr["sq"] += 1
            sq = sqt[i][:, 0:n]
            a = src(c)
            S.op("act", lambda e, sq=sq, a=a: e.activation(out=sq, in_=a, func=AF.Square),
                 reads=list(bsrc), writes=[B_sq[i]])
            def fn(e, sq=sq, c=c):
                return e.matmul(pb[:, 0:n], lhsT=ones_b, rhs=sq, start=(c == 0), stop=(c == nch - 1))
            S.op("pe", fn, reads=[B_sq[i], B_c], writes=[bpb])
        S.op("act", lambda e: e.activation(out=rstd[:, 0:n], in_=pb[:, 0:n], func=AF.Ln, scale=1.0 / dn, bias=EPS),
             reads=[bpb], writes=[B_rstd])
        S.op("act", lambda e: e.activation(out=rstd[:, 0:n], in_=rstd[:, 0:n], func=AF.Exp, scale=-0.5),
             reads=[B_rstd], writes=[B_rstd])
        for c in range(nch):
            a = src(c)
            g = gn[:, gcol0 + c:gcol0 + c + 1]
            if resid is None:
                o = dst(c)
                S.op("dve", lambda e, o=o, a=a, g=g: e.scalar_tensor_tensor(
                    out=o, in0=a, scalar=g, in1=rstd[:, 0:n], op0=ALU.mult, op1=ALU.mult),
                    reads=list(bsrc) + [B_rstd, B_c], writes=list(bdst))
            else:
                i = rr["tf"] % NTF
                rr["tf"] += 1
                t = tmpf[i][:, 0:n]
                x = resid(c)
                S.op("dve", lambda e, t=t, a=a, g=g: e.scalar_tensor_tensor(
                    out=t, in0=a, scalar=g, in1=rstd[:, 0:n], op0=ALU.mult, op1=ALU.mult),
                    reads=list(bsrc) + [B_rstd, B_c], writes=[B_tf[i]])
                S.op("pool", lambda e, x=x, t=t: e.tensor_tensor(out=x, in0=x, in1=t, op=ALU.add),
                     reads=[B_tf[i]], writes=list(bdst))

    def lin4(wm, rhs, n, sink, bsrc, groups=None, outw=None):
        outw = outw or (wm.gw // 128)
        for g in (groups if groups is not None else range(wm.ng)):
            accs = accs4(n)
            for kp in range(wm.nkp):
                wv, wb = wload(wm, g, kp)
                ops = [(accs[m][0][:, 0:n], wv[:, k, m * 128:(m + 1) * 128], rhs(kp * wm.kper + k),
                        (kp == 0 and k == 0), (kp == wm.nkp - 1 and k == wm.kper - 1))
                       for k in range(wm.kper) for m in range(outw)]
                def fn(e, ops=ops):
                    ins = None
                    for o, l, r, a0, a1 in ops:
                        ins = e.matmul(o, lhsT=l, rhs=r, start=a0, stop=a1)
                    return ins
                S.op("pe", fn, reads=list(bsrc) + [wb], writes=ubufs([a[1] for a in accs[:outw]]))
            for m in range(outw):
                sink(g * outw + m, accs[m][0][:, 0:n], accs[m][1])

    def lin_tok(wm, lhs, nsub, mrows, sink, bsrc, groups):
        for g in groups:
            accs = accs4(512)
            for kp in range(wm.nkp):
                wv, wb = wload(wm, g, kp)
                ops = [(accs[s][0][:mrows, 0:wm.gw], lhs(kp * wm.kper + k, s), wv[:, k, :],
                        (kp == 0 and k == 0), (kp == wm.nkp - 1 and k == wm.kper - 1))
                       for k in range(wm.kper) for s in range(nsub)]
                def fn(e, ops=ops):
                    ins = None
                    for o, l, r, a0, a1 in ops:
                        ins = e.matmul(o, lhsT=l, rhs=r, start=a0, stop=a1)
                    return ins
                S.op("pe", fn, reads=list(bsrc) + [wb], writes=ubufs([a[1] for a in accs[:nsub]]))
            for s in range(nsub):
                sink(g, s, accs[s][0][:mrows, 0:wm.gw], accs[s][1])

    def attn_unit(nq, nk, terms, tbufs, vt, vbufs, o_dst, o_bufs, bufset):
        Sb, B_S, Pbs, B_Ps, PT, B_PT = bufset
        u = rr["u"] % 2
        rr["u"] += 1
        Pb, B_P = Pbs[u], B_Ps[u]
        if Sb is None:
            pa, bpa = PA[u], B_PA[u]
            for k0 in range(0, nk, 512):
                kn = min(512, nk - k0)
                mm(pa[:nq, k0:k0 + kn], terms(k0, kn), reads=list(tbufs), writes=[bpa])
            ssrc, bs = pa[:nq, 0:nk], bpa
        else:
            for s0 in range(0, nk, 1024):
                sn = min(1024, nk - s0)
                j = (s0 // 1024) % 2
                for k0 in range(s0, s0 + sn, 512):
                    kn = min(512, nk - k0)
                    mm(PA[j][:nq, k0 - s0:k0 - s0 + kn], terms(k0, kn), reads=list(tbufs), writes=[B_PA[j]])
                evac(Sb[:nq, s0:s0 + sn], PA[j][:nq, 0:sn], reads=[B_PA[j]], writes=[B_S], eng="act")
            ssrc, bs = Sb[:nq, 0:nk], B_S
        i = rr["sm"] % 8
        rr["sm"] += 1
        mx = sm[:nq, 4 * i:4 * i + 1]; nm = sm[:nq, 4 * i + 1:4 * i + 2]
        rs = sm[:nq, 4 * i + 2:4 * i + 3]; ri = sm[:nq, 4 * i + 3:4 * i + 4]
        bsm, brs = B_sm[i], B_rs[i]
        S.op("pool", lambda e: e.memset(rs, 0.0), reads=[], writes=[brs])
        S.op("dve", lambda e: e.reduce_max(out=mx, in_=ssrc, axis=AX.X), reads=[bs], writes=[bsm])
        S.op("dve", lambda e: e.tensor_scalar(out=nm, in0=mx, scalar1=-1.0, scalar2=None, op0=ALU.mult),
             reads=[bsm], writes=[bsm])
        S.op("act", lambda e: e.activation(out=Pb[:nq, 0:nk], in_=ssrc, func=AF.Exp, bias=nm, scale=1.0,
                                            accum_out=rs), reads=[bs, bsm], writes=[B_P, brs])
        nkt = (nk + 127) // 128

        def stageB():
            for g0 in range(0, nkt, 4):
                pb, bpb = nextpb()
                pbb = pb[:].bitcast(BF16)
                gn_ = min(4, nkt - g0)
                tops = [(pbb[:min(128, nk - 128 * t), (t - g0) * 128:(t - g0) * 128 + nq],
                         Pb[:nq, 128 * t:128 * t + min(128, nk - 128 * t)]) for t in range(g0, g0 + gn_)]
                def fn(e, tops=tops):
                    ins = None
                    for o, a_ in tops:
                        ins = e.transpose(out=o, in_=a_, identity=ident_b[:nq, :nq])
                    return ins
                S.op("pe", fn, reads=[B_P, B_c], writes=[bpb])
                full = (nk - 128 * (g0 + gn_ - 1)) >= 128
                gi = (g0 // 4) % 8
                if full:
                    evac(PT[:, g0:g0 + gn_, 0:nq], pbb[:, 0:gn_ * 128].rearrange("p (t q) -> p t q", q=128)[:, :, 0:nq],
                         reads=[bpb], writes=[B_PT[gi]], eng="dve")
                else:
                    for t in range(g0, g0 + gn_):
                        kn = min(128, nk - 128 * t)
                        evac(PT[:kn, t, 0:nq], pbb[:kn, (t - g0) * 128:(t - g0) * 128 + nq],
                             reads=[bpb], writes=[B_PT[gi]], eng="dve")
            S.op("dve", lambda e: e.reciprocal(out=ri, in_=rs), reads=[brs], writes=[brs])
            pv, bpv = nextpb()
            pvops = [(PT[:min(128, nk - 128 * t), t, 0:nq], vt(t, min(128, nk - 128 * t))) for t in range(nkt)]
            def fnpv(e):
                ins = None
                for t, (l, r) in enumerate(pvops):
                    ins = e.matmul(pv[:nq, 0:128], lhsT=l, rhs=r, start=(t == 0), stop=(t == nkt - 1))
                return ins
            S.op("pe", fnpv, reads=ubufs([B_PT[(g // 4) % 8] for g in range(0, nkt, 4)] + list(vbufs)), writes=[bpv])
            S.op("dve", lambda e: e.tensor_scalar(out=o_dst, in0=pv[:nq, 0:128], scalar1=ri, scalar2=None, op0=ALU.mult),
                 reads=[bpv, brs], writes=list(o_bufs))

        return stageB

    def run_units(makers):
        pend = None
        for mk_ in makers:
            nxt = mk_()
            if pend is not None:
                pend()
            pend = nxt
        if pend is not None:
            pend()

    def tr4_f32(dst, src, nrows_in, reads, writes):
        raise NotImplementedError

    def load_rows_T(dst3, bdst, src_rows, stage, bstage, nchunks=16):
        S.dma("pool", lambda e: e.dma_start(out=stage[:, 0:nchunks * 128], in_=src_rows), writes=[bstage])
        for c0 in range(0, nchunks, 4):
            pb, bpb = nextpb()
            def fn(e, c0=c0, pb=pb):
                ins = None
                for c in range(c0, c0 + 4):
                    ins = e.transpose(out=pb[:, (c - c0) * 128:(c - c0 + 1) * 128],
                                      in_=stage[:, c * 128:(c + 1) * 128], identity=idf[:])
                return ins
            S.op("pe", fn, reads=[bstage, B_c], writes=[bpb])
            evac(dst3(c0), pb[:, 0:512].rearrange("p (c t) -> p c t", t=128), reads=[bpb], writes=[bdst])

    def store_tok(src, bsrc, nch, np_, ntok, dst_rows, stage, bstage):
        for s in range(ntok // 128):
            j = s % 2
            for c0 in range(0, nch, 4):
                cn = min(4, nch - c0)
                pb, bpb = nextpb()
                tops = [(pb[:, (c - c0) * np_:(c - c0 + 1) * np_], src(c)[:, s * 128:(s + 1) * 128]) for c in range(c0, c0 + cn)]
                def fn(e, tops=tops):
                    ins = None
                    for o, a in tops:
                        ins = e.transpose(out=o, in_=a, identity=idf[:np_, :np_])
                    return ins
                S.op("pe", fn, reads=list(bsrc) + [B_c], writes=[bpb])
                evac(stage[j][:, c0 * np_:(c0 + cn) * np_], pb[:, 0:cn * np_], reads=[bpb], writes=[bstage[j]])
            S.dma("pool", lambda e, s=s, j=j: e.dma_start(out=dst_rows(s), in_=stage[j][:, 0:nch * np_]),
                  reads=[bstage[j]])

    G_MPRE, G_MPOST, G_FPRE, G_FPOST, G_QN, G_KVN = 0, 32, 64, 96, 128, 132

    def ffn(l, xT, bx, hT, bh, yT, by, hid, ntok):
        rms(lambda c: xT[:, c, 0:ntok], 16, ntok, D, G_FPRE + 16 * l, dst=lambda c: hT[:, c, 0:ntok],
            bsrc=[bx], bdst=[bh])
        bhid = [Buf() for _ in range(64)]
        def sink1(m, ps, bps):
            i = rr["tf"] % NTF
            rr["tf"] += 1
            t = tmpf[i][:, 0:ntok]
            S.op("act", lambda e: e.activation(out=t, in_=ps, func=AF.Relu), reads=[bps], writes=[B_tf[i]])
            eng = ("pool", "dve")[m % 2]
            S.op(eng, lambda e: e.tensor_tensor(out=hid[:, m, 0:ntok], in0=t, in1=t, op=ALU.mult),
                 reads=[B_tf[i]], writes=[bhid[m]])
        lin4(W1m[l], lambda k: hT[:, k, 0:ntok], ntok, sink1, [bh])
        def sink2(m, ps, bps):
            evac(yT[:, m, 0:ntok], ps, reads=[bps], writes=[by])
        lin4(W2m[l], lambda k: hid[:, k, 0:ntok], ntok, sink2, bhid)
        rms(lambda c: yT[:, c, 0:ntok], 16, ntok, D, G_FPOST + 16 * l, resid=lambda c: xT[:, c, 0:ntok],
            bsrc=[by], bdst=[bx])

    xT = v32(0, 8192).rearrange("p (c t) -> p c t", t=512)
    hT = v16(16 * K1, 8192).rearrange("p (c t) -> p c t", t=512)
    yT = v32(24 * K1, 8192).rearrange("p (c t) -> p c t", t=512)
    hid = v16(40 * K1, 32768).rearrange("p (f t) -> p f t", t=512)
    QT = v16(24 * K1, 8192).rearrange("p (h t) -> p h t", t=512)
    KT = v16(32 * K1, 16384).rearrange("p (h t) -> p h t", t=1024)
    Vt = v16(48 * K1, 16384).rearrange("p (s c) -> p s c", c=2048)
    otok = v16(64 * K1, 4096).rearrange("p (j c) -> p j c", c=2048)
    stg = [v32(68 * K1 + 4096 * j, 2048) for j in range(2)]
    btile = [v16(76 * K1 + 640 * j, 640) for j in range(2)]
    btile6 = [v16(72 * K1 + 640 * j, 640) for j in range(6)]
    B_bt6 = [Buf() for _ in range(6)]
    Pb0 = [v16(77 * K1 + 256 + 640 * j, 640) for j in range(2)]
    PT0 = v16(78 * K1 + 512, 640).rearrange("p (t q) -> p t q", q=128)
    B_P0, B_PT0 = [Buf(), Buf()], [Buf() for _ in range(8)]
    set0 = (None, None, Pb0, B_P0, PT0, B_PT0)
    oT = hT
    B_bt = [Buf(), Buf()]

    def l0_tile(t):
        smp = (t == 8)
        ntok = 128 if smp else 512
        nsub = ntok // 128
        S.barrier()
        bx, bh, bq, bk, bv, bo, by = Buf(), Buf(), Buf(), Buf(), Buf(), Buf(), Buf()
        bstg = [Buf(), Buf()]
        botok = [Buf(), Buf()]
        for s in range(nsub):
            rows = xs[:, :] if smp else xp[t * 512 + s * 128:t * 512 + (s + 1) * 128, :]
            load_rows_T(lambda c0, s=s: xT[:, c0:c0 + 4, s * 128:(s + 1) * 128], bx, rows, stg[s % 2], bstg[s % 2])
        rms(lambda c: xT[:, c, 0:ntok], 16, ntok, D, G_MPRE, dst=lambda c: hT[:, c, 0:ntok], bsrc=[bx], bdst=[bh])
        def sinkq(m, ps, bps):
            evac(QT[:, m, 0:ntok], ps, reads=[bps], writes=[bq], scale=SC_A)
        lin4(Wqkv, lambda k: hT[:, k, 0:ntok], ntok, sinkq, [bh], groups=range(0, 4))
        kcur = 640 if smp else 512
        def sinkk(m, ps, bps):
            evac(KT[:, m - 16, kcur:kcur + ntok], ps, reads=[bps], writes=[bk])
        lin4(Wqkv, lambda k: hT[:, k, 0:ntok], ntok, sinkk, [bh], groups=range(4, 8))
        if int(os.environ.get('STOP_AT', '99')) <= 1:
            return
        want_out = smp or t == 7
        if want_out:
            def sinkko(g, s, ps, bps):
                i = rr["tf"] % NTF
                rr["tf"] += 1
                evac(tmpf[i][:, :], ps, reads=[bps], writes=[B_tf[i]])
                dst = (nak_s if smp else nak_p)[s * 128:(s + 1) * 128, (g - 4) * 512:(g - 3) * 512]
                S.dma("pool", lambda e: e.dma_start(out=dst, in_=tmpf[i][:, :]), reads=[B_tf[i]])
            for s_ in range(nsub):
                def sk(g, s, ps, bps, s_=s_):
                    sinkko(g, s_, ps, bps)
                lin_tok(Wqkv, lambda k, s, s_=s_: hT[:, k, s_ * 128:(s_ + 1) * 128], 1, 128, sk, [bh], groups=range(4, 8))
        if int(os.environ.get('STOP_AT', '99')) <= 2:
            return
        if not smp:
            def sinkv(g, s, ps, bps):
                evac(Vt[:, 4 + s, (g - 8) * 512:(g - 7) * 512], ps, reads=[bps], writes=[bv])
            lin_tok(Wqkv, lambda k, s: hT[:, k, s * 128:(s + 1) * 128], nsub, 128, sinkv, [bh], groups=range(8, 12))
            if want_out:
                for s_ in range(nsub):
                    def sv(g, s, ps, bps, s_=s_):
                        i = rr["tf"] % NTF
                        rr["tf"] += 1
                        evac(tmpf[i][:, :], ps, reads=[bps], writes=[B_tf[i]])
                        dst = nav_p[s_ * 128:(s_ + 1) * 128, (g - 8) * 512:(g - 7) * 512]
                        S.dma("pool", lambda e: e.dma_start(out=dst, in_=tmpf[i][:, :]), reads=[B_tf[i]])
                    lin_tok(Wqkv, lambda k, s, s_=s_: hT[:, k, s_ * 128:(s_ + 1) * 128], 1, 128, sv, [bh], groups=range(8, 12))
        else:
            def sinkvs(g, a, ps, bps):
                evac(Vt[:64, 5 + a, (g - 8) * 512:(g - 7) * 512], ps, reads=[bps], writes=[bv])
            def sinkvo(g, s, ps, bps):
                i = rr["tf"] % NTF
                rr["tf"] += 1
                evac(tmpf[i][:, :], ps, reads=[bps], writes=[B_tf[i]])
                dst = nav_s[:, (g - 8) * 512:(g - 7) * 512]
                S.dma("pool", lambda e: e.dma_start(out=dst, in_=tmpf[i][:, :]), reads=[B_tf[i]])
            lin_tok(Wqkv, lambda k, s: hT[:, k, 0:128], 1, 128, sinkvo, [bh], groups=range(8, 12))
            lin_tok(Wqkv, lambda k, a: hT[:, k, a * 64:(a + 1) * 64], 2, 64, sinkvs, [bh], groups=range(8, 12))
        if int(os.environ.get('STOP_AT', '99')) <= 3:
            return
        if not smp:
            if t < 7:
                S.dma("pool", lambda e, t=t: e.dma_start(out=KTs[:, :, t * 512:(t + 1) * 512], in_=KT[:, :, 512:1024]),
                      reads=[bk], writes=[B_kv[t]])
                S.dma("pool", lambda e, t=t: e.dma_start(
                    out=Vs[t * 512:(t + 1) * 512, :].rearrange("(s p) c -> p s c", p=128), in_=Vt[:, 4:8, :]),
                    reads=[bv], writes=[B_kv[t]])
            if t == 0:
                S.op("pool", lambda e: e.memset(KT[:, :, 0:512], 0.0), writes=[bk])
                S.op("pool", lambda e: e.memset(Vt[:, 0:4, :], 0.0), writes=[bv])
            else:
                S.dma("pool", lambda e, t=t: e.dma_start(out=KT[:, :, 0:512], in_=KTs[:, :, (t - 1) * 512:t * 512]),
                      reads=[B_kv[t - 1]], writes=[bk])
                S.dma("pool", lambda e, t=t: e.dma_start(
                    out=Vt[:, 0:4, :], in_=Vs[(t - 1) * 512:t * 512, :].rearrange("(s p) c -> p s c", p=128)),
                    reads=[B_kv[t - 1]], writes=[bv])
        kindb = mk[0:10, 0:640]

        def band_units(s, a=None, t=t):
            nq = 128 if a is None else 64
            nk = 640 if a is None else 576
            q0 = s * 128 if a is None else a * 64
            w0 = s * 128 if a is None else 0
            var = min(4 * t + s, 4) if a is None else None
            j = (s if a is None else a) % 2
            def mk_unit(h):
                if a is None:
                    bt_, bbt = btile6[h % 6], B_bt6[h % 6]
                else:
                    bt_, bbt = btile[h % 2], B_bt[h % 2]
                S.dma("sp", lambda e, h=h, bt_=bt_: e.dma_start(out=bt_, in_=relB[h]), reads=[B_relB], writes=[bbt])

                def terms(k0, kn, h=h, bt_=bt_):
                    l = [(QT[:, h, q0:q0 + nq], KT[:, h, w0 + k0:w0 + k0 + kn]),
                         (ident_b[:nq, :nq], bt_[:nq, k0:k0 + kn])]
                    if var is not None:
                        l.append((mk[0:10, 640 + var * 128:640 + (var + 1) * 128], kindb[:, k0:k0 + kn]))
                    return l

                def vt(kt, kn, h=h):
                    if a is not None and kt == 4:
                        return Vt[:kn, 5 + a, h * 128:(h + 1) * 128]
                    return Vt[:kn, (s if a is None else 0) + kt, h * 128:(h + 1) * 128]
                return attn_unit(nq, nk, terms, [bq, bk, bbt, B_c], vt, [bv],
                                 otok[:nq, j, h * 128:(h + 1) * 128], [botok[j]], set0)
            run_units([(lambda h=h: mk_unit(h)) for h in range(16)])
            for c0 in range(0, 16, 4):
                pb, bpb = nextpb()
                pbb = pb[:].bitcast(BF16)
                def fn(e, c0=c0, pbb=pbb):
                    ins = None
                    for c in range(c0, c0 + 4):
                        ins = e.transpose(out=pbb[:, (c - c0) * 128:(c - c0) * 128 + nq],
                                          in_=otok[:nq, j, c * 128:(c + 1) * 128], identity=ident_b[:nq, :nq])
                    return ins
                S.op("pe", fn, reads=[botok[j], B_c], writes=[bpb])
                evac(oT[:, c0:c0 + 4, q0:q0 + nq],
                     pbb[:, 0:512].rearrange("p (c q) -> p c q", q=128)[:, :, 0:nq], reads=[bpb], writes=[bo, bh])
        if not smp:
            for s in range(nsub):
                band_units(s)
        else:
            for a in ([] if os.environ.get('SKIP_A') else range(2)):
                for s4 in range(4):
                    load_rows_T(lambda c0, s4=s4: KT[:, c0:c0 + 4, s4 * 128:(s4 + 1) * 128], bk,
                                cak[a, s4 * 128:(s4 + 1) * 128, :], stg[s4 % 2], bstg[s4 % 2])
                for s4 in range(4):
                    j = s4 % 2
                    S.dma("pool", lambda e, a=a, s4=s4, j=j: e.dma_start(out=stg[j], in_=cav[a, s4 * 128:(s4 + 1) * 128, :]),
                          writes=[bstg[j]])
                    S.op("dve", lambda e, s4=s4, j=j: e.tensor_copy(out=Vt[:, s4, :], in_=stg[j]),
                         reads=[bstg[j]], writes=[bv])
                S.op("pool", lambda e, a=a: e.tensor_copy(out=KT[:, :, 512:576], in_=KT[:, :, 640 + a * 64:704 + a * 64]),
                     reads=[bk], writes=[bk])
                band_units(0, a)
        if int(os.environ.get('STOP_AT', '99')) <= 4:
            return
        S.barrier()
        by = Buf()
        def sinko(m, ps, bps):
            evac(yT[:, m, 0:ntok], ps, reads=[bps], writes=[by])
        lin4(Wao, lambda k: oT[:, k, 0:ntok], ntok, sinko, [bo, bh])
        rms(lambda c: yT[:, c, 0:ntok], 16, ntok, D, G_MPOST, resid=lambda c: xT[:, c, 0:ntok], bsrc=[by], bdst=[bx])
        if int(os.environ.get('STOP_AT', '99')) <= 5:
            return
        S.barrier()
        by = Buf(); bh = Buf()
        ffn(0, xT, bx, hT, bh, yT, by, hid, ntok)
        if smp:
            S.dma("pool", lambda e: e.dma_start(out=x1ss[:, :, :], in_=xT[:, :, 0:128]), reads=[bx], writes=[B_x1s])
        else:
            S.dma("pool", lambda e, t=t: e.dma_start(out=x1s[:, :, t * 512:(t + 1) * 512], in_=xT[:, :, :]),
                  reads=[bx], writes=[B_x1[t]])
        if int(os.environ.get('STOP_AT', '99')) <= 6:
            return
        S.barrier()
        by = Buf(); bh = Buf()
        rms(lambda c: xT[:, c, 0:ntok], 16, ntok, D, G_MPRE + 16, dst=lambda c: hT[:, c, 0:ntok], bsrc=[bx], bdst=[bh])
        def sinkc(m, ps, bps):
            evac(yT[:, m, 0:ntok], ps, reads=[bps], writes=[by])
        lin4(Wdkv_c, lambda k: hT[:, k, 0:ntok], ntok, sinkc, [bh])
        cs = v32(68 * K1, 1024).rearrange("p (a t) -> p a t", t=512)
        bcs, bkr = Buf(), Buf()
        cd, sd, p0 = (coss_d, sins_d, 0) if smp else (cosk_d, sink_d, t * 512)
        S.dma("pool", lambda e: e.dma_start(out=cs[:64, 0, 0:ntok], in_=cd[:, p0:p0 + ntok]), writes=[bcs])
        S.dma("pool", lambda e: e.dma_start(out=cs[:64, 1, 0:ntok], in_=sd[:, p0:p0 + ntok]), writes=[bcs])
        krf = yT[:64, 5, 0:ntok]
        def sinkr(m, ps, bps):
            pass
        accs = accs4(ntok)
        for kp in range(Wdkv_r.nkp):
            wv, wb = wload(Wdkv_r, 0, kp)
            ops = [(accs[m][0][:64, 0:ntok], wv[:, k, m * 64:(m + 1) * 64], hT[:, kp * Wdkv_r.kper + k, 0:ntok],
                    (kp == 0 and k == 0), (kp == Wdkv_r.nkp - 1 and k == Wdkv_r.kper - 1))
                   for k in range(Wdkv_r.kper) for m in range(2)]
            def fn(e, ops=ops):
                ins = None
                for o, l, r, a0, a1 in ops:
                    ins = e.matmul(o, lhsT=l, rhs=r, start=a0, stop=a1)
                return ins
            S.op("pe", fn, reads=[bh, wb], writes=ubufs([accs[0][1], accs[1][1]]))
        S.op("dve", lambda e: e.tensor_tensor(out=yT[:64, 4, 0:ntok], in0=accs[0][0][:64, 0:ntok], in1=cs[:64, 0, 0:ntok], op=ALU.mult),
             reads=[accs[0][1], bcs], writes=[bkr])
        S.op("dve", lambda e: e.tensor_tensor(out=krf, in0=accs[1][0][:64, 0:ntok], in1=cs[:64, 1, 0:ntok], op=ALU.mult),
             reads=[accs[1][1], bcs], writes=[bkr])
        S.op("pool", lambda e: e.tensor_tensor(out=krf, in0=krf, in1=yT[:64, 4, 0:ntok], op=ALU.add), reads=[bkr], writes=[bkr])
        bcn = Buf()
        rms(lambda c: yT[:, c, 0:ntok], 4, ntok, 512, G_KVN, dst=lambda c: yT[:, 8 + c, 0:ntok], bsrc=[by], bdst=[bcn])
        lat16 = hid[:, 40:45, :]
        bl16 = Buf()
        S.op("pool", lambda e: e.tensor_copy(out=lat16[:, 0:4, 0:ntok], in_=yT[:, 8:12, 0:ntok]), reads=[bcn], writes=[bl16])
        S.op("pool", lambda e: e.tensor_copy(out=lat16[:64, 4, 0:ntok], in_=krf), reads=[bkr], writes=[bl16])
        if smp:
            S.dma("pool", lambda e: e.dma_start(out=ckvTss[:, :, :], in_=lat16[:, 0:4, 0:128]), reads=[bl16], writes=[B_lats])
            S.dma("pool", lambda e: e.dma_start(out=krTss[:, :], in_=lat16[:64, 4, 0:128]), reads=[bl16], writes=[B_lats])
        else:
            S.dma("pool", lambda e, t=t: e.dma_start(out=ckvTs[:, :, t * 512:(t + 1) * 512], in_=lat16[:, 0:4, :]),
                  reads=[bl16], writes=[B_lat[t]])
            S.dma("pool", lambda e, t=t: e.dma_start(out=krTs[:, t * 512:(t + 1) * 512], in_=lat16[:64, 4, :]),
                  reads=[bl16], writes=[B_lat[t]])
        stg2 = [v32(72 * K1 + 2048 * j, 1024) for j in range(2)]
        bst2 = [Buf(), Buf()]
        if smp:
            store_tok(lambda c: yT[:, 8 + c, :], [bcn], 4, 128, ntok, lambda s: nckv_s[:, :], stg2, bst2)
            store_tok(lambda c: yT[:64, 5, :], [bkr], 1, 64, ntok, lambda s: nkr_s[:, :], stg2, bst2)
        else:
            store_tok(lambda c: yT[:, 8 + c, :], [bcn], 4, 128, ntok,
                      lambda s, t=t: nckv_p[t * 512 + s * 128:t * 512 + (s + 1) * 128, :], stg2, bst2)
            store_tok(lambda c: yT[:64, 5, :], [bkr], 1, 64, ntok,
                      lambda s, t=t: nkr_p[t * 512 + s * 128:t * 512 + (s + 1) * 128, :], stg2, bst2)
        if t == 0:
            for wm in [Wdq] + Wuq + [Wuk, Wuv, Wbo, W1m[1], W2m[1]]:
                wm.cast()

    for t_ in (range(l0_tiles) if isinstance(l0_tiles, int) else l0_tiles):
        l0_tile(t_)

    cq = v16(0, 2048).rearrange("p (c t) -> p c t", t=512)
    ckvT = v16(2 * K1, 16384).rearrange("p (c t) -> p c t", t=4096)
    krT = v16(18 * K1, 4096)
    kind = v16(22 * K1, 4096)
    qm = v16(26 * K1, 2048)
    knT = v16(28 * K1, 4096)
    v2 = v16(32 * K1, 8192).rearrange("p (t c) -> p t c", c=256)
    qn = v16(40 * K1, 1024).rearrange("p (j t) -> p j t", t=512)
    qr = v16(41 * K1, 1024).rearrange("p (j t) -> p j t", t=512)
    otk = v16(42 * K1, 8192).rearrange("p (s c) -> p s c", c=2048)
    rtab = v32(50 * K1, 1024).rearrange("p (a t) -> p a t", t=512)
    rtmp = v32(52 * K1, 1024).rearrange("p (a t) -> p a t", t=512)
    stq = [v32(54 * K1 + 1024 * j, 512) for j in range(2)]
    Sb1 = v32(56 * K1, 4096)
    Pb1 = v16(64 * K1, 4096)
    PT1 = v16(68 * K1, 4096).rearrange("p (t q) -> p t q", q=128)
    ovT = v16(72 * K1, 8192).rearrange("p (h t) -> p h t", t=512)
    Pb1b = st.enter_context(nc.sbuf_tensor("pb1b", [128, 4096], BF16))
    set1 = (Sb1, Buf(), [Pb1, Pb1b[:, :]], [Buf(), Buf()], PT1, [Buf() for _ in range(8)])
    B_keys, B_mask, B_kn, B_v2, B_qh, B_otk, B_rtab, B_rtmp, B_ov, B_cq = (
        Buf(), Buf(), Buf(), Buf(), [Buf(), Buf()], Buf(), Buf(), Buf(), Buf(), Buf())
    B_stq = [Buf(), Buf()]

    def l1_tile(i):
        smp = (i == 4)
        ntok = 128 if smp else 512
        nsub = ntok // 128
        S.barrier()
        bx, bh, by = Buf(), Buf(), Buf()

        def load_x1(bx, by, i=i, smp=smp):
            if smp:
                S.dma("pool", lambda e: e.dma_start(out=xT[:, :, 0:128], in_=x1ss[:, :, :]), reads=[B_x1s], writes=[bx])
            else:
                S.dma("pool", lambda e: e.dma_start(out=xT[:, :, :], in_=x1s[:, :, i * 512:(i + 1) * 512]),
                      reads=[B_x1[i]], writes=[bx])
                S.dma("pool", lambda e: e.dma_start(out=yT[:, :, :], in_=x1s[:, :, 2048 + i * 512:2048 + (i + 1) * 512]),
                      reads=[B_x1[4 + i]], writes=[by])
                S.op("dve", lambda e: e.tensor_scalar(out=yT[:, :, :], in0=yT[:, :, :], scalar1=sel[:, 1:2], scalar2=None,
                                                     op0=ALU.mult), reads=[by, B_c], writes=[by])
                S.op("dve", lambda e: e.scalar_tensor_tensor(out=xT[:, :, :], in0=xT[:, :, :], scalar=sel[:, 0:1],
                                                            in1=yT[:, :, :], op0=ALU.mult, op1=ALU.add),
                     reads=[by, B_c], writes=[bx])
        load_x1(bx, by)
        rms(lambda c: xT[:, c, 0:ntok], 16, ntok, D, G_MPRE + 16, dst=lambda c: hT[:, c, 0:ntok], bsrc=[bx], bdst=[bh])
        by = Buf()
        def sinkcq(m, ps, bps):
            evac(yT[:, m, 0:ntok], ps, reads=[bps], writes=[by])
        lin4(Wdq, lambda k: hT[:, k, 0:ntok], ntok, sinkcq, [bh])
        rms(lambda c: yT[:, c, 0:ntok], 4, ntok, 512, G_QN, dst=lambda c: cq[:, c, 0:ntok], bsrc=[by], bdst=[B_cq, bx])
        S.barrier()
        def seq_body(a):
            if a is None:
                nk = 2048 + 512 * (i + 1)
                S.dma("pool", lambda e, nk=nk: e.dma_start(out=ckvT[:, :, 0:nk], in_=ckvTs[:, :, 0:nk]),
                      reads=B_lat, writes=[B_keys])
                S.dma("pool", lambda e, nk=nk: e.dma_start(out=krT[:64, 0:nk], in_=krTs[:, 0:nk]), reads=B_lat, writes=[B_keys])
                S.dma("pool", lambda e: e.dma_start(out=krT[64:128, :], in_=kind_d), writes=[B_mask])
                for j_ in range(2):
                    S.dma("pool", lambda e, j_=j_: e.dma_start(out=qr[64:128, j_, :], in_=qm_d[:, i * 512:(i + 1) * 512]),
                          writes=[B_mask])
            else:
                nk = 2112
                S.dma("pool", lambda e: e.dma_start(out=krT[64:128, :], in_=kind_d), writes=[B_mask])
                S.op("pool", lambda e: e.memset(qr[64:128, :, :], 0.0), writes=[B_mask])
                for s16 in range(16):
                    j = s16 % 2
                    load_rows_T(lambda c0, s16=s16: ckvT[:, c0:c0 + 4, s16 * 128:(s16 + 1) * 128], B_keys,
                                cckv[a, s16 * 128:(s16 + 1) * 128, :], stq[j], B_stq[j], nchunks=4)
                for s4 in range(4):
                    j = s4 % 2
                    S.dma("pool", lambda e, a=a, s4=s4, j=j: e.dma_start(
                        out=stq[j][:, 0:256].rearrange("p (s r) -> p s r", r=64),
                        in_=ckr[a, s4 * 512:(s4 + 1) * 512, :].rearrange("(s p) r -> p s r", p=128)), writes=[B_stq[j]])
                    pb, bpb = nextpb()
                    def fn(e, pb=pb, j=j):
                        ins = None
                        for c in range(4):
                            ins = e.transpose(out=pb[:64, c * 128:(c + 1) * 128], in_=stq[j][:, c * 64:(c + 1) * 64], identity=idf[:])
                        return ins
                    S.op("pe", fn, reads=[B_stq[j], B_c], writes=[bpb])
                    evac(krT[:64, s4 * 512:(s4 + 1) * 512], pb[:64, 0:512], reads=[bpb], writes=[B_keys])
                S.dma("pool", lambda e, a=a: e.dma_start(out=ckvT[:, :, 2048:2112], in_=ckvTss[:, :, a * 64:(a + 1) * 64]),
                      reads=[B_lats], writes=[B_keys])
                S.dma("pool", lambda e, a=a: e.dma_start(out=krT[:64, 2048:2112], in_=krTss[:, a * 64:(a + 1) * 64]),
                      reads=[B_lats], writes=[B_keys])
            nq_tok = ntok if a is None else 64
            q0 = 0 if a is None else a * 64
            if smp:
                S.dma("pool", lambda e, q0=q0: e.dma_start(out=rtab[:64, 0, 0:64], in_=coss_d[:, q0:q0 + 64]), writes=[B_rtab])
                S.dma("pool", lambda e, q0=q0: e.dma_start(out=rtab[:64, 1, 0:64], in_=sins_d[:, q0:q0 + 64]), writes=[B_rtab])
            else:
                S.dma("pool", lambda e: e.dma_start(out=rtab[:64, 0, :], in_=cosq_d[:, i * 512:(i + 1) * 512]), writes=[B_rtab])
                S.dma("pool", lambda e: e.dma_start(out=rtab[:64, 1, :], in_=sinq_d[:, i * 512:(i + 1) * 512]), writes=[B_rtab])
            nkt = (nk + 127) // 128
            def head_body(h):
                j = h % 2
                if h % 2 == 0:
                    wvv, bwv = wload(Wuv, h // 2, 0)
                    for g0 in range(0, nkt, 2):
                        pb, bpb = nextpb()
                        gcnt = min(2, nkt - g0)
                        ops = [(pb[:min(128, nk - 128 * kt), (kt - g0) * 256:(kt - g0 + 1) * 256],
                                ckvT[:, c, kt * 128:kt * 128 + min(128, nk - 128 * kt)], wvv[:, c, :], c == 0, c == 3)
                               for kt in range(g0, g0 + gcnt) for c in range(4)]
                        def fn(e, ops=ops):
                            ins = None
                            for o, l, r, a0, a1 in ops:
                                ins = e.matmul(o, lhsT=l, rhs=r, start=a0, stop=a1)
                            return ins
                        S.op("pe", fn, reads=[B_keys, bwv], writes=[bpb])
                        for kt in range(g0, g0 + gcnt):
                            kn = min(128, nk - 128 * kt)
                            evac(v2[:kn, kt, :], pb[:kn, (kt - g0) * 256:(kt - g0 + 1) * 256], reads=[bpb], writes=[B_v2])
                wq, bw = wload(Wuq[h], 0, 0)
                pb, bpb = nextpb()
                mm(pb[:, 0:nq_tok], [(wq[:, c, 0:128], cq[:, c, q0:q0 + nq_tok]) for c in range(4)], reads=[bw, B_cq], writes=[bpb])
                evac(qn[:, j, 0:nq_tok], pb[:, 0:nq_tok], reads=[bpb], writes=[B_qh[j]], scale=SC_B)
                pr, bpr = nextpb()
                mm(pr[:64, 0:nq_tok], [(wq[:, c, 128:192], cq[:, c, q0:q0 + nq_tok]) for c in range(4)], reads=[bw, B_cq], writes=[bpr])
                pw, bpw = nextpb()
                mm(pw[:64, 0:nq_tok], [(wq[:, c, 192:256], cq[:, c, q0:q0 + nq_tok]) for c in range(4)], reads=[bw, B_cq], writes=[bpw])
                S.op("dve", lambda e, pr=pr: e.tensor_tensor(out=rtmp[:64, 0, 0:nq_tok], in0=pr[:64, 0:nq_tok],
                                                             in1=rtab[:64, 0, 0:nq_tok], op=ALU.mult), reads=[bpr, B_rtab], writes=[B_rtmp])
                S.op("dve", lambda e, pw=pw: e.tensor_tensor(out=rtmp[:64, 1, 0:nq_tok], in0=pw[:64, 0:nq_tok],
                                                             in1=rtab[:64, 1, 0:nq_tok], op=ALU.mult), reads=[bpw, B_rtab], writes=[B_rtmp])
                S.op("pool", lambda e: e.tensor_tensor(out=rtmp[:64, 0, 0:nq_tok], in0=rtmp[:64, 0, 0:nq_tok],
                                                       in1=rtmp[:64, 1, 0:nq_tok], op=ALU.add), reads=[B_rtmp], writes=[B_rtmp])
                S.op("pool", lambda e, j=j: e.tensor_scalar(out=qr[:64, j, 0:nq_tok], in0=rtmp[:64, 0, 0:nq_tok], scalar1=SC_B,
                                                            scalar2=None, op0=ALU.mult), reads=[B_rtmp], writes=[B_qh[j]])
                wk, bwk = wload(Wuk, h, 0)
                for k0 in range(0, nk, 512):
                    kn = min(512, nk - k0)
                    pb, bpb = nextpb()
                    mm(pb[:, 0:kn], [(wk[:, c, :], ckvT[:, c, k0:k0 + kn]) for c in range(4)], reads=[bwk, B_keys], writes=[bpb])
                    evac(knT[:, k0:k0 + kn], pb[:, 0:kn], reads=[bpb], writes=[B_kn])
                def unit_body(s):
                    if a is None:
                        jq = 4 * i + s
                        nk_u = 128 * (17 + jq)
                        nq, qc0, srow = 128, s * 128, s
                    else:
                        jq, nk_u, nq, qc0, srow = None, 2112, 64, 0, a

                    def terms(k0, kn, j=j, qc0=qc0, nq=nq, jq=jq):
                        return [(qn[:, j, qc0:qc0 + nq], knT[:, k0:k0 + kn]),
                                (qr[:, j, qc0:qc0 + nq], krT[:, k0:k0 + kn])]

                    def vt(kt, kn, h=h):
                        return v2[:kn, kt, (h % 2) * 128:(h % 2 + 1) * 128]
                    return attn_unit(nq, nk_u, terms, [B_qh[j], B_kn, B_keys, B_mask], vt, [B_v2],
                                     otk[:nq, srow, h * 128:(h + 1) * 128], [B_otk], set1)
                run_units([(lambda s_=s_: unit_body(s_)) for s_ in range(nsub if a is None else 1)])
            for h_ in range(16):
                head_body(h_)
            for s in range(nsub if a is None else 1):
                srow = s if a is None else a
                nq = 128 if a is None else 64
                c0q = s * 128 if a is None else a * 64
                for c0 in range(0, 16, 4):
                    pb, bpb = nextpb()
                    pbb = pb[:].bitcast(BF16)
                    def fn(e, c0=c0, pbb=pbb, srow=srow, nq=nq):
                        ins = None
                        for c in range(c0, c0 + 4):
                            ins = e.transpose(out=pbb[:, (c - c0) * 128:(c - c0) * 128 + nq],
                                              in_=otk[:nq, srow, c * 128:(c + 1) * 128], identity=ident_b[:nq, :nq])
                        return ins
                    S.op("pe", fn, reads=[B_otk, B_c], writes=[bpb])
                    evac(ovT[:, c0:c0 + 4, c0q:c0q + nq],
                         pbb[:, 0:512].rearrange("p (c q) -> p c q", q=128)[:, :, 0:nq], reads=[bpb], writes=[B_ov])
        for a_ in ([None] if not smp else [0, 1]):
            seq_body(a_)
        S.barrier()
        bx, bh, by = Buf(), Buf(), Buf()
        load_x1(bx, by)
        by = Buf()
        def sinkbo(m, ps, bps):
            evac(yT[:, m, 0:ntok], ps, reads=[bps], writes=[by])
        lin4(Wbo, lambda k: ovT[:, k, 0:ntok], ntok, sinkbo, [B_ov, bx])
        rms(lambda c: yT[:, c, 0:ntok], 16, ntok, D, G_MPOST + 16, resid=lambda c: xT[:, c, 0:ntok], bsrc=[by], bdst=[bx])
        S.barrier()
        by = Buf()
        ffn(1, xT, bx, hT, bh, yT, by, hid, ntok)
        S.barrier()
        bstg = [Buf(), Buf()]
        stgy = [v32(24 * K1 + 4096 * j, 2048) for j in range(2)]
        if smp:
            store_tok(lambda c: xT[:, c, :], [bx], 16, 128, ntok, lambda s: y_s[:, :], stgy, bstg)
        else:
            store_tok(lambda c: xT[:, c, :], [bx], 16, 128, ntok,
                      lambda s, i=i: y_p[i * 512 + s * 128:i * 512 + (s + 1) * 128, :], stgy, bstg)

    for i_ in (range(l1_tiles) if isinstance(l1_tiles, int) else l1_tiles):
        l1_tile(i_)
    S.emit(st)
    st.close()
    return nc


def _pos_tables(pos):
    inv = 1.0 / (10000.0 ** (np.arange(32, dtype=np.float32) / 32.0))
    ang = pos.astype(np.float32)[None, :] * inv[:, None].astype(np.float32)
    c = np.cos(ang).astype(np.float32); s = np.sin(ang).astype(np.float32)
    return np.concatenate([c, c], 0), np.concatenate([-s, s], 0)


def _consts(half):
    bf = ml_dtypes.bfloat16
    ident = np.eye(128, dtype=np.float32)
    cb = np.concatenate([ident, np.ones((128, 128), np.float32)], 1).astype(bf)
    kindb = np.zeros((10, 640), np.float32)
    for c in range(10):
        kindb[c, c * 64:(c + 1) * 64] = 1.0
    qmb = np.zeros((10, 5 * 128), np.float32)
    for var in range(5):
        for r in range(128):
            qc = r // 64
            for c in range(10):
                masked = (c > 8 + qc) or (c < qc)
                if var < 4 and c < 8 - 2 * var:
                    masked = True
                qmb[c, var * 128 + r] = NEG if masked else 0.0
    kind = np.zeros((64, 4096), np.float32)
    for c in range(64):
        kind[c, c * 64:(c + 1) * 64] = 1.0
    qm = np.zeros((64, 2048), np.float32)
    for r in range(2048):
        qc = (half * 2048 + r) // 64
        qm[qc + 1:, r] = NEG
    sel = np.zeros((128, 2), np.float32)
    sel[:, half] = 1.0
    cosk, sink = _pos_tables(np.arange(4096))
    cosq, sinq = _pos_tables(half * 2048 + np.arange(2048))
    p = 2048 + np.arange(64)
    coss, sins = _pos_tables(np.concatenate([p, p]))
    return dict(ident_f=ident, cb=cb, kindb=kindb.astype(bf), qmb=qmb.astype(bf), kind=kind.astype(bf),
                qm=qm.astype(bf), sel=sel, cosk=cosk, sink=sink, cosq=cosq, sinq=sinq, coss=coss, sins=sins)


def _colmajor(g):
    return np.ascontiguousarray(g.reshape(-1, 128).T)


_NC_CACHE = {}


def prep(inp):
    f = lambda a: np.ascontiguousarray(np.asarray(a, dtype=np.float32))
    x_prompt, x_sample = f(inp["x_prompt"]), f(inp["x_sample"])
    gains = np.concatenate(
        [_colmajor(f(inp[n])[l]) for n in ("ln_mix_pre", "ln_mix_post", "ln_ffn_pre", "ln_ffn_post") for l in range(2)]
        + [_colmajor(f(inp["mla_q_norm"])[0]), _colmajor(f(inp["mla_kv_norm"])[0])], axis=1)
    rb = f(inp["a_rel_bias"])[0]
    r = np.arange(128)[:, None]; w = np.arange(640)[None, :]
    relT = np.ascontiguousarray(rb[:, np.clip(512 + r - w, -256, 256) + 256])
    wuq = f(inp["mla_w_uq"])[0]
    w_uq = np.ascontiguousarray(np.concatenate([wuq, wuq[:, :, 160:192], wuq[:, :, 128:160]], axis=2))
    wdkv = f(inp["mla_w_dkv"])[0]
    w_dkv = np.ascontiguousarray(np.concatenate([wdkv, wdkv[:, 544:576], wdkv[:, 512:544]], axis=1))
    shared = dict(
        gains=np.ascontiguousarray(gains), w_qkv=f(inp["a_w_qkv"])[0], w_ao=f(inp["a_w_o"])[0], relT=relT,
        w_dq=f(inp["mla_w_dq"])[0], w_uq=w_uq, w_dkv=w_dkv,
        w_uk=f(inp["mla_w_uk"])[0].reshape(512, 2048), w_uv=f(inp["mla_w_uv"])[0].reshape(512, 2048),
        w_bo=f(inp["mla_w_o"])[0], w1=f(inp["ffn_w1"]), w2=f(inp["ffn_w2"]))
    cak, cav = f(inp["cache_a_k"])[0].reshape(16, 512, 2048), f(inp["cache_a_v"])[0].reshape(16, 512, 2048)
    cckv, ckr = f(inp["cache_mla_ckv"])[0], f(inp["cache_mla_kr"])[0]
    consts = [_consts(0), _consts(1)]
    in_maps = []
    for c in range(N_CORES):
        b, half = c // 2, c % 2
        m = dict(shared)
        m.update(consts[half])
        m.update(xp=x_prompt[b], xs=x_sample[2 * c:2 * c + 2].reshape(128, 2048),
                 cak=cak[2 * c:2 * c + 2], cav=cav[2 * c:2 * c + 2],
                 cckv=cckv[2 * c:2 * c + 2], ckr=ckr[2 * c:2 * c + 2])
        in_maps.append(m)
    return in_maps


def post(res):
    R = lambda c, n: np.asarray(res[c][n], dtype=np.float32)
    y_prompt = np.stack([np.concatenate([R(2 * b, "y_p"), R(2 * b + 1, "y_p")], 0) for b in range(4)], 0)
    y_sample = np.concatenate([R(c, "y_s").reshape(2, 64, 2048) for c in range(8)], 0)
    nakp = np.stack([R(2 * b + 1, "nak_p") for b in range(4)], 0).reshape(1, 4, 512, 16, 128)
    navp = np.stack([R(2 * b + 1, "nav_p") for b in range(4)], 0).reshape(1, 4, 512, 16, 128)
    naks = np.concatenate([R(c, "nak_s").reshape(2, 64, 16, 128) for c in range(8)], 0)[None]
    navs = np.concatenate([R(c, "nav_s").reshape(2, 64, 16, 128) for c in range(8)], 0)[None]
    ckvp = np.stack([np.concatenate([R(2 * b, "nckv_p")[:2048], R(2 * b + 1, "nckv_p")[2048:]], 0) for b in range(4)], 0)[None]
    krp = np.stack([np.concatenate([R(2 * b, "nkr_p")[:2048], R(2 * b + 1, "nkr_p")[2048:]], 0) for b in range(4)], 0)[None]
    ckvs = np.concatenate([R(c, "nckv_s").reshape(2, 64, 512) for c in range(8)], 0)[None]
    krs = np.concatenate([R(c, "nkr_s").reshape(2, 64, 64) for c in range(8)], 0)[None]
    return (y_prompt, y_sample, nakp, navp, naks, navs, ckvp, krp, ckvs, krs)


def kernel(**inp):
    in_maps = prep(inp)
    if "nc" not in _NC_CACHE:
        _NC_CACHE["nc"] = build_nc()
    res = run_bass_kernel_spmd(_NC_CACHE["nc"], in_maps, core_ids=list(range(N_CORES))).results
    return post(res)
```

```python
import contextlib
import os
import numpy as np
import ml_dtypes
import concourse.bass as bass
import concourse.mybir as mybir
from concourse.bass_utils import run_bass_kernel_spmd

F32 = mybir.dt.float32
BF16 = mybir.dt.bfloat16
AF = mybir.ActivationFunctionType
ALU = mybir.AluOpType
AX = mybir.AxisListType

N_CORES = 8
D = 2048
SEQ = 4096
HALF = 2048
TT = 512
NEG = -1.0e30
EPS = 1e-6
SC_A = 128 ** -0.5
SC_B = 192 ** -0.5


class Buf:
    __slots__ = ("name", "w", "r")

    def __init__(self, name=""):
        self.name = name
        self.w = None
        self.r = {}


class Sched:
    NSLOT = 8
    ENGS = ("pe", "act", "dve", "pool", "sp")

    def __init__(self, nc):
        self.nc = nc
        self.prog = {k: [] for k in self.ENGS}
        self.cnt = {k: 0 for k in self.ENGS}
        self.dcnt = {"sp": 0, "pool": 0}
        self.waited = {k: {} for k in self.ENGS}
        self.sems = {}

    def _need(self, eng, tok, waits):
        if tok is None:
            return
        key, val = tok
        if key == ("eng", "pe") and eng == "pe":
            return
        if self.waited[eng].get(key, 0) >= val:
            return
        self.waited[eng][key] = val
        waits.append((key, val))

    def _deps(self, eng, reads, writes, waits):
        for b in reads:
            self._need(eng, b.w, waits)
        for b in writes:
            self._need(eng, b.w, waits)
            for key, val in b.r.items():
                self._need(eng, (key, val), waits)

    def _commit(self, tok, reads, writes):
        key, val = tok
        for b in reads:
            if b.r.get(key, 0) < val:
                b.r[key] = val
        for b in writes:
            b.w = tok
            b.r = {}

    def op(self, eng, fn, reads=(), writes=()):
        waits = []
        self._deps(eng, reads, writes, waits)
        self.cnt[eng] += 1
        tok = (("eng", eng), self.cnt[eng])
        self.prog[eng].append((waits, fn, ("eng", eng), 1))
        self._commit(tok, reads, writes)
        return tok

    def dma(self, q, fn, reads=(), writes=()):
        waits = []
        i = self.dcnt[q]
        self.dcnt[q] += 1
        slot = i % self.NSLOT
        key = ("dma", q, slot)
        if i >= self.NSLOT:
            self._need(q, (key, 16 * (i // self.NSLOT)), waits)
        self._deps(q, reads, writes, waits)
        tok = (key, 16 * (i // self.NSLOT + 1))
        self.prog[q].append((waits, fn, key, 16))
        self._commit(tok, reads, writes)
        return tok

    def _all_tokens(self, queues):
        toks = [(("eng", e), c) for e, c in self.cnt.items() if c > 0 and e != "sp"]
        for q in queues:
            n = self.dcnt[q]
            for s in range(min(n, self.NSLOT)):
                toks.append((("dma", q, s), 16 * ((n - s + self.NSLOT - 1) // self.NSLOT)))
        return toks

    def barrier(self, final=False):
        toks = self._all_tokens(("sp", "pool") if final else ("pool",))
        for e in (self.ENGS if final else ("pe", "act", "dve", "pool")):
            waits = []
            for t in toks:
                if t[0] == ("eng", e):
                    continue
                if self.waited[e].get(t[0], 0) >= t[1]:
                    continue
                self.waited[e][t[0]] = t[1]
                waits.append(t)
            if waits:
                self.prog[e].append((waits, None, None, 0))

    def emit(self, st):
        nc = self.nc
        keys = [("eng", e) for e in self.ENGS]
        for q in self.dcnt:
            keys += [("dma", q, s) for s in range(self.NSLOT)]
        for k in keys:
            self.sems[k] = st.enter_context(nc.semaphore("s_" + "_".join(map(str, k))))
        self.barrier(final=True)
        block = st.enter_context(nc.Block())

        def replay(name):
            def run(e):
                for waits, fn, key, inc in self.prog[name]:
                    for k, v in waits:
                        e.wait_ge(self.sems[k], v)
                    if fn is not None:
                        fn(e).then_inc(self.sems[key], inc)
            return run

        block.tensor(replay("pe"))
        block.scalar(replay("act"))
        block.vector(replay("dve"))
        block.gpsimd(replay("pool"))
        block.sync(replay("sp"))


def build_nc(l0_tiles=9, l1_tiles=5):
    nc = bass.Bass("TRN2", target_bir_lowering=False)
    st = contextlib.ExitStack()
    S = Sched(nc)

    def din(name, shape, dt=F32):
        return nc.dram_tensor(name, list(shape), dt, kind="ExternalInput").ap()

    def dout(name, shape):
        return nc.dram_tensor(name, list(shape), F32, kind="ExternalOutput").ap()

    def dscr(name, shape, dt):
        return nc.dram_tensor(name, list(shape), dt).ap()

    xp = din("xp", [SEQ, D]); xs = din("xs", [128, D])
    cak = din("cak", [2, 512, D]); cav = din("cav", [2, 512, D])
    cckv = din("cckv", [2, 2048, 512]); ckr = din("ckr", [2, 2048, 64])
    gains = din("gains", [128, 136])
    w_qkv = din("w_qkv", [D, 6144]); w_ao = din("w_ao", [D, D])
    relT = din("relT", [16, 128, 640])
    w_dq = din("w_dq", [D, 512]); w_uq = din("w_uq", [512, 16, 256])
    w_dkv = din("w_dkv", [D, 640]); w_uk = din("w_uk", [512, D]); w_uv = din("w_uv", [512, D])
    w_bo = din("w_bo", [D, D])
    w1 = din("w1", [2, D, 4 * D]); w2 = din("w2", [2, 4 * D, D])
    ident_f_d = din("ident_f", [128, 128]); cb_d = din("cb", [128, 256], BF16)
    kindb_d = din("kindb", [10, 640], BF16); qmb_d = din("qmb", [10, 640], BF16)
    kind_d = din("kind", [64, 4096], BF16); qm_d = din("qm", [64, 2048], BF16)
    sel_d = din("sel", [128, 2])
    cosk_d = din("cosk", [64, SEQ]); sink_d = din("sink", [64, SEQ])
    cosq_d = din("cosq", [64, HALF]); sinq_d = din("sinq", [64, HALF])
    coss_d = din("coss", [64, 128]); sins_d = din("sins", [64, 128])
    y_p = dout("y_p", [HALF, D]); y_s = dout("y_s", [128, D])
    nak_p = dout("nak_p", [512, D]); nav_p = dout("nav_p", [512, D])
    nak_s = dout("nak_s", [128, D]); nav_s = dout("nav_s", [128, D])
    nckv_p = dout("nckv_p", [SEQ, 512]); nkr_p = dout("nkr_p", [SEQ, 64])
    nckv_s = dout("nckv_s", [128, 512]); nkr_s = dout("nkr_s", [128, 64])

    class WMat:
        def __init__(self, name, src, Kd, Nd, gw=512, kper=8):
            self.src, self.gw, self.kper = src, gw, min(kper, Kd // 128)
            self.ng, self.nkp = Nd // gw, (Kd // 128) // self.kper
            self.d = dscr(name, [self.ng * self.nkp, 128, self.kper, gw], BF16)
            self.b = [Buf() for _ in range(self.ng * self.nkp)]
            self.done = False

        def cast(self):
            if self.done:
                return
            self.done = True
            for g in range(self.ng):
                for kp in range(self.nkp):
                    i = g * self.nkp + kp
                    r0 = kp * self.kper * 128
                    src = self.src[r0:r0 + self.kper * 128, g * self.gw:(g + 1) * self.gw].rearrange("(k p) c -> p k c", p=128)
                    dst = self.d[i]
                    S.dma("pool", lambda e, dst=dst, src=src: e.dma_start(out=dst, in_=src), writes=[self.b[i]])

        def piece(self, g, kp):
            self.cast()
            i = g * self.nkp + kp
            return self.d[i], self.b[i]

    Wqkv = WMat("Wqkv", w_qkv, D, 6144)
    Wao = WMat("Wao", w_ao, D, D)
    W1m = [WMat("W1_%d" % l, w1[l], D, 4 * D) for l in range(2)]
    W2m = [WMat("W2_%d" % l, w2[l], 4 * D, D) for l in range(2)]
    Wdq = WMat("Wdq", w_dq, D, 512)
    Wdkv_c = WMat("Wdkv_c", w_dkv[:, 0:512], D, 512)
    Wdkv_r = WMat("Wdkv_r", w_dkv[:, 512:640], D, 128, gw=128)
    Wuq = [WMat("Wuq%d" % h, w_uq[:, h, :], 512, 256, gw=256, kper=4) for h in range(16)]
    Wuk = WMat("Wuk", w_uk, 512, D, gw=128, kper=4)
    Wuv = WMat("Wuv", w_uv, 512, D, gw=256, kper=4)
    Wbo = WMat("Wbo", w_bo, D, D)
    relB = dscr("relB", [16, 128, 640], BF16); B_relB = Buf()
    KTs = dscr("KTs", [128, 16, SEQ], BF16); Vs = dscr("Vs", [SEQ, D], BF16)
    B_kv = [Buf() for _ in range(8)]
    x1s = dscr("x1s", [128, 16, SEQ], F32); B_x1 = [Buf() for _ in range(8)]
    x1ss = dscr("x1ss", [128, 16, 128], F32); B_x1s = Buf()
    ckvTs = dscr("ckvTs", [128, 4, SEQ], BF16); krTs = dscr("krTs", [64, SEQ], BF16)
    B_lat = [Buf() for _ in range(8)]
    ckvTss = dscr("ckvTss", [128, 4, 128], BF16); krTss = dscr("krTss", [64, 128], BF16); B_lats = Buf()

    K1 = 1024
    NA = 80 * K1
    AR = st.enter_context(nc.sbuf_tensor("arena", [128, NA], BF16))

    def v16(off, n):
        return AR[:, off:off + n]

    def v32(off, n):
        return AR[:, off:off + 2 * n].bitcast(F32)

    NWS = 3
    WS = [st.enter_context(nc.sbuf_tensor("ws%d" % i, [128, 4096], BF16)) for i in range(NWS)]
    B_ws = [Buf() for _ in range(NWS)]
    wcount = [0]
    cst = st.enter_context(nc.sbuf_tensor("sb_cst", [128, 256], BF16)); B_c = Buf()
    idf = st.enter_context(nc.sbuf_tensor("sb_idf", [128, 128], F32))
    gn = st.enter_context(nc.sbuf_tensor("sb_gn", [128, 136], F32))
    sel = st.enter_context(nc.sbuf_tensor("sb_sel", [128, 2], F32))
    mk = st.enter_context(nc.sbuf_tensor("sb_mk", [16, 1280], BF16))
    ident_b = cst[:, 0:128]; ones_b = cst[:, 128:256]
    sm = st.enter_context(nc.sbuf_tensor("sb_sm", [128, 64], F32)); B_sm = [Buf() for _ in range(8)]; B_rs = [Buf() for _ in range(8)]
    rstd = st.enter_context(nc.sbuf_tensor("sb_rstd", [128, 512], F32)); B_rstd = Buf()
    sqt = [st.enter_context(nc.sbuf_tensor("sqt%d" % i, [128, 512], BF16)) for i in range(2)]
    B_sq = [Buf(), Buf()]
    NTF = 2
    tmpf = [st.enter_context(nc.sbuf_tensor("tmpf%d" % i, [128, 512], F32)) for i in range(NTF)]
    B_tf = [Buf() for _ in range(NTF)]
    PA = [st.enter_context(nc.psum_tensor("pa%d" % i, [128, 1024], F32)) for i in range(2)]
    B_PA = [Buf(), Buf()]
    PB = [st.enter_context(nc.psum_tensor("pb%d" % i, [128, 512], F32)) for i in range(4)]
    B_PB = [Buf() for _ in range(4)]
    rr = {"pb": 0, "ev": 0, "sq": 0, "tf": 0, "sm": 0, "u": 0, "g": 0}

    S.dma("pool", lambda e: e.dma_start(out=cst[:], in_=cb_d), writes=[B_c])
    S.dma("pool", lambda e: e.dma_start(out=idf[:], in_=ident_f_d), writes=[B_c])
    S.dma("pool", lambda e: e.dma_start(out=gn[:], in_=gains), writes=[B_c])
    S.dma("pool", lambda e: e.dma_start(out=sel[:], in_=sel_d), writes=[B_c])
    S.dma("pool", lambda e: e.dma_start(out=mk[0:10, 0:640], in_=kindb_d), writes=[B_c])
    S.dma("pool", lambda e: e.dma_start(out=mk[0:10, 640:1280], in_=qmb_d), writes=[B_c])
    for h in range(16):
        S.dma("pool", lambda e, h=h: e.dma_start(out=relB[h], in_=relT[h]), writes=[B_relB])

    def wload(wm, g, kp):
        pap, pbuf = wm.piece(g, kp)
        i = wcount[0] % NWS
        wcount[0] += 1
        kc, gw = pap.shape[1], pap.shape[2]
        dst = WS[i][:, 0:kc * gw].rearrange("p (k c) -> p k c", c=gw)
        S.dma("sp", lambda e: e.dma_start(out=dst, in_=pap), reads=[pbuf], writes=[B_ws[i]])
        return dst, B_ws[i]

    def nextpb():
        i = rr["pb"] % 4
        rr["pb"] += 1
        return PB[i], B_PB[i]

    def accs4(n):
        g = rr["g"] % 2
        rr["g"] += 1
        if g == 0:
            return [(PB[m], B_PB[m]) for m in range(4)]
        return [(PA[m // 2][:, (m % 2) * 512:(m % 2 + 1) * 512], B_PA[m // 2]) for m in range(4)]

    def ubufs(l):
        return list({id(b): b for b in l}.values())

    def mm(out_ap, pairs, reads, writes):
        def fn(e):
            n = len(pairs)
            ins = None
            for i, (l, r) in enumerate(pairs):
                ins = e.matmul(out_ap, lhsT=l, rhs=r, start=(i == 0), stop=(i == n - 1))
            return ins
        S.op("pe", fn, reads, writes)

    def evac(out_ap, in_ap, reads, writes, scale=None, eng=None):
        if eng is None:
            eng = ("act", "dve")[rr["ev"] % 2]
            rr["ev"] += 1
        if eng == "act":
            if scale is None:
                S.op("act", lambda e: e.activation(out=out_ap, in_=in_ap, func=AF.Copy), reads, writes)
            else:
                S.op("act", lambda e: e.activation(out=out_ap, in_=in_ap, func=AF.Copy, scale=scale), reads, writes)
        else:
            if scale is None:
                S.op("dve", lambda e: e.tensor_copy(out=out_ap, in_=in_ap), reads, writes)
            else:
                S.op("dve", lambda e: e.tensor_scalar(out=out_ap, in0=in_ap, scalar1=scale, scalar2=None,
                                                     op0=ALU.mult), reads, writes)

    def rms(src, nch, n, dn, gcol0, dst=None, resid=None, bsrc=(), bdst=()):
        pb, bpb = nextpb()
        for c in range(nch):
            i = rr["sq"] % 2
            rr["sq"] += 1
            sq = sqt[i][:, 0:n]
            a = src(c)
            S.op("act", lambda e, sq=sq, a=a: e.activation(out=sq, in_=a, func=AF.Square),
                 reads=list(bsrc), writes=[B_sq[i]])
            def fn(e, sq=sq, c=c):
                return e.matmul(pb[:, 0:n], lhsT=ones_b, rhs=sq, start=(c == 0), stop=(c == nch - 1))
            S.op("pe", fn, reads=[B_sq[i], B_c], writes=[bpb])
        S.op("act", lambda e: e.activation(out=rstd[:, 0:n], in_=pb[:, 0:n], func=AF.Ln, scale=1.0 / dn, bias=EPS),
             reads=[bpb], writes=[B_rstd])
        S.op("act", lambda e: e.activation(out=rstd[:, 0:n], in_=rstd[:, 0:n], func=AF.Exp, scale=-0.5),
             reads=[B_rstd], writes=[B_rstd])
        for c in range(nch):
            a = src(c)
            g = gn[:, gcol0 + c:gcol0 + c + 1]
            if resid is None:
                o = dst(c)
                S.op("dve", lambda e, o=o, a=a, g=g: e.scalar_tensor_tensor(
                    out=o, in0=a, scalar=g, in1=rstd[:, 0:n], op0=ALU.mult, op1=ALU.mult),
                    reads=list(bsrc) + [B_rstd, B_c], writes=list(bdst))
            else:
                i = rr["tf"] % NTF
                rr["tf"] += 1
                t = tmpf[i][:, 0:n]
                x = resid(c)
                S.op("dve", lambda e, t=t, a=a, g=g: e.scalar_tensor_tensor(
                    out=t, in0=a, scalar=g, in1=rstd[:, 0:n], op0=ALU.mult, op1=ALU.mult),
                    reads=list(bsrc) + [B_rstd, B_c], writes=[B_tf[i]])
                S.op("pool", lambda e, x=x, t=t: e.tensor_tensor(out=x, in0=x, in1=t, op=ALU.add),
                     reads=[B_tf[i]], writes=list(bdst))

    def lin4(wm, rhs, n, sink, bsrc, groups=None, outw=None):
        outw = outw or (wm.gw // 128)
        for g in (groups if groups is not None else range(wm.ng)):
            accs = accs4(n)
            for kp in range(wm.nkp):
                wv, wb = wload(wm, g, kp)
                ops = [(accs[m][0][:, 0:n], wv[:, k, m * 128:(m + 1) * 128], rhs(kp * wm.kper + k),
                        (kp == 0 and k == 0), (kp == wm.nkp - 1 and k == wm.kper - 1))
                       for k in range(wm.kper) for m in range(outw)]
                def fn(e, ops=ops):
                    ins = None
                    for o, l, r, a0, a1 in ops:
                        ins = e.matmul(o, lhsT=l, rhs=r, start=a0, stop=a1)
                    return ins
                S.op("pe", fn, reads=list(bsrc) + [wb], writes=ubufs([a[1] for a in accs[:outw]]))
            for m in range(outw):
                sink(g * outw + m, accs[m][0][:, 0:n], accs[m][1])

    def lin_tok(wm, lhs, nsub, mrows, sink, bsrc, groups):
        for g in groups:
            accs = accs4(512)
            for kp in range(wm.nkp):
                wv, wb = wload(wm, g, kp)
                ops = [(accs[s][0][:mrows, 0:wm.gw], lhs(kp * wm.kper + k, s), wv[:, k, :],
                        (kp == 0 and k == 0), (kp == wm.nkp - 1 and k == wm.kper - 1))
                       for k in range(wm.kper) for s in range(nsub)]
                def fn(e, ops=ops):
                    ins = None
                    for o, l, r, a0, a1 in ops:
                        ins = e.matmul(o, lhsT=l, rhs=r, start=a0, stop=a1)
                    return ins
                S.op("pe", fn, reads=list(bsrc) + [wb], writes=ubufs([a[1] for a in accs[:nsub]]))
            for s in range(nsub):
                sink(g, s, accs[s][0][:mrows, 0:wm.gw], accs[s][1])

    def attn_unit(nq, nk, terms, tbufs, vt, vbufs, o_dst, o_bufs, bufset):
        Sb, B_S, Pbs, B_Ps, PT, B_PT = bufset
        u = rr["u"] % 2
        rr["u"] += 1
        Pb, B_P = Pbs[u], B_Ps[u]
        if Sb is None:
            pa, bpa = PA[u], B_PA[u]
            for k0 in range(0, nk, 512):
                kn = min(512, nk - k0)
                mm(pa[:nq, k0:k0 + kn], terms(k0, kn), reads=list(tbufs), writes=[bpa])
            ssrc, bs = pa[:nq, 0:nk], bpa
        else:
            for s0 in range(0, nk, 1024):
                sn = min(1024, nk - s0)
                j = (s0 // 1024) % 2
                for k0 in range(s0, s0 + sn, 512):
                    kn = min(512, nk - k0)
                    mm(PA[j][:nq, k0 - s0:k0 - s0 + kn], terms(k0, kn), reads=list(tbufs), writes=[B_PA[j]])
                sbi = s0 // 1024
                evac(Sb[:nq, s0:s0 + sn], PA[j][:nq, 0:sn], reads=[B_PA[j]], writes=[B_S[sbi]], eng="act")
            ssrc, bs = Sb[:nq, 0:nk], None
        i = rr["sm"] % 8
        rr["sm"] += 1
        mx = sm[:nq, 4 * i:4 * i + 1]; nm = sm[:nq, 4 * i + 1:4 * i + 2]
        rs = sm[:nq, 4 * i + 2:4 * i + 3]; ri = sm[:nq, 4 * i + 3:4 * i + 4]
        bsm, brs = B_sm[i], B_rs[i]
        S.op("pool", lambda e: e.memset(rs, 0.0), reads=[], writes=[brs])
        if Sb is None:
            S.op("dve", lambda e: e.reduce_max(out=mx, in_=ssrc, axis=AX.X), reads=[bs], writes=[bsm])
            sread = [bs]
        else:
            nsb = (nk + 1023) // 1024
            pm = sm[:nq, 32 + 4 * i:32 + 4 * i + nsb]
            for sbi in range(nsb):
                s0 = sbi * 1024
                sn = min(1024, nk - s0)
                S.op("dve", lambda e, sbi=sbi, s0=s0, sn=sn: e.reduce_max(
                    out=sm[:nq, 32 + 4 * i + sbi:32 + 4 * i + sbi + 1], in_=Sb[:nq, s0:s0 + sn], axis=AX.X),
                    reads=[B_S[sbi]], writes=[bsm])
            S.op("dve", lambda e: e.reduce_max(out=mx, in_=pm, axis=AX.X), reads=[bsm], writes=[bsm])
            sread = [B_S[k] for k in range(nsb)]
        S.op("dve", lambda e: e.tensor_scalar(out=nm, in0=mx, scalar1=-1.0, scalar2=None, op0=ALU.mult),
             reads=[bsm], writes=[bsm])
        S.op("act", lambda e: e.activation(out=Pb[:nq, 0:nk], in_=ssrc, func=AF.Exp, bias=nm, scale=1.0,
                                            accum_out=rs), reads=sread + [bsm], writes=[B_P, brs])
        nkt = (nk + 127) // 128

        def stageB():
            for g0 in range(0, nkt, 4):
                pb, bpb = nextpb()
                pbb = pb[:].bitcast(BF16)
                gn_ = min(4, nkt - g0)
                tops = [(pbb[:min(128, nk - 128 * t), (t - g0) * 128:(t - g0) * 128 + nq],
                         Pb[:nq, 128 * t:128 * t + min(128, nk - 128 * t)]) for t in range(g0, g0 + gn_)]
                def fn(e, tops=tops):
                    ins = None
                    for o, a_ in tops:
                        ins = e.transpose(out=o, in_=a_, identity=ident_b[:nq, :nq])
                    return ins
                S.op("pe", fn, reads=[B_P, B_c], writes=[bpb])
                full = (nk - 128 * (g0 + gn_ - 1)) >= 128
                gi = (g0 // 4) % 8
                if full:
                    evac(PT[:, g0:g0 + gn_, 0:nq], pbb[:, 0:gn_ * 128].rearrange("p (t q) -> p t q", q=128)[:, :, 0:nq],
                         reads=[bpb], writes=[B_PT[gi]], eng="dve")
                else:
                    for t in range(g0, g0 + gn_):
                        kn = min(128, nk - 128 * t)
                        evac(PT[:kn, t, 0:nq], pbb[:kn, (t - g0) * 128:(t - g0) * 128 + nq],
                             reads=[bpb], writes=[B_PT[gi]], eng="dve")
            S.op("dve", lambda e: e.reciprocal(out=ri, in_=rs), reads=[brs], writes=[brs])
            pv, bpv = nextpb()
            pvops = [(PT[:min(128, nk - 128 * t), t, 0:nq], vt(t, min(128, nk - 128 * t))) for t in range(nkt)]
            def fnpv(e):
                ins = None
                for t, (l, r) in enumerate(pvops):
                    ins = e.matmul(pv[:nq, 0:128], lhsT=l, rhs=r, start=(t == 0), stop=(t == nkt - 1))
                return ins
            S.op("pe", fnpv, reads=ubufs([B_PT[(g // 4) % 8] for g in range(0, nkt, 4)] + list(vbufs)), writes=[bpv])
            S.op("dve", lambda e: e.tensor_scalar(out=o_dst, in0=pv[:nq, 0:128], scalar1=ri, scalar2=None, op0=ALU.mult),
                 reads=[bpv, brs], writes=list(o_bufs))

        return stageB

    def run_units(makers):
        pend = None
        for mk_ in makers:
            nxt = mk_()
            if pend is not None:
                pend()
            pend = nxt
        if pend is not None:
            pend()

    def tr4_f32(dst, src, nrows_in, reads, writes):
        raise NotImplementedError

    def load_rows_T(dst3, bdst, src_rows, stage, bstage, nchunks=16):
        S.dma("pool", lambda e: e.dma_start(out=stage[:, 0:nchunks * 128], in_=src_rows), writes=[bstage])
        for c0 in range(0, nchunks, 4):
            pb, bpb = nextpb()
            def fn(e, c0=c0, pb=pb):
                ins = None
                for c in range(c0, c0 + 4):
                    ins = e.transpose(out=pb[:, (c - c0) * 128:(c - c0 + 1) * 128],
                                      in_=stage[:, c * 128:(c + 1) * 128], identity=idf[:])
                return ins
            S.op("pe", fn, reads=[bstage, B_c], writes=[bpb])
            evac(dst3(c0), pb[:, 0:512].rearrange("p (c t) -> p c t", t=128), reads=[bpb], writes=[bdst])

    def store_tok(src, bsrc, nch, np_, ntok, dst_rows, stage, bstage):
        for s in range(ntok // 128):
            j = s % 2
            for c0 in range(0, nch, 4):
                cn = min(4, nch - c0)
                pb, bpb = nextpb()
                tops = [(pb[:, (c - c0) * np_:(c - c0 + 1) * np_], src(c)[:, s * 128:(s + 1) * 128]) for c in range(c0, c0 + cn)]
                def fn(e, tops=tops):
                    ins = None
                    for o, a in tops:
                        ins = e.transpose(out=o, in_=a, identity=idf[:np_, :np_])
                    return ins
                S.op("pe", fn, reads=list(bsrc) + [B_c], writes=[bpb])
                evac(stage[j][:, c0 * np_:(c0 + cn) * np_], pb[:, 0:cn * np_], reads=[bpb], writes=[bstage[j]])
            S.dma("pool", lambda e, s=s, j=j: e.dma_start(out=dst_rows(s), in_=stage[j][:, 0:nch * np_]),
                  reads=[bstage[j]])

    G_MPRE, G_MPOST, G_FPRE, G_FPOST, G_QN, G_KVN = 0, 32, 64, 96, 128, 132

    def ffn(l, xT, bx, hT, bh, yT, by, hid, ntok):
        rms(lambda c: xT[:, c, 0:ntok], 16, ntok, D, G_FPRE + 16 * l, dst=lambda c: hT[:, c, 0:ntok],
            bsrc=[bx], bdst=[bh])
        bhid = [Buf() for _ in range(64)]
        def sink1(m, ps, bps):
            i = rr["tf"] % NTF
            rr["tf"] += 1
            t = tmpf[i][:, 0:ntok]
            S.op("act", lambda e: e.activation(out=t, in_=ps, func=AF.Relu), reads=[bps], writes=[B_tf[i]])
            eng = ("pool", "dve")[m % 2]
            S.op(eng, lambda e: e.tensor_tensor(out=hid[:, m, 0:ntok], in0=t, in1=t, op=ALU.mult),
                 reads=[B_tf[i]], writes=[bhid[m]])
        lin4(W1m[l], lambda k: hT[:, k, 0:ntok], ntok, sink1, [bh])
        def sink2(m, ps, bps):
            evac(yT[:, m, 0:ntok], ps, reads=[bps], writes=[by])
        lin4(W2m[l], lambda k: hid[:, k, 0:ntok], ntok, sink2, bhid)
        rms(lambda c: yT[:, c, 0:ntok], 16, ntok, D, G_FPOST + 16 * l, resid=lambda c: xT[:, c, 0:ntok],
            bsrc=[by], bdst=[bx])

    xT = v32(0, 8192).rearrange("p (c t) -> p c t", t=512)
    hT = v16(16 * K1, 8192).rearrange("p (c t) -> p c t", t=512)
    yT = v32(24 * K1, 8192).rearrange("p (c t) -> p c t", t=512)
    hid = v16(40 * K1, 32768).rearrange("p (f t) -> p f t", t=512)
    QT = v16(24 * K1, 8192).rearrange("p (h t) -> p h t", t=512)
    KT = v16(32 * K1, 16384).rearrange("p (h t) -> p h t", t=1024)
    Vt = v16(48 * K1, 16384).rearrange("p (s c) -> p s c", c=2048)
    otok = v16(64 * K1, 4096).rearrange("p (j c) -> p j c", c=2048)
    stg = [v32(68 * K1 + 4096 * j, 2048) for j in range(2)]
    btile = [v16(76 * K1 + 640 * j, 640) for j in range(2)]
    btile6 = [v16(72 * K1 + 640 * j, 640) for j in range(6)]
    B_bt6 = [Buf() for _ in range(6)]
    Pb0 = [v16(77 * K1 + 256 + 640 * j, 640) for j in range(2)]
    PT0 = v16(78 * K1 + 512, 640).rearrange("p (t q) -> p t q", q=128)
    B_P0, B_PT0 = [Buf(), Buf()], [Buf() for _ in range(8)]
    set0 = (None, None, Pb0, B_P0, PT0, B_PT0)
    oT = hT
    B_bt = [Buf(), Buf()]

    def l0_tile(t):
        smp = (t == 8)
        ntok = 128 if smp else 512
        nsub = ntok // 128
        S.barrier()
        bx, bh, bq, bk, bv, bo, by = Buf(), Buf(), Buf(), Buf(), Buf(), Buf(), Buf()
        bstg = [Buf(), Buf()]
        botok = [Buf(), Buf()]
        for s in range(nsub):
            rows = xs[:, :] if smp else xp[t * 512 + s * 128:t * 512 + (s + 1) * 128, :]
            load_rows_T(lambda c0, s=s: xT[:, c0:c0 + 4, s * 128:(s + 1) * 128], bx, rows, stg[s % 2], bstg[s % 2])
        rms(lambda c: xT[:, c, 0:ntok], 16, ntok, D, G_MPRE, dst=lambda c: hT[:, c, 0:ntok], bsrc=[bx], bdst=[bh])
        def sinkq(m, ps, bps):
            evac(QT[:, m, 0:ntok], ps, reads=[bps], writes=[bq], scale=SC_A)
        lin4(Wqkv, lambda k: hT[:, k, 0:ntok], ntok, sinkq, [bh], groups=range(0, 4))
        kcur = 640 if smp else 512
        def sinkk(m, ps, bps):
            evac(KT[:, m - 16, kcur:kcur + ntok], ps, reads=[bps], writes=[bk])
        lin4(Wqkv, lambda k: hT[:, k, 0:ntok], ntok, sinkk, [bh], groups=range(4, 8))
        if int(os.environ.get('STOP_AT', '99')) <= 1:
            return
        want_out = smp or t == 7
        if want_out:
            def sinkko(g, s, ps, bps):
                i = rr["tf"] % NTF
                rr["tf"] += 1
                evac(tmpf[i][:, :], ps, reads=[bps], writes=[B_tf[i]])
                dst = (nak_s if smp else nak_p)[s * 128:(s + 1) * 128, (g - 4) * 512:(g - 3) * 512]
                S.dma("pool", lambda e: e.dma_start(out=dst, in_=tmpf[i][:, :]), reads=[B_tf[i]])
            for s_ in range(nsub):
                def sk(g, s, ps, bps, s_=s_):
                    sinkko(g, s_, ps, bps)
                lin_tok(Wqkv, lambda k, s, s_=s_: hT[:, k, s_ * 128:(s_ + 1) * 128], 1, 128, sk, [bh], groups=range(4, 8))
        if int(os.environ.get('STOP_AT', '99')) <= 2:
            return
        if not smp:
            def sinkv(g, s, ps, bps):
                evac(Vt[:, 4 + s, (g - 8) * 512:(g - 7) * 512], ps, reads=[bps], writes=[bv])
            lin_tok(Wqkv, lambda k, s: hT[:, k, s * 128:(s + 1) * 128], nsub, 128, sinkv, [bh], groups=range(8, 12))
            if want_out:
                for s_ in range(nsub):
                    def sv(g, s, ps, bps, s_=s_):
                        i = rr["tf"] % NTF
                        rr["tf"] += 1
                        evac(tmpf[i][:, :], ps, reads=[bps], writes=[B_tf[i]])
                        dst = nav_p[s_ * 128:(s_ + 1) * 128, (g - 8) * 512:(g - 7) * 512]
                        S.dma("pool", lambda e: e.dma_start(out=dst, in_=tmpf[i][:, :]), reads=[B_tf[i]])
                    lin_tok(Wqkv, lambda k, s, s_=s_: hT[:, k, s_ * 128:(s_ + 1) * 128], 1, 128, sv, [bh], groups=range(8, 12))
        else:
            def sinkvs(g, a, ps, bps):
                evac(Vt[:64, 5 + a, (g - 8) * 512:(g - 7) * 512], ps, reads=[bps], writes=[bv])
            def sinkvo(g, s, ps, bps):
                i = rr["tf"] % NTF
                rr["tf"] += 1
                evac(tmpf[i][:, :], ps, reads=[bps], writes=[B_tf[i]])
                dst = nav_s[:, (g - 8) * 512:(g - 7) * 512]
                S.dma("pool", lambda e: e.dma_start(out=dst, in_=tmpf[i][:, :]), reads=[B_tf[i]])
            lin_tok(Wqkv, lambda k, s: hT[:, k, 0:128], 1, 128, sinkvo, [bh], groups=range(8, 12))
            lin_tok(Wqkv, lambda k, a: hT[:, k, a * 64:(a + 1) * 64], 2, 64, sinkvs, [bh], groups=range(8, 12))
        if int(os.environ.get('STOP_AT', '99')) <= 3:
            return
        if not smp:
            if t < 7:
                S.dma("pool", lambda e, t=t: e.dma_start(out=KTs[:, :, t * 512:(t + 1) * 512], in_=KT[:, :, 512:1024]),
                      reads=[bk], writes=[B_kv[t]])
                S.dma("pool", lambda e, t=t: e.dma_start(
                    out=Vs[t * 512:(t + 1) * 512, :].rearrange("(s p) c -> p s c", p=128), in_=Vt[:, 4:8, :]),
                    reads=[bv], writes=[B_kv[t]])
            if t == 0:
                S.op("pool", lambda e: e.memset(KT[:, :, 0:512], 0.0), writes=[bk])
                S.op("pool", lambda e: e.memset(Vt[:, 0:4, :], 0.0), writes=[bv])
            else:
                S.dma("pool", lambda e, t=t: e.dma_start(out=KT[:, :, 0:512], in_=KTs[:, :, (t - 1) * 512:t * 512]),
                      reads=[B_kv[t - 1]], writes=[bk])
                S.dma("pool", lambda e, t=t: e.dma_start(
                    out=Vt[:, 0:4, :], in_=Vs[(t - 1) * 512:t * 512, :].rearrange("(s p) c -> p s c", p=128)),
                    reads=[B_kv[t - 1]], writes=[bv])
        kindb = mk[0:10, 0:640]

        def band_units(s, a=None, t=t):
            nq = 128 if a is None else 64
            nk = 640 if a is None else 576
            q0 = s * 128 if a is None else a * 64
            w0 = s * 128 if a is None else 0
            var = min(4 * t + s, 4) if a is None else None
            j = (s if a is None else a) % 2
            def mk_unit(h):
                if a is None:
                    bt_, bbt = btile6[h % 6], B_bt6[h % 6]
                else:
                    bt_, bbt = btile[h % 2], B_bt[h % 2]
                S.dma("sp", lambda e, h=h, bt_=bt_: e.dma_start(out=bt_, in_=relB[h]), reads=[B_relB], writes=[bbt])

                def terms(k0, kn, h=h, bt_=bt_):
                    l = [(QT[:, h, q0:q0 + nq], KT[:, h, w0 + k0:w0 + k0 + kn]),
                         (ident_b[:nq, :nq], bt_[:nq, k0:k0 + kn])]
                    if var is not None:
                        l.append((mk[0:10, 640 + var * 128:640 + (var + 1) * 128], kindb[:, k0:k0 + kn]))
                    return l

                def vt(kt, kn, h=h):
                    if a is not None and kt == 4:
                        return Vt[:kn, 5 + a, h * 128:(h + 1) * 128]
                    return Vt[:kn, (s if a is None else 0) + kt, h * 128:(h + 1) * 128]
                return attn_unit(nq, nk, terms, [bq, bk, bbt, B_c], vt, [bv],
                                 otok[:nq, j, h * 128:(h + 1) * 128], [botok[j]], set0)
            run_units([(lambda h=h: mk_unit(h)) for h in range(16)])
            for c0 in range(0, 16, 4):
                pb, bpb = nextpb()
                pbb = pb[:].bitcast(BF16)
                def fn(e, c0=c0, pbb=pbb):
                    ins = None
                    for c in range(c0, c0 + 4):
                        ins = e.transpose(out=pbb[:, (c - c0) * 128:(c - c0) * 128 + nq],
                                          in_=otok[:nq, j, c * 128:(c + 1) * 128], identity=ident_b[:nq, :nq])
                    return ins
                S.op("pe", fn, reads=[botok[j], B_c], writes=[bpb])
                evac(oT[:, c0:c0 + 4, q0:q0 + nq],
                     pbb[:, 0:512].rearrange("p (c q) -> p c q", q=128)[:, :, 0:nq], reads=[bpb], writes=[bo, bh])
        if not smp:
            for s in range(nsub):
                band_units(s)
        else:
            for a in ([] if os.environ.get('SKIP_A') else range(2)):
                for s4 in range(4):
                    load_rows_T(lambda c0, s4=s4: KT[:, c0:c0 + 4, s4 * 128:(s4 + 1) * 128], bk,
                                cak[a, s4 * 128:(s4 + 1) * 128, :], stg[s4 % 2], bstg[s4 % 2])
                for s4 in range(4):
                    j = s4 % 2
                    S.dma("pool", lambda e, a=a, s4=s4, j=j: e.dma_start(out=stg[j], in_=cav[a, s4 * 128:(s4 + 1) * 128, :]),
                          writes=[bstg[j]])
                    S.op("dve", lambda e, s4=s4, j=j: e.tensor_copy(out=Vt[:, s4, :], in_=stg[j]),
                         reads=[bstg[j]], writes=[bv])
                S.op("pool", lambda e, a=a: e.tensor_copy(out=KT[:, :, 512:576], in_=KT[:, :, 640 + a * 64:704 + a * 64]),
                     reads=[bk], writes=[bk])
                band_units(0, a)
        if int(os.environ.get('STOP_AT', '99')) <= 4:
            return
        S.barrier()
        by = Buf()
        def sinko(m, ps, bps):
            evac(yT[:, m, 0:ntok], ps, reads=[bps], writes=[by])
        lin4(Wao, lambda k: oT[:, k, 0:ntok], ntok, sinko, [bo, bh])
        rms(lambda c: yT[:, c, 0:ntok], 16, ntok, D, G_MPOST, resid=lambda c: xT[:, c, 0:ntok], bsrc=[by], bdst=[bx])
        if int(os.environ.get('STOP_AT', '99')) <= 5:
            return
        S.barrier()
        by = Buf(); bh = Buf()
        ffn(0, xT, bx, hT, bh, yT, by, hid, ntok)
        if smp:
            S.dma("pool", lambda e: e.dma_start(out=x1ss[:, :, :], in_=xT[:, :, 0:128]), reads=[bx], writes=[B_x1s])
        else:
            S.dma("pool", lambda e, t=t: e.dma_start(out=x1s[:, :, t * 512:(t + 1) * 512], in_=xT[:, :, :]),
                  reads=[bx], writes=[B_x1[t]])
        if int(os.environ.get('STOP_AT', '99')) <= 6:
            return
        S.barrier()
        by = Buf(); bh = Buf()
        rms(lambda c: xT[:, c, 0:ntok], 16, ntok, D, G_MPRE + 16, dst=lambda c: hT[:, c, 0:ntok], bsrc=[bx], bdst=[bh])
        def sinkc(m, ps, bps):
            evac(yT[:, m, 0:ntok], ps, reads=[bps], writes=[by])
        lin4(Wdkv_c, lambda k: hT[:, k, 0:ntok], ntok, sinkc, [bh])
        cs = v32(68 * K1, 1024).rearrange("p (a t) -> p a t", t=512)
        bcs, bkr = Buf(), Buf()
        cd, sd, p0 = (coss_d, sins_d, 0) if smp else (cosk_d, sink_d, t * 512)
        S.dma("pool", lambda e: e.dma_start(out=cs[:64, 0, 0:ntok], in_=cd[:, p0:p0 + ntok]), writes=[bcs])
        S.dma("pool", lambda e: e.dma_start(out=cs[:64, 1, 0:ntok], in_=sd[:, p0:p0 + ntok]), writes=[bcs])
        krf = yT[:64, 5, 0:ntok]
        def sinkr(m, ps, bps):
            pass
        accs = accs4(ntok)
        for kp in range(Wdkv_r.nkp):
            wv, wb = wload(Wdkv_r, 0, kp)
            ops = [(accs[m][0][:64, 0:ntok], wv[:, k, m * 64:(m + 1) * 64], hT[:, kp * Wdkv_r.kper + k, 0:ntok],
                    (kp == 0 and k == 0), (kp == Wdkv_r.nkp - 1 and k == Wdkv_r.kper - 1))
                   for k in range(Wdkv_r.kper) for m in range(2)]
            def fn(e, ops=ops):
                ins = None
                for o, l, r, a0, a1 in ops:
                    ins = e.matmul(o, lhsT=l, rhs=r, start=a0, stop=a1)
                return ins
            S.op("pe", fn, reads=[bh, wb], writes=ubufs([accs[0][1], accs[1][1]]))
        S.op("dve", lambda e: e.tensor_tensor(out=yT[:64, 4, 0:ntok], in0=accs[0][0][:64, 0:ntok], in1=cs[:64, 0, 0:ntok], op=ALU.mult),
             reads=[accs[0][1], bcs], writes=[bkr])
        S.op("dve", lambda e: e.tensor_tensor(out=krf, in0=accs[1][0][:64, 0:ntok], in1=cs[:64, 1, 0:ntok], op=ALU.mult),
             reads=[accs[1][1], bcs], writes=[bkr])
        S.op("pool", lambda e: e.tensor_tensor(out=krf, in0=krf, in1=yT[:64, 4, 0:ntok], op=ALU.add), reads=[bkr], writes=[bkr])
        bcn = Buf()
        rms(lambda c: yT[:, c, 0:ntok], 4, ntok, 512, G_KVN, dst=lambda c: yT[:, 8 + c, 0:ntok], bsrc=[by], bdst=[bcn])
        lat16 = hid[:, 40:45, :]
        bl16 = Buf()
        S.op("pool", lambda e: e.tensor_copy(out=lat16[:, 0:4, 0:ntok], in_=yT[:, 8:12, 0:ntok]), reads=[bcn], writes=[bl16])
        S.op("pool", lambda e: e.tensor_copy(out=lat16[:64, 4, 0:ntok], in_=krf), reads=[bkr], writes=[bl16])
        if smp:
            S.dma("pool", lambda e: e.dma_start(out=ckvTss[:, :, :], in_=lat16[:, 0:4, 0:128]), reads=[bl16], writes=[B_lats])
            S.dma("pool", lambda e: e.dma_start(out=krTss[:, :], in_=lat16[:64, 4, 0:128]), reads=[bl16], writes=[B_lats])
        else:
            S.dma("pool", lambda e, t=t: e.dma_start(out=ckvTs[:, :, t * 512:(t + 1) * 512], in_=lat16[:, 0:4, :]),
                  reads=[bl16], writes=[B_lat[t]])
            S.dma("pool", lambda e, t=t: e.dma_start(out=krTs[:, t * 512:(t + 1) * 512], in_=lat16[:64, 4, :]),
                  reads=[bl16], writes=[B_lat[t]])
        stg2 = [v32(72 * K1 + 2048 * j, 1024) for j in range(2)]
        bst2 = [Buf(), Buf()]
        if smp:
            store_tok(lambda c: yT[:, 8 + c, :], [bcn], 4, 128, ntok, lambda s: nckv_s[:, :], stg2, bst2)
            store_tok(lambda c: yT[:64, 5, :], [bkr], 1, 64, ntok, lambda s: nkr_s[:, :], stg2, bst2)
        else:
            store_tok(lambda c: yT[:, 8 + c, :], [bcn], 4, 128, ntok,
                      lambda s, t=t: nckv_p[t * 512 + s * 128:t * 512 + (s + 1) * 128, :], stg2, bst2)
            store_tok(lambda c: yT[:64, 5, :], [bkr], 1, 64, ntok,
                      lambda s, t=t: nkr_p[t * 512 + s * 128:t * 512 + (s + 1) * 128, :], stg2, bst2)
        if t == 0:
            for wm in [Wdq] + Wuq + [Wuk, Wuv, Wbo, W1m[1], W2m[1]]:
                wm.cast()

    for t_ in (range(l0_tiles) if isinstance(l0_tiles, int) else l0_tiles):
        l0_tile(t_)

    cq = v16(0, 2048).rearrange("p (c t) -> p c t", t=512)
    ckvT = v16(2 * K1, 16384).rearrange("p (c t) -> p c t", t=4096)
    krT = v16(18 * K1, 4096)
    kind = v16(22 * K1, 4096)
    qm = v16(26 * K1, 2048)
    knT = v16(28 * K1, 4096)
    v2 = v16(32 * K1, 8192).rearrange("p (t c) -> p t c", c=256)
    qn = v16(40 * K1, 1024).rearrange("p (j t) -> p j t", t=512)
    qr = v16(41 * K1, 1024).rearrange("p (j t) -> p j t", t=512)
    otk = v16(42 * K1, 8192).rearrange("p (s c) -> p s c", c=2048)
    rtab = v32(50 * K1, 1024).rearrange("p (a t) -> p a t", t=512)
    rtmp = v32(52 * K1, 1024).rearrange("p (a t) -> p a t", t=512)
    stq = [v32(54 * K1 + 1024 * j, 512) for j in range(2)]
    Sb1 = v32(56 * K1, 4096)
    Pb1 = v16(64 * K1, 4096)
    PT1 = v16(68 * K1, 4096).rearrange("p (t q) -> p t q", q=128)
    ovT = v16(72 * K1, 8192).rearrange("p (h t) -> p h t", t=512)
    Pb1b = st.enter_context(nc.sbuf_tensor("pb1b", [128, 4096], BF16))
    set1 = (Sb1, [Buf() for _ in range(4)], [Pb1, Pb1b[:, :]], [Buf(), Buf()], PT1, [Buf() for _ in range(8)])
    B_keys, B_mask, B_kn, B_v2, B_qh, B_otk, B_rtab, B_rtmp, B_ov, B_cq = (
        Buf(), Buf(), Buf(), Buf(), [Buf(), Buf()], Buf(), Buf(), Buf(), Buf(), Buf())
    B_stq = [Buf(), Buf()]

    def l1_tile(i):
        smp = (i == 4)
        ntok = 128 if smp else 512
        nsub = ntok // 128
        S.barrier()
        bx, bh, by = Buf(), Buf(), Buf()

        def load_x1(bx, by, i=i, smp=smp):
            if smp:
                S.dma("pool", lambda e: e.dma_start(out=xT[:, :, 0:128], in_=x1ss[:, :, :]), reads=[B_x1s], writes=[bx])
            else:
                S.dma("pool", lambda e: e.dma_start(out=xT[:, :, :], in_=x1s[:, :, i * 512:(i + 1) * 512]),
                      reads=[B_x1[i]], writes=[bx])
                S.dma("pool", lambda e: e.dma_start(out=yT[:, :, :], in_=x1s[:, :, 2048 + i * 512:2048 + (i + 1) * 512]),
                      reads=[B_x1[4 + i]], writes=[by])
                S.op("dve", lambda e: e.tensor_scalar(out=yT[:, :, :], in0=yT[:, :, :], scalar1=sel[:, 1:2], scalar2=None,
                                                     op0=ALU.mult), reads=[by, B_c], writes=[by])
                S.op("dve", lambda e: e.scalar_tensor_tensor(out=xT[:, :, :], in0=xT[:, :, :], scalar=sel[:, 0:1],
                                                            in1=yT[:, :, :], op0=ALU.mult, op1=ALU.add),
                     reads=[by, B_c], writes=[bx])
        load_x1(bx, by)
        rms(lambda c: xT[:, c, 0:ntok], 16, ntok, D, G_MPRE + 16, dst=lambda c: hT[:, c, 0:ntok], bsrc=[bx], bdst=[bh])
        by = Buf()
        def sinkcq(m, ps, bps):
            evac(yT[:, m, 0:ntok], ps, reads=[bps], writes=[by])
        lin4(Wdq, lambda k: hT[:, k, 0:ntok], ntok, sinkcq, [bh])
        rms(lambda c: yT[:, c, 0:ntok], 4, ntok, 512, G_QN, dst=lambda c: cq[:, c, 0:ntok], bsrc=[by], bdst=[B_cq, bx])
        S.barrier()
        def seq_body(a):
            if a is None:
                nk = 2048 + 512 * (i + 1)
                S.dma("pool", lambda e, nk=nk: e.dma_start(out=ckvT[:, :, 0:nk], in_=ckvTs[:, :, 0:nk]),
                      reads=B_lat, writes=[B_keys])
                S.dma("pool", lambda e, nk=nk: e.dma_start(out=krT[:64, 0:nk], in_=krTs[:, 0:nk]), reads=B_lat, writes=[B_keys])
                S.dma("pool", lambda e: e.dma_start(out=krT[64:128, :], in_=kind_d), writes=[B_mask])
                for j_ in range(2):
                    S.dma("pool", lambda e, j_=j_: e.dma_start(out=qr[64:128, j_, :], in_=qm_d[:, i * 512:(i + 1) * 512]),
                          writes=[B_mask])
            else:
                nk = 2112
                S.dma("pool", lambda e: e.dma_start(out=krT[64:128, :], in_=kind_d), writes=[B_mask])
                S.op("pool", lambda e: e.memset(qr[64:128, :, :], 0.0), writes=[B_mask])
                for s16 in range(16):
                    j = s16 % 2
                    load_rows_T(lambda c0, s16=s16: ckvT[:, c0:c0 + 4, s16 * 128:(s16 + 1) * 128], B_keys,
                                cckv[a, s16 * 128:(s16 + 1) * 128, :], stq[j], B_stq[j], nchunks=4)
                for s4 in range(4):
                    j = s4 % 2
                    S.dma("pool", lambda e, a=a, s4=s4, j=j: e.dma_start(
                        out=stq[j][:, 0:256].rearrange("p (s r) -> p s r", r=64),
                        in_=ckr[a, s4 * 512:(s4 + 1) * 512, :].rearrange("(s p) r -> p s r", p=128)), writes=[B_stq[j]])
                    pb, bpb = nextpb()
                    def fn(e, pb=pb, j=j):
                        ins = None
                        for c in range(4):
                            ins = e.transpose(out=pb[:64, c * 128:(c + 1) * 128], in_=stq[j][:, c * 64:(c + 1) * 64], identity=idf[:])
                        return ins
                    S.op("pe", fn, reads=[B_stq[j], B_c], writes=[bpb])
                    evac(krT[:64, s4 * 512:(s4 + 1) * 512], pb[:64, 0:512], reads=[bpb], writes=[B_keys])
                S.dma("pool", lambda e, a=a: e.dma_start(out=ckvT[:, :, 2048:2112], in_=ckvTss[:, :, a * 64:(a + 1) * 64]),
                      reads=[B_lats], writes=[B_keys])
                S.dma("pool", lambda e, a=a: e.dma_start(out=krT[:64, 2048:2112], in_=krTss[:, a * 64:(a + 1) * 64]),
                      reads=[B_lats], writes=[B_keys])
            nq_tok = ntok if a is None else 64
            q0 = 0 if a is None else a * 64
            if smp:
                S.dma("pool", lambda e, q0=q0: e.dma_start(out=rtab[:64, 0, 0:64], in_=coss_d[:, q0:q0 + 64]), writes=[B_rtab])
                S.dma("pool", lambda e, q0=q0: e.dma_start(out=rtab[:64, 1, 0:64], in_=sins_d[:, q0:q0 + 64]), writes=[B_rtab])
            else:
                S.dma("pool", lambda e: e.dma_start(out=rtab[:64, 0, :], in_=cosq_d[:, i * 512:(i + 1) * 512]), writes=[B_rtab])
                S.dma("pool", lambda e: e.dma_start(out=rtab[:64, 1, :], in_=sinq_d[:, i * 512:(i + 1) * 512]), writes=[B_rtab])
            nkt = (nk + 127) // 128
            def head_body(h):
                j = h % 2
                if h % 2 == 0:
                    wvv, bwv = wload(Wuv, h // 2, 0)
                    for g0 in range(0, nkt, 2):
                        pb, bpb = nextpb()
                        gcnt = min(2, nkt - g0)
                        ops = [(pb[:min(128, nk - 128 * kt), (kt - g0) * 256:(kt - g0 + 1) * 256],
                                ckvT[:, c, kt * 128:kt * 128 + min(128, nk - 128 * kt)], wvv[:, c, :], c == 0, c == 3)
                               for kt in range(g0, g0 + gcnt) for c in range(4)]
                        def fn(e, ops=ops):
                            ins = None
                            for o, l, r, a0, a1 in ops:
                                ins = e.matmul(o, lhsT=l, rhs=r, start=a0, stop=a1)
                            return ins
                        S.op("pe", fn, reads=[B_keys, bwv], writes=[bpb])
                        for kt in range(g0, g0 + gcnt):
                            kn = min(128, nk - 128 * kt)
                            evac(v2[:kn, kt, :], pb[:kn, (kt - g0) * 256:(kt - g0 + 1) * 256], reads=[bpb], writes=[B_v2])
                wq, bw = wload(Wuq[h], 0, 0)
                pb, bpb = nextpb()
                mm(pb[:, 0:nq_tok], [(wq[:, c, 0:128], cq[:, c, q0:q0 + nq_tok]) for c in range(4)], reads=[bw, B_cq], writes=[bpb])
                evac(qn[:, j, 0:nq_tok], pb[:, 0:nq_tok], reads=[bpb], writes=[B_qh[j]], scale=SC_B)
                pr, bpr = nextpb()
                mm(pr[:64, 0:nq_tok], [(wq[:, c, 128:192], cq[:, c, q0:q0 + nq_tok]) for c in range(4)], reads=[bw, B_cq], writes=[bpr])
                pw, bpw = nextpb()
                mm(pw[:64, 0:nq_tok], [(wq[:, c, 192:256], cq[:, c, q0:q0 + nq_tok]) for c in range(4)], reads=[bw, B_cq], writes=[bpw])
                S.op("dve", lambda e, pr=pr: e.tensor_tensor(out=rtmp[:64, 0, 0:nq_tok], in0=pr[:64, 0:nq_tok],
                                                             in1=rtab[:64, 0, 0:nq_tok], op=ALU.mult), reads=[bpr, B_rtab], writes=[B_rtmp])
                S.op("dve", lambda e, pw=pw: e.tensor_tensor(out=rtmp[:64, 1, 0:nq_tok], in0=pw[:64, 0:nq_tok],
                                                             in1=rtab[:64, 1, 0:nq_tok], op=ALU.mult), reads=[bpw, B_rtab], writes=[B_rtmp])
                S.op("pool", lambda e: e.tensor_tensor(out=rtmp[:64, 0, 0:nq_tok], in0=rtmp[:64, 0, 0:nq_tok],
                                                       in1=rtmp[:64, 1, 0:nq_tok], op=ALU.add), reads=[B_rtmp], writes=[B_rtmp])
                S.op("pool", lambda e, j=j: e.tensor_scalar(out=qr[:64, j, 0:nq_tok], in0=rtmp[:64, 0, 0:nq_tok], scalar1=SC_B,
                                                            scalar2=None, op0=ALU.mult), reads=[B_rtmp], writes=[B_qh[j]])
                wk, bwk = wload(Wuk, h, 0)
                for k0 in range(0, nk, 512):
                    kn = min(512, nk - k0)
                    pb, bpb = nextpb()
                    mm(pb[:, 0:kn], [(wk[:, c, :], ckvT[:, c, k0:k0 + kn]) for c in range(4)], reads=[bwk, B_keys], writes=[bpb])
                    evac(knT[:, k0:k0 + kn], pb[:, 0:kn], reads=[bpb], writes=[B_kn])
                def unit_body(s):
                    if a is None:
                        jq = 4 * i + s
                        nk_u = 128 * (17 + jq)
                        nq, qc0, srow = 128, s * 128, s
                    else:
                        jq, nk_u, nq, qc0, srow = None, 2112, 64, 0, a

                    def terms(k0, kn, j=j, qc0=qc0, nq=nq, jq=jq):
                        return [(qn[:, j, qc0:qc0 + nq], knT[:, k0:k0 + kn]),
                                (qr[:, j, qc0:qc0 + nq], krT[:, k0:k0 + kn])]

                    def vt(kt, kn, h=h):
                        return v2[:kn, kt, (h % 2) * 128:(h % 2 + 1) * 128]
                    return attn_unit(nq, nk_u, terms, [B_qh[j], B_kn, B_keys, B_mask], vt, [B_v2],
                                     otk[:nq, srow, h * 128:(h + 1) * 128], [B_otk], set1)
                run_units([(lambda s_=s_: unit_body(s_)) for s_ in range(nsub if a is None else 1)])
            for h_ in range(16):
                head_body(h_)
            for s in range(nsub if a is None else 1):
                srow = s if a is None else a
                nq = 128 if a is None else 64
                c0q = s * 128 if a is None else a * 64
                for c0 in range(0, 16, 4):
                    pb, bpb = nextpb()
                    pbb = pb[:].bitcast(BF16)
                    def fn(e, c0=c0, pbb=pbb, srow=srow, nq=nq):
                        ins = None
                        for c in range(c0, c0 + 4):
                            ins = e.transpose(out=pbb[:, (c - c0) * 128:(c - c0) * 128 + nq],
                                              in_=otk[:nq, srow, c * 128:(c + 1) * 128], identity=ident_b[:nq, :nq])
                        return ins
                    S.op("pe", fn, reads=[B_otk, B_c], writes=[bpb])
                    evac(ovT[:, c0:c0 + 4, c0q:c0q + nq],
                         pbb[:, 0:512].rearrange("p (c q) -> p c q", q=128)[:, :, 0:nq], reads=[bpb], writes=[B_ov])
        for a_ in ([None] if not smp else [0, 1]):
            seq_body(a_)
        S.barrier()
        bx, bh, by = Buf(), Buf(), Buf()
        load_x1(bx, by)
        by = Buf()
        def sinkbo(m, ps, bps):
            evac(yT[:, m, 0:ntok], ps, reads=[bps], writes=[by])
        lin4(Wbo, lambda k: ovT[:, k, 0:ntok], ntok, sinkbo, [B_ov, bx])
        rms(lambda c: yT[:, c, 0:ntok], 16, ntok, D, G_MPOST + 16, resid=lambda c: xT[:, c, 0:ntok], bsrc=[by], bdst=[bx])
        S.barrier()
        by = Buf()
        ffn(1, xT, bx, hT, bh, yT, by, hid, ntok)
        S.barrier()
        bstg = [Buf(), Buf()]
        stgy = [v32(24 * K1 + 4096 * j, 2048) for j in range(2)]
        if smp:
            store_tok(lambda c: xT[:, c, :], [bx], 16, 128, ntok, lambda s: y_s[:, :], stgy, bstg)
        else:
            store_tok(lambda c: xT[:, c, :], [bx], 16, 128, ntok,
                      lambda s, i=i: y_p[i * 512 + s * 128:i * 512 + (s + 1) * 128, :], stgy, bstg)

    for i_ in (range(l1_tiles) if isinstance(l1_tiles, int) else l1_tiles):
        l1_tile(i_)
    S.emit(st)
    st.close()
    return nc


def _pos_tables(pos):
    inv = 1.0 / (10000.0 ** (np.arange(32, dtype=np.float32) / 32.0))
    ang = pos.astype(np.float32)[None, :] * inv[:, None].astype(np.float32)
    c = np.cos(ang).astype(np.float32); s = np.sin(ang).astype(np.float32)
    return np.concatenate([c, c], 0), np.concatenate([-s, s], 0)


def _consts(half):
    bf = ml_dtypes.bfloat16
    ident = np.eye(128, dtype=np.float32)
    cb = np.concatenate([ident, np.ones((128, 128), np.float32)], 1).astype(bf)
    kindb = np.zeros((10, 640), np.float32)
    for c in range(10):
        kindb[c, c * 64:(c + 1) * 64] = 1.0
    qmb = np.zeros((10, 5 * 128), np.float32)
    for var in range(5):
        for r in range(128):
            qc = r // 64
            for c in range(10):
                masked = (c > 8 + qc) or (c < qc)
                if var < 4 and c < 8 - 2 * var:
                    masked = True
                qmb[c, var * 128 + r] = NEG if masked else 0.0
    kind = np.zeros((64, 4096), np.float32)
    for c in range(64):
        kind[c, c * 64:(c + 1) * 64] = 1.0
    qm = np.zeros((64, 2048), np.float32)
    for r in range(2048):
        qc = (half * 2048 + r) // 64
        qm[qc + 1:, r] = NEG
    sel = np.zeros((128, 2), np.float32)
    sel[:, half] = 1.0
    cosk, sink = _pos_tables(np.arange(4096))
    cosq, sinq = _pos_tables(half * 2048 + np.arange(2048))
    p = 2048 + np.arange(64)
    coss, sins = _pos_tables(np.concatenate([p, p]))
    return dict(ident_f=ident, cb=cb, kindb=kindb.astype(bf), qmb=qmb.astype(bf), kind=kind.astype(bf),
                qm=qm.astype(bf), sel=sel, cosk=cosk, sink=sink, cosq=cosq, sinq=sinq, coss=coss, sins=sins)


def _colmajor(g):
    return np.ascontiguousarray(g.reshape(-1, 128).T)


_NC_CACHE = {}


def prep(inp):
    f = lambda a: np.ascontiguousarray(np.asarray(a, dtype=np.float32))
    x_prompt, x_sample = f(inp["x_prompt"]), f(inp["x_sample"])
    gains = np.concatenate(
        [_colmajor(f(inp[n])[l]) for n in ("ln_mix_pre", "ln_mix_post", "ln_ffn_pre", "ln_ffn_post") for l in range(2)]
        + [_colmajor(f(inp["mla_q_norm"])[0]), _colmajor(f(inp["mla_kv_norm"])[0])], axis=1)
    rb = f(inp["a_rel_bias"])[0]
    r = np.arange(128)[:, None]; w = np.arange(640)[None, :]
    relT = np.ascontiguousarray(rb[:, np.clip(512 + r - w, -256, 256) + 256])
    wuq = f(inp["mla_w_uq"])[0]
    w_uq = np.ascontiguousarray(np.concatenate([wuq, wuq[:, :, 160:192], wuq[:, :, 128:160]], axis=2))
    wdkv = f(inp["mla_w_dkv"])[0]
    w_dkv = np.ascontiguousarray(np.concatenate([wdkv, wdkv[:, 544:576], wdkv[:, 512:544]], axis=1))
    shared = dict(
        gains=np.ascontiguousarray(gains), w_qkv=f(inp["a_w_qkv"])[0], w_ao=f(inp["a_w_o"])[0], relT=relT,
        w_dq=f(inp["mla_w_dq"])[0], w_uq=w_uq, w_dkv=w_dkv,
        w_uk=f(inp["mla_w_uk"])[0].reshape(512, 2048), w_uv=f(inp["mla_w_uv"])[0].reshape(512, 2048),
        w_bo=f(inp["mla_w_o"])[0], w1=f(inp["ffn_w1"]), w2=f(inp["ffn_w2"]))
    cak, cav = f(inp["cache_a_k"])[0].reshape(16, 512, 2048), f(inp["cache_a_v"])[0].reshape(16, 512, 2048)
    cckv, ckr = f(inp["cache_mla_ckv"])[0], f(inp["cache_mla_kr"])[0]
    consts = [_consts(0), _consts(1)]
    in_maps = []
    for c in range(N_CORES):
        b, half = c // 2, c % 2
        m = dict(shared)
        m.update(consts[half])
        m.update(xp=x_prompt[b], xs=x_sample[2 * c:2 * c + 2].reshape(128, 2048),
                 cak=cak[2 * c:2 * c + 2], cav=cav[2 * c:2 * c + 2],
                 cckv=cckv[2 * c:2 * c + 2], ckr=ckr[2 * c:2 * c + 2])
        in_maps.append(m)
    return in_maps


def post(res):
    R = lambda c, n: np.asarray(res[c][n], dtype=np.float32)
    y_prompt = np.stack([np.concatenate([R(2 * b, "y_p"), R(2 * b + 1, "y_p")], 0) for b in range(4)], 0)
    y_sample = np.concatenate([R(c, "y_s").reshape(2, 64, 2048) for c in range(8)], 0)
    nakp = np.stack([R(2 * b + 1, "nak_p") for b in range(4)], 0).reshape(1, 4, 512, 16, 128)
    navp = np.stack([R(2 * b + 1, "nav_p") for b in range(4)], 0).reshape(1, 4, 512, 16, 128)
    naks = np.concatenate([R(c, "nak_s").reshape(2, 64, 16, 128) for c in range(8)], 0)[None]
    navs = np.concatenate([R(c, "nav_s").reshape(2, 64, 16, 128) for c in range(8)], 0)[None]
    ckvp = np.stack([np.concatenate([R(2 * b, "nckv_p")[:2048], R(2 * b + 1, "nckv_p")[2048:]], 0) for b in range(4)], 0)[None]
    krp = np.stack([np.concatenate([R(2 * b, "nkr_p")[:2048], R(2 * b + 1, "nkr_p")[2048:]], 0) for b in range(4)], 0)[None]
    ckvs = np.concatenate([R(c, "nckv_s").reshape(2, 64, 512) for c in range(8)], 0)[None]
    krs = np.concatenate([R(c, "nkr_s").reshape(2, 64, 64) for c in range(8)], 0)[None]
    return (y_prompt, y_sample, nakp, navp, naks, navs, ckvp, krp, ckvs, krs)


def kernel(**inp):
    in_maps = prep(inp)
    if "nc" not in _NC_CACHE:
        _NC_CACHE["nc"] = build_nc()
    res = run_bass_kernel_spmd(_NC_CACHE["nc"], in_maps, core_ids=list(range(N_CORES))).results
    return post(res)
```
